# Optimizing a Trainium2 kernel written in Bass

```python
import math
import jax, jax.numpy as jnp
from jax import lax
import numpy as np

D_MODEL = 2048
BATCH = 8
SEQ = 2048
DEPTH = 4
DEC_BATCH = 8
DEC_SEQ = 32
PAST_LEN = 1024

CHUNK = 64
H_R = 8
DK = D_MODEL // 16
DV = D_MODEL // 8
QKW = H_R * DK
RV = H_R * DV
D_MLP = D_MODEL
G_MLP = 8
DG = D_MLP // G_MLP
GMLP_CHUNK = 128
ROPE_BASE = 10000.0
EPS = 1e-6
IN_SIZES = (QKW, QKW, RV, RV, D_MLP, D_MLP, D_MLP, D_MODEL, D_MODEL)
N_IN = QKW * 2 + RV * 2 + D_MLP * 3 + D_MODEL * 2

kernel_name = "retention_gmlp_gated_stream_step"


def _split_points():
    pts, acc = [], 0
    for s in IN_SIZES[:-1]:
        acc += s
        pts.append(acc)
    return pts


def rmsnorm(x, g):
    xf = x.astype(jnp.float32)
    y = xf * lax.rsqrt(jnp.mean(xf * xf, axis=-1, keepdims=True) + EPS)
    return (y * g.astype(jnp.float32)).astype(x.dtype)


def layernorm(x, g, b):
    xf = x.astype(jnp.float32)
    mu = jnp.mean(xf, axis=-1, keepdims=True)
    var = jnp.mean(jnp.square(xf - mu), axis=-1, keepdims=True)
    y = (xf - mu) * lax.rsqrt(var + EPS)
    return (y * g.astype(jnp.float32) + b.astype(jnp.float32)).astype(x.dtype)


def head_groupnorm(o):
    of = o.astype(jnp.float32)
    mu = jnp.mean(of, axis=-1, keepdims=True)
    var = jnp.mean(jnp.square(of - mu), axis=-1, keepdims=True)
    return ((of - mu) * lax.rsqrt(var + EPS)).astype(o.dtype)


def rotary(x, pos):
    half = DK // 2
    inv_freq = 1.0 / (ROPE_BASE ** (jnp.arange(half, dtype=jnp.float32) / half))
    ang = pos.astype(jnp.float32)[:, None] * inv_freq[None, :]
    cos = jnp.cos(ang)[None, :, None, :].astype(x.dtype)
    sin = jnp.sin(ang)[None, :, None, :].astype(x.dtype)
    x1, x2 = x[..., :half], x[..., half:]
    return jnp.concatenate([x1 * cos - x2 * sin, x1 * sin + x2 * cos], axis=-1)


def retention(q, k, v, s0, L):
    B, S, H, _ = q.shape
    nc = S // L
    dt = q.dtype
    log_gamma = jnp.log(1.0 - 2.0 ** (-5.0 - jnp.arange(H, dtype=jnp.float32)))
    idx = jnp.arange(L, dtype=jnp.float32)
    dist = jnp.abs(idx[:, None] - idx[None, :])
    intra_decay = jnp.exp(log_gamma[:, None, None] * dist[None]).astype(dt)
    q_dec = jnp.exp(log_gamma[None, :] * (idx[:, None] + 1.0))[..., None].astype(dt)
    k_dec = jnp.exp(log_gamma[None, :] * (L - 1.0 - idx[:, None]))[..., None].astype(dt)
    chunk_dec = jnp.exp(log_gamma * L)[None, :, None, None]

    def to_chunks(t):
        return jnp.moveaxis(t.reshape(B, nc, L, H, t.shape[-1]), 1, 0)

    qc, kc, vc = to_chunks(q), to_chunks(k), to_chunks(v)

    def step(state, inp):
        qi, ki, vi = inp
        scores = jnp.einsum('bihd,bjhd->bhij', qi, ki) * intra_decay[None]
        intra = jnp.einsum('bhij,bjhe->bihe', scores, vi)
        cross = jnp.einsum('bihd,bhde->bihe', qi * q_dec, state)
        new_state = (chunk_dec.astype(state.dtype) * state
                     + jnp.einsum('bjhd,bjhe->bhde', ki * k_dec, vi)).astype(state.dtype)
        return new_state, intra + cross

    s_fin, out = lax.scan(step, s0, (qc, kc, vc))
    out = jnp.moveaxis(out, 0, 1).reshape(B, S, H, v.shape[-1])
    return out, s_fin


def spatial_gate(u, vn, ws, ws_b):
    B, S, _ = u.shape
    L = GMLP_CHUNK if S >= GMLP_CHUNK else S
    ncg = S // L
    i = jnp.arange(L)
    mask = (i[:, None] // CHUNK) >= (i[None, :] // CHUNK)
    w = jnp.where(mask[None], ws[:, :L, :L], 0.0).astype(vn.dtype)
    vr = vn.reshape(B, ncg, L, G_MLP, DG)
    bias = jnp.transpose(ws_b[:, :L])[None, None, :, :, None].astype(vn.dtype)
    s = jnp.einsum('gij,bcjgd->bcigd', w, vr) + bias
    return u * s.reshape(B, S, D_MLP)


def mixer_layer(x, s0, pos, ret_chunk, norm_g, w_in, ws, ws_b, ln_g, ln_b,
                w_ret_out, w_mlp_out, w_o):
    B, S, _ = x.shape
    h = rmsnorm(x, norm_g)
    z = h @ w_in
    q, k, v, gr, u, vm, gm, ar, am = jnp.split(z, _split_points(), axis=-1)
    q = rotary(q.reshape(B, S, H_R, DK), pos)
    k = rotary(k.reshape(B, S, H_R, DK), pos) * (DK ** -0.5)
    v = v.reshape(B, S, H_R, DV)
    o, s_new = retention(q, k, v, s0, ret_chunk)
    o = head_groupnorm(o).reshape(B, S, RV)
    r_branch = (jax.nn.silu(gr) * o) @ w_ret_out
    u = jax.nn.gelu(u, approximate=False)
    vn = layernorm(jax.nn.gelu(vm, approximate=False), ln_g, ln_b)
    m = spatial_gate(u, vn, ws, ws_b)
    m_branch = (jax.nn.silu(gm) * m) @ w_mlp_out
    merged = jax.nn.sigmoid(ar) * r_branch + jax.nn.sigmoid(am) * m_branch
    return x + merged @ w_o, s_new, vn


def setup_inputs(seed: int = 0) -> dict:
    key = jax.random.key(seed)
    ks = jax.random.split(key, 14)
    f32 = jnp.float32
    nrm = lambda k, shape, scale: jax.random.normal(k, shape, f32) * scale
    return {
        "x_prompt": nrm(ks[0], (BATCH, SEQ, D_MODEL), 1.0),
        "x_sample": nrm(ks[1], (DEC_BATCH, DEC_SEQ, D_MODEL), 1.0),
        "state_ret": nrm(ks[2], (DEPTH, DEC_BATCH, H_R, DK, DV), 0.5),
        "norm_g": 1.0 + nrm(ks[3], (DEPTH, D_MODEL), 0.02),
        "w_in": nrm(ks[4], (DEPTH, D_MODEL, N_IN), D_MODEL ** -0.5),
        "ws": nrm(ks[5], (DEPTH, G_MLP, GMLP_CHUNK, GMLP_CHUNK), GMLP_CHUNK ** -0.5),
        "ws_b": 1.0 + nrm(ks[6], (DEPTH, G_MLP, GMLP_CHUNK), 0.1),
        "ln_g": 1.0 + nrm(ks[7], (DEPTH, D_MLP), 0.02),
        "ln_b": nrm(ks[8], (DEPTH, D_MLP), 0.01),
        "w_ret_out": nrm(ks[9], (DEPTH, RV, D_MODEL), RV ** -0.5),
        "w_mlp_out": nrm(ks[10], (DEPTH, D_MLP, D_MODEL), D_MLP ** -0.5),
        "w_o": nrm(ks[11], (DEPTH, D_MODEL, D_MODEL), D_MODEL ** -0.5),
        "final_g": 1.0 + nrm(ks[12], (D_MODEL,), 0.02),
    }


def reference(x_prompt, x_sample, state_ret, norm_g, w_in, ws, ws_b, ln_g, ln_b,
              w_ret_out, w_mlp_out, w_o, final_g):
    b_p, s_p = x_prompt.shape[0], x_prompt.shape[1]
    s_s = x_sample.shape[1]
    pos_p = jnp.arange(s_p)
    pos_s = PAST_LEN + jnp.arange(s_s)
    hp, hs = x_prompt, x_sample
    st_p, st_s, v_s = [], [], []
    for l in range(DEPTH):
        w_l = (norm_g[l], w_in[l], ws[l], ws_b[l], ln_g[l], ln_b[l],
               w_ret_out[l], w_mlp_out[l], w_o[l])
        s0 = jnp.zeros((b_p, H_R, DK, DV), hp.dtype)
        hp, sp, _ = mixer_layer(hp, s0, pos_p, CHUNK, *w_l)
        hs, ss, vs = mixer_layer(hs, state_ret[l], pos_s, s_s, *w_l)
        st_p.append(sp)
        st_s.append(ss)
        v_s.append(vs)
    y_prompt = rmsnorm(hp, final_g)
    y_sample = rmsnorm(hs, final_g)
    new_state_ret_prompt = jnp.stack(st_p)
    new_state_ret_sample = jnp.stack(st_s)
    new_gmlp_v_sample = jnp.stack(v_s)
    return (y_prompt, y_sample, new_state_ret_prompt, new_state_ret_sample, new_gmlp_v_sample)
```

```python
import numpy as np
import ml_dtypes
from contextlib import ExitStack
import concourse.bass as bass
import concourse.mybir as mybir
from concourse.bass_utils import run_bass_kernel_spmd

F32, BF16 = mybir.dt.float32, mybir.dt.bfloat16
AF = mybir.ActivationFunctionType
ALU = mybir.AluOpType

D = 2048
SEQ = 2048
DEPTH = 4
SS = 32
PAST = 1024
H = 8
DK = 128
DV = 256
NIN = 16384
EPS = 1e-6
NS = 6
TMAX = 544
NTILE = 4
USE_WCACHE = True
WC_MOD = 1
ENGS = ("pe", "act", "dve", "pool", "sp")

C_ID, C_PERM, C_DT, C_QD, C_KD, C_MASK, C_ONES, C_DT32, C_KD32 = 0, 128, 256, 1280, 2304, 2312, 2440, 2568, 2824
NCONST = 2832


class Buf:
    __slots__ = ("name", "writers", "readers")

    def __init__(self, name, prior=()):
        self.name = name
        self.writers = list(prior)
        self.readers = []

    def pending(self):
        return list(self.writers) + list(self.readers)


class Op:
    __slots__ = ("eng", "fn", "deps", "sig", "sig_idx", "is_dma", "dsem", "dval", "ndma", "prev_on_sem")

    def __init__(self, eng, fn, is_dma=False, ndma=1):
        self.eng = eng
        self.fn = fn
        self.deps = []
        self.sig = False
        self.sig_idx = None
        self.is_dma = is_dma
        self.ndma = ndma
        self.dsem = None
        self.dval = None
        self.prev_on_sem = None


class Sched:
    def __init__(self, n_dma_sems=64):
        self.ops = {e: [] for e in ENGS}
        self.n_dma_sems = n_dma_sems
        self.dma_rr = 0
        self.dma_sem_val = [0] * n_dma_sems
        self.dma_sem_last = [None] * n_dma_sems
        self.stores = []

    def add(self, eng, fn, r=(), w=(), is_dma=False, ndma=1, store=False):
        op = Op(eng, fn, is_dma, ndma)
        deps = []
        seen = set()
        for b in r:
            for d in b.writers:
                if id(d) not in seen:
                    seen.add(id(d)); deps.append(d)
        for b in w:
            for d in b.writers:
                if id(d) not in seen:
                    seen.add(id(d)); deps.append(d)
            for d in b.readers:
                if id(d) not in seen:
                    seen.add(id(d)); deps.append(d)
        op.deps = [d for d in deps if d is not op]
        for b in r:
            b.readers.append(op)
        for b in w:
            b.writers = [op]
            b.readers = []
        if is_dma:
            k = self.dma_rr
            self.dma_rr = (self.dma_rr + 1) % self.n_dma_sems
            op.dsem = k
            op.prev_on_sem = self.dma_sem_last[k]
            self.dma_sem_val[k] += 16 * ndma
            op.dval = self.dma_sem_val[k]
            self.dma_sem_last[k] = op
        self.ops[eng].append(op)
        if store:
            self.stores.append(op)
        return op

    def finalize(self):
        for e in ENGS:
            for op in self.ops[e]:
                for d in op.deps:
                    if d.is_dma:
                        continue
                    if d.eng == "pe" and op.eng == "pe" and not op.is_dma:
                        continue
                    d.sig = True
        for e in ENGS:
            n = 0
            for op in self.ops[e]:
                if op.sig and not op.is_dma:
                    n += 1
                    op.sig_idx = n

    def emit(self, eng_name, e, esems, dsems):
        seen = {}
        for op in self.ops[eng_name]:
            waits = []
            for d in op.deps:
                if d.is_dma:
                    waits.append((("d", d.dsem), d.dval))
                else:
                    if d.eng == "pe" and eng_name == "pe" and not op.is_dma:
                        continue
                    waits.append((("e", d.eng), d.sig_idx))
            if op.is_dma and op.prev_on_sem is not None:
                p = op.prev_on_sem
                waits.append((("d", p.dsem), p.dval))
            need = {}
            for k, v in waits:
                if v > need.get(k, 0):
                    need[k] = v
            for k, v in need.items():
                if seen.get(k, 0) >= v:
                    continue
                seen[k] = v
                sem = dsems[k[1]] if k[0] == "d" else esems[k[1]]
                e.wait_ge(sem, v)
            ins = op.fn(e)
            if op.is_dma:
                if not isinstance(ins, (list, tuple)):
                    ins = [ins]
                assert len(ins) == op.ndma
                for i in ins:
                    i.then_inc(dsems[op.dsem], 16)
            elif op.sig:
                if isinstance(ins, (list, tuple)):
                    ins = ins[-1]
                ins.then_inc(esems[eng_name], 1)


class Arena:
    def __init__(self, t32, nwords):
        self.t32 = t32
        self.tb = t32.bitcast(BF16)
        self.nwords = nwords
        self.bufs = []
        self.off = 0
        self.prior = []

    def begin(self):
        prior = []
        seen = set()
        for b in self.bufs:
            for o in b.pending():
                if id(o) not in seen:
                    seen.add(id(o)); prior.append(o)
        self.prior = prior
        self.bufs = []
        self.off = 0

    def alloc(self, name, shape, dtype=F32):
        n = int(np.prod(shape[1:]))
        words = n if dtype == F32 else (n + 1) // 2
        words = (words + 7) // 8 * 8
        assert self.off + words <= self.nwords, (name, self.off, words, self.nwords)
        if dtype == F32:
            ap = self.t32[:, self.off:self.off + n]
        else:
            ap = self.tb[:, 2 * self.off:2 * self.off + n]
        if len(shape) == 3:
            ap = ap.rearrange("p (a b) -> p a b", a=shape[1])
        elif len(shape) == 4:
            ap = ap.rearrange("p (a b c) -> p a b c", a=shape[1], b=shape[2])
        self.off += words
        b = Buf(name, self.prior)
        self.bufs.append(b)
        return ap, b


def mkbufs(names, prior=()):
    return [Buf(n, prior) for n in names]


def pending_of(bufs):
    out, seen = [], set()
    for b in bufs:
        for o in b.pending():
            if id(o) not in seen:
                seen.add(id(o)); out.append(o)
    return out


def _tables():
    f = np.float32
    lg = np.log((1.0 - 2.0 ** (-5.0 - np.arange(H, dtype=f))).astype(f)).astype(f)
    c = np.zeros((128, NCONST), f)
    c[:, C_ID:C_ID + 128] = np.eye(128, dtype=f)
    for m in range(128):
        c[(m + 64) % 128, C_PERM + m] = 1.0
    idx = np.arange(128, dtype=f)
    dist = np.abs(idx[:, None] - idx[None, :]).astype(f)
    blk = (np.arange(128)[None, :] // 64) >= (np.arange(128)[:, None] // 64)
    for h in range(H):
        dt = np.exp((lg[h] * dist).astype(f)).astype(f) * blk.astype(f)
        c[:, C_DT + h * 128:C_DT + (h + 1) * 128] = dt
        c[:, C_QD + h * 128:C_QD + (h + 1) * 128] = np.exp((lg[h] * (idx + 1.0)).astype(f)).astype(f)[None, :]
        c[:, C_KD + h] = np.exp((lg[h] * (127.0 - idx)).astype(f)).astype(f)
        i32 = np.arange(32, dtype=f)
        d32 = np.abs(i32[:, None] - i32[None, :]).astype(f)
        c[0:32, C_DT32 + h * 32:C_DT32 + (h + 1) * 32] = np.exp((lg[h] * d32).astype(f)).astype(f)
        c[0:32, C_KD32 + h] = np.exp((lg[h] * (31.0 - i32)).astype(f)).astype(f)
    c[:, C_MASK:C_MASK + 128] = blk.astype(f)
    c[:, C_ONES:C_ONES + 128] = 1.0
    cd128 = [float(np.exp(np.float32(lg[h] * np.float32(128.0)))) for h in range(H)]
    cd32 = [float(np.exp(np.float32(lg[h] * np.float32(32.0)))) for h in range(H)]
    half = DK // 2
    inv_freq = (1.0 / (np.float32(10000.0) ** (np.arange(half, dtype=f) / np.float32(half)))).astype(f)
    pos = np.concatenate([np.arange(SEQ), PAST + np.arange(SS)]).astype(f)
    ang = (pos[:, None] * inv_freq[None, :]).astype(f)
    cos = np.cos(ang).astype(f).T
    sin = np.sin(ang).astype(f).T
    cs = np.zeros((2, 128, SEQ + SS), f)
    cs[0, 0:64] = cos
    cs[0, 64:128] = cos
    cs[1, 0:64] = -sin
    cs[1, 64:128] = sin
    return c, cs, cd128, cd32


def build(depth=DEPTH, ntile=NTILE, stop=None):
    consts_np, cs_np, cd128, cd32 = _tables()
    nc = bass.Bass("TRN2", target_bir_lowering=False)
    dt_ = nc.dram_tensor
    xp = dt_("xp", [SEQ, D], F32, kind="ExternalInput")
    xs = dt_("xs", [SS, D], F32, kind="ExternalInput")
    st0 = dt_("st0", [depth, H, DK, DV], F32, kind="ExternalInput")
    norm_g = dt_("norm_g", [depth, D], F32, kind="ExternalInput")
    w_in = dt_("w_in", [depth, D, NIN], F32, kind="ExternalInput")
    ws = dt_("ws", [depth, 8, 128, 128], F32, kind="ExternalInput")
    ws_b = dt_("ws_b", [depth, 8 * 128], F32, kind="ExternalInput")
    ln_g = dt_("ln_g", [depth, D], F32, kind="ExternalInput")
    ln_b = dt_("ln_b", [depth, D], F32, kind="ExternalInput")
    w_ro = dt_("w_ro", [depth, D, D], F32, kind="ExternalInput")
    w_mo = dt_("w_mo", [depth, D, D], F32, kind="ExternalInput")
    w_o = dt_("w_o", [depth, D, D], F32, kind="ExternalInput")
    final_g = dt_("final_g", [D], F32, kind="ExternalInput")
    cst = dt_("cst", [128, NCONST], F32, kind="ExternalInput")
    cs = dt_("cs", [2, 128, SEQ + SS], F32, kind="ExternalInput")
    yp = dt_("yp", [SEQ, D], F32, kind="ExternalOutput")
    ys = dt_("ys", [SS, D], F32, kind="ExternalOutput")
    stp = dt_("stp", [depth, H, DK, DV], F32, kind="ExternalOutput")
    sts = dt_("sts", [depth, H, DK, DV], F32, kind="ExternalOutput")
    vs = dt_("vs", [depth, SS, D], F32, kind="ExternalOutput")
    xscr = dt_("xscr", [SEQ + SS, D], F32, kind="Internal")
    wcs = [dt_(f"wcache{q}", [88, 128, 4096], BF16, kind="Internal") for q in range(2)]

    es = ExitStack()
    sb = lambda name, shape, dt: es.enter_context(nc.sbuf_tensor(name, shape, dt))
    hT = sb("hT", [128, 16, TMAX], BF16)
    gmT = sb("gmT", [128, 16, TMAX], BF16)
    Cr = sb("Cr", [128, 5 * 2048], F32)
    ring = sb("ring", [128, NS, 16, 256], BF16)
    CST = sb("CST", [128, NCONST], F32)
    IDB = sb("IDB", [128, 128], BF16)
    rows = sb("rows", [128, 2, 2048], F32)
    S32 = sb("S32", [128, H, DV], F32)
    WsT = sb("WsT", [128, 8, 128], BF16)
    AR = sb("AR", [128, 10240], F32)
    PS = es.enter_context(nc.psum_tensor("PS", [128, 8, 512], F32))
    PSb = PS.bitcast(BF16)
    Cb = Cr.bitcast(BF16)

    S = Sched()
    arena = Arena(AR, 10240)

    hTB = mkbufs([f"hT{j}" for j in range(5)])
    gmB = mkbufs([f"gm{b}" for b in range(16)])
    ringB = mkbufs([f"ring{s}" for s in range(NS)])
    psB = mkbufs([f"ps{k}" for k in range(8)])
    rowB = mkbufs(["row0", "row1"])
    S32B = mkbufs([f"S32_{h}" for h in range(H)])
    cstB = Buf("cst")
    idbB = Buf("idb")
    wstB = Buf("WsT")
    xscrB = mkbufs([f"xscr{j}" for j in range(17)])
    state = {"ring": 0, "ps": 0, "Cbufs": []}

    def ps_alloc():
        lim = state.get("pslim", 8)
        k = state["ps"] % lim
        state["ps"] = (k + 1) % lim
        return k, psB[k]

    wcBs = [mkbufs([f"wc{q}_{i}" for i in range(96)]) for q in range(2)]
    convB = mkbufs(["conv0", "conv1"])
    state["fidx"] = 0
    state["wb"] = []
    state["cur"] = (0, 0)
    state["units"] = []
    state["conv"] = None

    def flush_wb(keep):
        while len(state["wb"]) > keep:
            (idx, s) = state["wb"].pop(0)
            S.add("pool", lambda e, idx=idx, s=s: e.dma_start(out=wcs[0][idx], in_=ring[:, s].rearrange("p k c -> p (k c)")),
                  r=[ringB[s]], w=[wcBs[0][idx]], is_dma=True)

    def maybe_convert():
        cv = state["conv"]
        if cv is None:
            return
        lc, nxt_i, cnt, every = cv
        cnt += 1
        if cnt >= every and nxt_i < len(state["units"]):
            cnt = 0
            tens, c0 = state["units"][nxt_i]
            q = lc % 2
            i_ = nxt_i
            S.add("pool", lambda e, tens=tens, c0=c0, q=q, i_=i_, lc=lc: e.dma_start(
                out=wcs[q][i_].rearrange("p (k c) -> p k c", k=16),
                in_=tens[lc, :, c0:c0 + 256].rearrange("(kc p) c -> p kc c", p=128)),
                w=[wcBs[q][i_], convB[i_ % 2]], is_dma=True)
            nxt_i += 1
        state["conv"] = (lc, nxt_i, cnt, every)

    def fetch_u(tens, c0):
        s = state["ring"]
        state["ring"] = (s + 1) % NS
        l_, t_ = state["cur"]
        idx = state["fidx"]; state["fidx"] += 1
        if l_ == 0 and t_ == 0:
            state["units"].append((tens, c0))
            flush_wb(1)
            S.add("pool", lambda e, s=s, tens=tens, c0=c0: e.dma_start(
                out=ring[:, s, :, :], in_=tens[0, :, c0:c0 + 256].rearrange("(kc p) c -> p kc c", p=128)),
                w=[ringB[s]], is_dma=True)
            if ntile > 1:
                state["wb"].append((idx, s))
        else:
            q = l_ % 2
            S.add("pool", lambda e, s=s, idx=idx, q=q: e.dma_start(out=ring[:, s].rearrange("p k c -> p (k c)"), in_=wcs[q][idx]),
                  r=[wcBs[q][idx]], w=[ringB[s]], is_dma=True)
            maybe_convert()
        return s

    S.add("sp", lambda e: e.dma_start(out=CST[:], in_=cst.ap()), w=[cstB], is_dma=True)
    S.add("dve", lambda e: e.tensor_copy(IDB[:], CST[:, C_ID:C_ID + 128]), r=[cstB], w=[idbB])

    def tile_chunks(t):
        ch = []
        for j in range(4):
            g = t * 4 + j
            ch.append(dict(j=j, off=j * 128, P=128, g=g, row0=g * 128, sample=False))
        if t == ntile - 1:
            ch.append(dict(j=4, off=512, P=32, g=16, row0=SEQ, sample=True))
        return ch

    def tile_pieces(t):
        if t == ntile - 1:
            return [(0, 256), (256, 288)]
        return [(0, 512)]

    def chunks_in(chs, a, n):
        return [c for c in chs if c["off"] >= a and c["off"] < a + n]

    def x_src(l, c, cols=None):
        if l == 0:
            base = xs.ap() if c["sample"] else xp[c["row0"]:c["row0"] + c["P"], :]
        else:
            base = xscr[c["row0"]:c["row0"] + c["P"], :]
        return base if cols is None else base[:, cols[0]:cols[1]]

    def rstd_chain(x_ap, xB, eps, k, P, tmp):
        xe, xeB = tmp["xe"]; s_, sB = tmp["s"]; y, yB = tmp["y"]; t_, tB = tmp["t"]
        sl = lambda a: a[0:P, 0:k]
        S.add("dve", lambda e: e.tensor_scalar_add(sl(xe), x_ap, eps), r=[xB], w=[xeB])
        S.add("act", lambda e: e.activation(sl(s_), sl(xe), AF.Sqrt), r=[xeB], w=[sB])
        S.add("dve", lambda e: e.reciprocal(sl(y), sl(s_)), r=[sB], w=[yB])
        for _ in range(1):
            S.add("dve", lambda e: e.tensor_tensor(sl(t_), sl(y), sl(y), ALU.mult), r=[yB], w=[tB])
            S.add("dve", lambda e: e.tensor_tensor(sl(t_), sl(t_), sl(xe), ALU.mult), r=[tB, xeB], w=[tB])
            S.add("dve", lambda e: e.tensor_scalar(sl(t_), sl(t_), -0.5, 1.5, ALU.mult, ALU.add), r=[tB], w=[tB])
            S.add("dve", lambda e: e.tensor_tensor(sl(y), sl(y), sl(t_), ALU.mult), r=[yB, tB], w=[yB])
        return y, yB

    def mm_group(out_fn, pieces, lhs_fn, rhs_fn, reads, pbanks):
        def fn(e):
            last = None
            for kc in range(16):
                for pi, (a, n) in enumerate(pieces):
                    last = e.matmul(out_fn(pi, a, n), lhsT=lhs_fn(kc), rhs=rhs_fn(kc, a, n),
                                    start=(kc == 0), stop=(kc == 15))
            return last
        S.add("pe", fn, r=reads, w=[psB[k] for k in pbanks])

    io = {}

    def io_view():
        arena.begin()
        io["xin"] = [arena.alloc(f"xin{i}", [128, 2048]) for i in range(2)]
        io["hb"] = arena.alloc("hb", [128, 2048], BF16)
        io["junk"] = arena.alloc("junk", [128, 2048], BF16)
        io["xi"] = [arena.alloc(f"xi{i}", [128, 768]) for i in range(2)]
        io["xo"] = [arena.alloc(f"xo{i}", [128, 768]) for i in range(2)]
        io["st"] = arena.alloc("st", [128, 4, 6])
        io["mv"] = arena.alloc("mv", [128, 2])
        io["msq"] = arena.alloc("msq", [128, 1])
        io["tmp"] = {n: arena.alloc("r" + n, [128, 1]) for n in ("xe", "s", "y", "t")}
        io["xin_i"] = 0

    def norm_load(c, src_ap):
        P = c["P"]
        i = io["xin_i"]; io["xin_i"] ^= 1
        xin, xinB = io["xin"][i]
        S.add("sp", lambda e: e.dma_start(out=xin[0:P, :], in_=src_ap), r=[xscrB[c["g"]]], w=[xinB], is_dma=True)
        return xin, xinB

    def norm_dve(c, src_ap, rowbuf, out_kind):
        xp_ = norm_load(c, src_ap)
        norm_square(c, xp_)
        norm_compute(c, xp_, rowbuf, out_kind)

    def norm_square(c, xpair):
        P = c["P"]
        xin, xinB = xpair
        mv, mvB = io["mv"]
        hbj, hbjB = io["junk"]
        S.add("act", lambda e: e.activation(hbj[0:P, :], xin[0:P, :], AF.Square, accum_out=mv[0:P, 0:1]), r=[xinB], w=[hbjB, mvB])

    def norm_compute(c, xpair, rowbuf, out_kind):
        P = c["P"]
        xin, xinB = xpair
        mv, mvB = io["mv"]; msq, msqB = io["msq"]
        S.add("dve", lambda e: e.tensor_scalar_mul(msq[0:P, :], mv[0:P, 0:1], 1.0 / D), r=[mvB], w=[msqB])
        y, yB = rstd_chain(msq[0:P, 0:1], msqB, EPS, 1, P, io["tmp"])
        if out_kind == "y":
            S.add("dve", lambda e: e.scalar_tensor_tensor(xin[0:P, :], xin[0:P, :], y[0:P, 0:1], rows[0:P, 0, :], ALU.mult, ALU.mult),
                  r=[xinB, yB, rowbuf], w=[xinB])
            dst = ys.ap() if c["sample"] else yp[c["row0"]:c["row0"] + P, :]
            S.add("sp", lambda e: e.dma_start(out=dst, in_=xin[0:P, :]), r=[xinB], is_dma=True, store=True)
            return
        hb, hbB = io["hb"]
        S.add("dve", lambda e: e.scalar_tensor_tensor(hb[0:P, :], xin[0:P, :], y[0:P, 0:1], rows[0:P, 0, :], ALU.mult, ALU.mult),
              r=[xinB, yB, rowbuf], w=[hbB])

    def norm_pe(c):
        P = c["P"]
        hb, hbB = io["hb"]
        for half in range(2):
            k, kB = ps_alloc()

            def fn(e, k=k, half=half):
                last = None
                for q in range(8):
                    kc = half * 8 + q
                    last = e.transpose(PSb[:, k, q * 128:q * 128 + P], hb[0:P, kc * 128:(kc + 1) * 128], IDB[0:P, 0:P])
                return last
            S.add("pe", fn, r=[hbB, idbB], w=[kB])
            src = PSb[:, k, :].rearrange("p (q c) -> p q c", q=8)[:, :, 0:P]
            dstap = hT[:, half * 8:(half + 1) * 8, c["off"]:c["off"] + P]
            S.add("act", lambda e, src=src, dstap=dstap: e.copy(dstap, src), r=[kB], w=[hTB[c["j"]]])

    pre0 = {}

    def phase0_steps(l, t):
        chs = tile_chunks(t)
        steps = []
        steps.append(lambda: S.add("sp", lambda e: e.dma_start(out=rows[:, 0, :], in_=norm_g[l].partition_broadcast(128)),
                                   w=[rowB[0]], is_dma=True))
        hold = {}

        def L(c):
            if c["j"] == 0 and (l, t) in pre0:
                hold[0] = pre0.pop((l, t))
            else:
                hold[c["j"]] = norm_load(c, x_src(l, c))
            norm_square(c, hold[c["j"]])
        for c in chs:
            steps.append((lambda c=c: L(c), lambda c=c: norm_compute(c, hold[c["j"]], rowB[0], "hT"), lambda c=c: norm_pe(c)))
        return steps

    def tile_layer(l, t, nxt):
        state["cur"] = (l, t)
        state["fidx"] = 0
        if t == 0 and state["conv"] is not None:
            lc, nxt_i, cnt, every = state["conv"]
            while nxt_i < len(state["units"]):
                state["conv"] = (lc, nxt_i, every, every)
                maybe_convert()
                lc, nxt_i, cnt, every = state["conv"]
            state["conv"] = None
        if l + 1 < depth:
            if l == 0 and t == 1:
                state["conv"] = (1, 0, 0, 3)
            elif l >= 1 and t == 0:
                state["conv"] = (l + 1, 0, 0, 4)
        chs = tile_chunks(t)
        pieces = tile_pieces(t)
        sch = [c for c in chs if c["sample"]]
        nch = len(chs)
        npc = len(pieces)
        maxn = max(n for _, n in pieces)

        arena.begin()
        etmp0 = arena.alloc("etmp0", [128, 512])
        t1, t1B = arena.alloc("t1", [128, 2048])
        mst, mstB = arena.alloc("mst", [128, 4, 6])
        s12, s12B = arena.alloc("s12", [128, 2, 5, 4])
        S12, S12B = arena.alloc("S12", [128, 2, 5])
        lmean, lmeanB = arena.alloc("lmean", [128, 5])
        lvar, lvarB = arena.alloc("lvar", [128, 5])
        ltmp = {n: arena.alloc("l" + n, [128, 5]) for n in ("xe", "s", "y", "t")}
        mmv, mmvB = arena.alloc("mmv", [128, 2])
        mtmp = {n: arena.alloc("m" + n, [128, 1]) for n in ("xe", "s", "y", "t")}
        eU_all, _ = arena.alloc("eUall", [128, 2 * npc * maxn])
        eUBs = mkbufs([f"eU{i}" for i in range(2 * npc)], prior=arena.prior); arena.bufs += eUBs
        eU = [[(eU_all[:, (i * npc + p) * maxn:(i * npc + p + 1) * maxn], eUBs[i * npc + p]) for p in range(npc)] for i in range(2)]
        junk = eU_all.bitcast(BF16)[:, 0:2048]
        sG = [[arena.alloc(f"sG{i}{p}", [128, maxn]) for p in range(npc)] for i in range(2)]
        aT = [[arena.alloc(f"aT{i}{p}", [128, maxn]) for p in range(npc)] for i in range(2)]
        bT = [[arena.alloc(f"bT{i}{p}", [128, maxn]) for p in range(npc)] for i in range(2)]
        cT, cB = arena.alloc("cT", [128, max(maxn, 512)])
        etmp = [etmp0, (cT, cB)]
        wsb, wsbB = arena.alloc("wsb", [128, 1024])
        if t == 0:
            wsraw, wsrawB = arena.alloc("wsraw", [128, 8, 128])
        gvB = mkbufs([f"gv{j}" for j in range(5)], prior=pending_of(state["Cbufs"]))
        state["Cbufs"] = gvB
        gv = lambda j: Cr[:, j * 2048:(j + 1) * 2048]
        vnb = lambda j: Cb[:, j * 4096:j * 4096 + 2048]

        S.add("sp", lambda e: e.dma_start(out=rows[:, 1, :], in_=ln_b[l].partition_broadcast(128)), w=[rowB[1]], is_dma=True)
        S.add("sp", lambda e: e.dma_start(out=wsb[:, :], in_=ws_b[l].partition_broadcast(128)), w=[wsbB], is_dma=True)
        if t == 0:
            S.add("sp", lambda e: e.dma_start(out=wsraw, in_=ws[l].rearrange("g i j -> i g j")), w=[wsrawB], is_dma=True)
            for hf in range(2):
                k, kB = ps_alloc()

                def fn(e, k=k, hf=hf):
                    last = None
                    for q in range(4):
                        g = hf * 4 + q
                        last = e.transpose(PS[:, k, q * 128:(q + 1) * 128], wsraw[:, g, :], CST[:, C_ID:C_ID + 128])
                    return last
                S.add("pe", fn, r=[wsrawB, cstB], w=[kB])
                src = PS[:, k, :].rearrange("p (q c) -> p q c", q=4)
                msk = CST[:, C_MASK:C_MASK + 128].unsqueeze(1).to_broadcast([128, 4, 128])
                S.add("dve", lambda e, src=src, hf=hf, msk=msk: e.tensor_tensor(WsT[:, hf * 4:(hf + 1) * 4, :], src, msk, ALU.mult),
                      r=[kB, cstB], w=[wstB])

        ei = 0
        if state["ring"] % 2 == 1:
            state["ring"] = (state["ring"] + 1) % NS
        for up in range(4):
            s0 = fetch_u(w_in, 8192 + up * 512)
            s1 = fetch_u(w_in, 8192 + up * 512 + 256)
            assert s1 == s0 + 1
            for c in chs:
                P = c["P"]; off = c["off"]
                k, kB = ps_alloc()

                def fn(e, k=k, s0=s0, P=P, off=off):
                    last = None
                    for kc in range(16):
                        last = e.matmul(PS[0:P, k, :].rearrange("p (a b) -> p a b", a=2), lhsT=hT[:, kc, off:off + P],
                                        rhs=ring[:, s0:s0 + 2, kc, :], start=(kc == 0), stop=(kc == 15))
                    return last
                S.add("pe", fn, r=[hTB[c["j"]], ringB[s0], ringB[s1]], w=[kB])
                et, etB = etmp[ei]; ei ^= 1
                S.add("act", lambda e, et=et, k=k, P=P: e.activation(et[0:P, 0:512], PS[0:P, k, :], AF.Erf, scale=0.7071067811865476),
                      r=[kB], w=[etB])
                S.add("dve", lambda e, et=et, k=k, P=P, j=c["j"], up=up: e.scalar_tensor_tensor(
                    gv(j)[0:P, up * 512:(up + 1) * 512], et[0:P, 0:512], 1.0, PS[0:P, k, :], ALU.add, ALU.mult,
                    accum_out=s12[0:P, 0, j, up:up + 1]),
                    r=[etB, kB], w=[gvB[c["j"]], s12B])
                S.add("act", lambda e, P=P, j=c["j"], up=up: e.activation(junk[0:P, 0:512], gv(j)[0:P, up * 512:(up + 1) * 512], AF.Square,
                                                                         accum_out=s12[0:P, 1, j, up:up + 1]),
                      r=[gvB[c["j"]]], w=eUBs + [s12B])
        S.add("sp", lambda e: e.dma_start(out=rows[:, 0, :], in_=ln_g[l].partition_broadcast(128)), w=[rowB[0]], is_dma=True)

        s12v = s12.rearrange("p a c k -> p (a c) k")
        S12v = S12.rearrange("p a c -> p (a c)")
        S.add("dve", lambda e: e.tensor_tensor(S12v, s12v[:, :, 0], s12v[:, :, 1], ALU.add), r=[s12B], w=[S12B])
        S.add("dve", lambda e: e.tensor_tensor(S12v, S12v, s12v[:, :, 2], ALU.add), r=[s12B, S12B], w=[S12B])
        S.add("dve", lambda e: e.tensor_tensor(S12v, S12v, s12v[:, :, 3], ALU.add), r=[s12B, S12B], w=[S12B])
        S.add("dve", lambda e: e.tensor_scalar_mul(lmean[:, :], S12[:, 0, :], 1.0 / D), r=[S12B], w=[lmeanB])
        S.add("dve", lambda e: e.tensor_tensor(lvar[:, :], lmean[:, :], lmean[:, :], ALU.mult), r=[lmeanB], w=[lvarB])
        S.add("dve", lambda e: e.scalar_tensor_tensor(lvar[:, :], S12[:, 1, :], 1.0 / D, lvar[:, :], ALU.mult, ALU.subtract),
              r=[S12B, lvarB], w=[lvarB])
        lrstd, lrstdB = rstd_chain(lvar[:, 0:5], lvarB, 4.0 * EPS, 5, 128, ltmp)

        vnhB = [[Buf(f"vn{j}_{hf}") for hf in range(2)] for j in range(5)]
        state["Cbufs"] = gvB + [b_ for row_ in vnhB for b_ in row_]

        def ln_half(c, hf):
            P = c["P"]; j = c["j"]
            c0, c1 = hf * 1024, (hf + 1) * 1024
            y, yB = lrstd[:, j:j + 1], lrstdB
            S.add("dve", lambda e: e.scalar_tensor_tensor(t1[0:P, c0:c1], gv(j)[0:P, c0:c1], lmean[0:P, j:j + 1], rows[0:P, 0, c0:c1],
                                                          ALU.subtract, ALU.mult),
                  r=[gvB[j], lmeanB, rowB[0]], w=[t1B])
            if not c["sample"]:
                S.add("dve", lambda e: e.scalar_tensor_tensor(vnb(j)[0:P, c0:c1], t1[0:P, c0:c1], y[0:P, 0:1], rows[0:P, 1, c0:c1],
                                                              ALU.mult, ALU.add),
                      r=[t1B, yB, rowB[1]], w=[gvB[j], vnhB[j][hf]])
            else:
                S.add("dve", lambda e: e.scalar_tensor_tensor(gv(j)[0:P, c0:c1], t1[0:P, c0:c1], y[0:P, 0:1], rows[0:P, 1, c0:c1],
                                                              ALU.mult, ALU.add),
                      r=[t1B, yB, rowB[1]], w=[gvB[j]])
        for hf in range(2):
            for c in chs:
                ln_half(c, hf)
        for c in chs:
            if c["sample"]:
                P = c["P"]; j = c["j"]
                S.add("sp", lambda e, P=P, j=j: e.dma_start(out=vs[l], in_=gv(j)[0:P, :]), r=[gvB[j]], is_dma=True, store=True)
                t1b = t1.bitcast(BF16)
                S.add("act", lambda e, P=P, j=j, t1b=t1b: e.copy(t1b[0:P, 0:2048], gv(j)[0:P, :]), r=[gvB[j]], w=[t1B])

        if nxt is not None:
            c0n = tile_chunks(nxt[1])[0]
            srcn = x_src(nxt[0], c0n)
            S.add("sp", lambda e: e.dma_start(out=rows[:, 1, :], in_=srcn), r=[xscrB[c0n["g"]]], w=[rowB[1]], is_dma=True)
            pre0[nxt] = (rows[:, 1, :], rowB[1])

        def vn_lhs(c, b):
            if c["sample"]:
                return t1.bitcast(BF16)[0:c["P"], b * 128:(b + 1) * 128], t1B
            return vnb(c["j"])[0:c["P"], b * 128:(b + 1) * 128], vnhB[c["j"]][b // 8]

        m2 = {}

        def m2_A(b):
            if b % 2 == 0:
                m2["sU"] = fetch_u(w_in, 6144 + b * 128)
                m2["sG"] = fetch_u(w_in, 10240 + b * 128)
            sU2, sG2 = m2["sU"], m2["sG"]
            bo = (b % 2) * 128
            for pi, (a, n) in enumerate(pieces):
                pcs = chunks_in(chs, a, n)
                kU, kUB = ps_alloc(); kG, kGB = ps_alloc()
                hr = [hTB[c["j"]] for c in pcs]
                mm_group(lambda pi_, a_, n_, kU=kU: PS[:, kU, 0:n_], [(a, n)],
                         lambda kc, s=sU2, bo=bo: ring[:, s, kc, bo:bo + 128], lambda kc, a_, n_: hT[:, kc, a_:a_ + n_],
                         hr + [ringB[sU2]], [kU])
                mm_group(lambda pi_, a_, n_, kG=kG: PS[:, kG, 0:n_], [(a, n)],
                         lambda kc, s=sG2, bo=bo: ring[:, s, kc, bo:bo + 128], lambda kc, a_, n_: hT[:, kc, a_:a_ + n_],
                         hr + [ringB[sG2]], [kG])
                eu, euB = eU[b % 2][pi]; sg, sgB = sG[b % 2][pi]
                a_t, a_B = aT[b % 2][pi]; b_t, b_B = bT[b % 2][pi]
                S.add("act", lambda e, eu=eu, kU=kU, n=n: e.activation(eu[:, 0:n], PS[:, kU, 0:n], AF.Erf, scale=0.7071067811865476),
                      r=[kUB], w=[euB])
                S.add("act", lambda e, sg=sg, kG=kG, n=n: e.activation(sg[:, 0:n], PS[:, kG, 0:n], AF.Sigmoid), r=[kGB], w=[sgB])
                S.add("dve", lambda e, eu=eu, kU=kU, n=n, a_t=a_t: e.scalar_tensor_tensor(a_t[:, 0:n], eu[:, 0:n], 1.0, PS[:, kU, 0:n], ALU.add, ALU.mult),
                      r=[euB, kUB], w=[a_B])
                S.add("dve", lambda e, sg=sg, kG=kG, n=n, b_t=b_t: e.tensor_tensor(b_t[:, 0:n], sg[:, 0:n], PS[:, kG, 0:n], ALU.mult),
                      r=[sgB, kGB], w=[b_B])

        def m2_B(b):
            g = b // 2
            for pi, (a, n) in enumerate(pieces):
                pcs = chunks_in(chs, a, n)
                kS, kSB = ps_alloc()

                def fnS(e, kS=kS, pcs=pcs, a=a):
                    last = None
                    for c in pcs:
                        P = c["P"]; o = c["off"] - a
                        lhs, _ = vn_lhs(c, b)
                        last = e.matmul(PS[:, kS, o:o + P], lhsT=lhs, rhs=WsT[0:P, g, 0:P], start=True, stop=True)
                    return last
                S.add("pe", fnS, r=[vn_lhs(c, b)[1] for c in pcs] + [wstB], w=[kSB])
                a_t, a_B = aT[b % 2][pi]; b_t, b_B = bT[b % 2][pi]
                npc_ = [c for c in pcs if not c["sample"]]
                if npc_:
                    o0 = npc_[0]["off"] - a; m_ = len(npc_)
                    btab = wsb[:, g * 128:(g + 1) * 128].unsqueeze(1).to_broadcast([128, m_, 128])
                    S.add("dve", lambda e, kS=kS, o0=o0, m_=m_, btab=btab: e.tensor_tensor(
                        cT[:, o0:o0 + m_ * 128].rearrange("p (c i) -> p c i", c=m_),
                        PS[:, kS, o0:o0 + m_ * 128].rearrange("p (c i) -> p c i", c=m_), btab, ALU.add),
                        r=[kSB, wsbB], w=[cB])
                for c in pcs:
                    if c["sample"]:
                        o = c["off"] - a
                        S.add("dve", lambda e, kS=kS, o=o: e.tensor_tensor(cT[:, o:o + 32], PS[:, kS, o:o + 32], wsb[:, g * 128:g * 128 + 32], ALU.add),
                              r=[kSB, wsbB], w=[cB])
                S.add("dve", lambda e, n=n, a_t=a_t: e.tensor_tensor(cT[:, 0:n], cT[:, 0:n], a_t[:, 0:n], ALU.mult),
                      r=[a_B, cB], w=[cB])
                S.add("dve", lambda e, a=a, n=n, b_t=b_t: e.scalar_tensor_tensor(gmT[:, b, a:a + n], cT[:, 0:n], 0.5, b_t[:, 0:n], ALU.mult, ALU.mult),
                      r=[cB, b_B], w=[gmB[b]])
        m2_A(0)
        for b in range(16):
            if b + 1 < 16:
                m2_A(b + 1)
            m2_B(b)

        if stop == 'm':
            return True
        arena.begin()
        cosT, cosB = arena.alloc("cosT", [128, TMAX])
        sinT, sinB = arena.alloc("sinT", [128, TMAX])
        xsb, xsbB = arena.alloc("xsb", [128, TMAX])
        r1, r1B = arena.alloc("r1", [128, TMAX])
        r2, r2B = arena.alloc("r2", [128, TMAX])
        hb_ = []
        for i in range(2):
            d_ = {}
            d_["qTb"] = arena.alloc(f"qTb{i}", [128, TMAX], BF16)
            d_["kTb"] = arena.alloc(f"kTb{i}", [128, TMAX], BF16)
            d_["qdTb"] = arena.alloc(f"qdTb{i}", [128, TMAX], BF16)
            d_["vb"], _ = arena.alloc(f"vb{i}", [128, 5, 256], BF16)
            d_["vbB"] = mkbufs([f"vb{i}_{j}" for j in range(5)], prior=arena.prior); arena.bufs += d_["vbB"]
            d_["sgr"], _ = arena.alloc(f"sgr{i}", [128, 2, TMAX])
            d_["sgrB"] = mkbufs([f"sgr{i}_0", f"sgr{i}_1"], prior=arena.prior); arena.bufs += d_["sgrB"]
            d_["Ss32"] = arena.alloc(f"Ss32_{i}", [128, 256])
            hb_.append(d_)
        kdec, _ = arena.alloc("kdec", [128, 5, 128], BF16)
        kdecB = mkbufs([f"kdec{j}" for j in range(5)], prior=arena.prior); arena.bufs += kdecB
        Pm, _ = arena.alloc("Pm", [128, 5, 128], BF16)
        PmB = mkbufs([f"Pm{j}" for j in range(5)], prior=arena.prior); arena.bufs += PmB
        onb, _ = arena.alloc("onb", [128, 2, 256], BF16)
        onB = mkbufs(["on0", "on1"], prior=arena.prior); arena.bufs += onB
        Sbv, _ = arena.alloc("Sbv", [128, 5, 256], BF16)
        SbvB = mkbufs([f"Sbv{j}" for j in range(5)], prior=arena.prior); arena.bufs += SbvB
        ost, ostB = arena.alloc("ost", [128, 5, 6])
        omv, omvB = arena.alloc("omv", [128, 5, 2])
        nmr, nmrB = arena.alloc("nmr", [128, 5])
        otmp = {n: arena.alloc("o" + n, [128, 5]) for n in ("xe", "s", "y", "t")}
        goB = mkbufs([f"go{b}" for b in range(16)], prior=pending_of(state["Cbufs"]))
        mgB = mkbufs([f"mg{b}" for b in range(16)], prior=pending_of(state["Cbufs"]))
        state["Cbufs"] = goB + mgB
        goT = Cb[:, 0:16 * TMAX].rearrange("p (k t) -> p k t", k=16)
        mgT = Cb[:, 16 * TMAX:32 * TMAX].rearrange("p (k t) -> p k t", k=16)

        def fn_cs(e, which, dst):
            out = [e.dma_start(out=dst[:, 0:512], in_=cs[which, :, t * 512:(t + 1) * 512])]
            if sch:
                out.append(e.dma_start(out=dst[:, 512:544], in_=cs[which, :, SEQ:SEQ + SS]))
            return out
        S.add("sp", lambda e: fn_cs(e, 0, cosT), w=[cosB], is_dma=True, ndma=1 + len(sch))
        S.add("sp", lambda e: fn_cs(e, 1, sinT), w=[sinB], is_dma=True, ndma=1 + len(sch))
        if t == 0:
            for h in range(H):
                S.add("dve", lambda e, h=h: e.memset(S32[:, h, :], 0.0), w=[S32B[h]])
        rs = {}

        def proj_qk(h):
            hbuf = hb_[h % 2]
            if h % 2 == 0:
                rs["sQ"] = fetch_u(w_in, h * 128)
                rs["sK"] = fetch_u(w_in, 1024 + h * 128)
            ho = (h % 2) * 128
            if sch:
                Ss32, Ss32B = hbuf["Ss32"]
                S.add("sp", lambda e: e.dma_start(out=Ss32, in_=st0[l, h]), w=[Ss32B], is_dma=True)
            qTb, qTbB = hbuf["qTb"]; kTb, kTbB = hbuf["kTb"]; qdTb, qdTbB = hbuf["qdTb"]
            for which in range(2):
                for (a, n) in pieces:
                    pcs = chunks_in(chs, a, n)
                    hr = [hTB[c["j"]] for c in pcs]
                    k1, k1B = ps_alloc()
                    sqk = rs["sQ"] if which == 0 else rs["sK"]
                    mm_group(lambda pi, a_, n_, k1=k1: PS[:, k1, 0:n_], [(a, n)],
                             lambda kc, s=sqk, ho=ho: ring[:, s, kc, ho:ho + 128],
                             lambda kc, a_, n_: hT[:, kc, a_:a_ + n_], hr + [ringB[sqk]], [k1])
                    if which == 0:
                        S.add("act", lambda e, k1=k1, a=a, n=n: e.copy(xsb[:, a:a + n], PS[:, k1, 0:n]), r=[k1B], w=[xsbB])
                    else:
                        S.add("act", lambda e, k1=k1, a=a, n=n: e.mul(xsb[:, a:a + n], PS[:, k1, 0:n], float(DK) ** -0.5), r=[k1B], w=[xsbB])
                    k2, k2B = ps_alloc()
                    S.add("pe", lambda e, k2=k2, a=a, n=n: e.matmul(PS[:, k2, 0:n], lhsT=CST[:, C_PERM:C_PERM + 128], rhs=xsb[:, a:a + n],
                                                                    start=True, stop=True), r=[xsbB, cstB], w=[k2B])
                    S.add("dve", lambda e, a=a, n=n: e.tensor_tensor(r1[:, a:a + n], xsb[:, a:a + n], cosT[:, a:a + n], ALU.mult),
                          r=[xsbB, cosB], w=[r1B])
                    S.add("dve", lambda e, k2=k2, a=a, n=n: e.tensor_tensor(r2[:, a:a + n], PS[:, k2, 0:n], sinT[:, a:a + n], ALU.mult),
                          r=[k2B, sinB], w=[r2B])
                    if which == 0:
                        S.add("dve", lambda e, a=a, n=n: e.tensor_tensor(r1[:, a:a + n], r1[:, a:a + n], r2[:, a:a + n], ALU.add),
                              r=[r1B, r2B], w=[r1B])
                        S.add("act", lambda e, a=a, n=n: e.copy(qTb[:, a:a + n], r1[:, a:a + n]), r=[r1B], w=[qTbB])
                        npc_ = [c for c in pcs if not c["sample"]]
                        if npc_:
                            a0 = npc_[0]["off"]; m = len(npc_)
                            qdt = CST[:, C_QD + h * 128:C_QD + (h + 1) * 128].unsqueeze(1).to_broadcast([128, m, 128])
                            S.add("dve", lambda e, a0=a0, m=m, qdt=qdt: e.tensor_tensor(
                                qdTb[:, a0:a0 + m * 128].rearrange("p (c i) -> p c i", c=m),
                                r1[:, a0:a0 + m * 128].rearrange("p (c i) -> p c i", c=m), qdt, ALU.mult),
                                r=[r1B, cstB], w=[qdTbB])
                        for c in pcs:
                            if c["sample"]:
                                o = c["off"]
                                S.add("dve", lambda e, o=o: e.tensor_tensor(qdTb[:, o:o + 32], r1[:, o:o + 32],
                                                                           CST[:, C_QD + h * 128:C_QD + h * 128 + 32], ALU.mult),
                                      r=[r1B, cstB], w=[qdTbB])
                    else:
                        S.add("dve", lambda e, a=a, n=n: e.tensor_tensor(kTb[:, a:a + n], r1[:, a:a + n], r2[:, a:a + n], ALU.add),
                              r=[r1B, r2B], w=[kTbB])

        def proj_vg(h):
            hbuf = hb_[h % 2]
            vb, vbB = hbuf["vb"], hbuf["vbB"]; sgr, sgrB = hbuf["sgr"], hbuf["sgrB"]
            sV = fetch_u(w_in, 2048 + h * 256)
            sGR = fetch_u(w_in, 4096 + h * 256)
            for c in chs:
                P = c["P"]; off = c["off"]; j = c["j"]
                k, kB = ps_alloc()

                def fn(e, k=k, s=sV, P=P, off=off):
                    last = None
                    for kc in range(16):
                        last = e.matmul(PS[0:P, k, 0:256], lhsT=hT[:, kc, off:off + P], rhs=ring[:, s, kc, :],
                                        start=(kc == 0), stop=(kc == 15))
                    return last
                S.add("pe", fn, r=[hTB[j], ringB[sV]], w=[kB])
                S.add("act", lambda e, k=k, P=P, j=j: e.copy(vb[0:P, j, :], PS[0:P, k, 0:256]), r=[kB], w=[vbB[j]])
            for half in range(2):
                for (a, n) in pieces:
                    pcs = chunks_in(chs, a, n)
                    hr = [hTB[c["j"]] for c in pcs]
                    k1, k1B = ps_alloc()
                    mm_group(lambda pi, a_, n_, k1=k1: PS[:, k1, 0:n_], [(a, n)],
                             lambda kc, s=sGR, half=half: ring[:, s, kc, half * 128:(half + 1) * 128],
                             lambda kc, a_, n_: hT[:, kc, a_:a_ + n_], hr + [ringB[sGR]], [k1])
                    S.add("act", lambda e, k1=k1, a=a, n=n, half=half: e.activation(sgr[:, half, a:a + n], PS[:, k1, 0:n], AF.Sigmoid),
                          r=[k1B], w=[sgrB[half]])
                    S.add("dve", lambda e, k1=k1, a=a, n=n, half=half: e.tensor_tensor(sgr[:, half, a:a + n], sgr[:, half, a:a + n], PS[:, k1, 0:n], ALU.mult),
                          r=[sgrB[half], k1B], w=[sgrB[half]])

        rec = {}

        def rec1(h):
            hbuf = hb_[h % 2]
            qTb, qTbB = hbuf["qTb"]; kTb, kTbB = hbuf["kTb"]
            vb, vbB = hbuf["vb"], hbuf["vbB"]
            Ss32, Ss32B = hbuf["Ss32"]
            for c in chs:
                P = c["P"]; off = c["off"]; j = c["j"]; smp = c["sample"]
                k, kB = ps_alloc()
                S.add("pe", lambda e, k=k, P=P, off=off: e.transpose(PSb[0:P, k, 0:128], kTb[:, off:off + P], IDB[:, :]),
                      r=[kTbB, idbB], w=[kB])
                kdcol = (CST[0:P, C_KD32 + h:C_KD32 + h + 1] if smp else CST[0:P, C_KD + h:C_KD + h + 1])
                S.add("dve", lambda e, k=k, P=P, j=j, kdcol=kdcol: e.tensor_scalar_mul(kdec[0:P, j, :], PSb[0:P, k, 0:128], kdcol),
                      r=[kB, cstB], w=[kdecB[j]])
            for c in chs:
                P = c["P"]; off = c["off"]; j = c["j"]; smp = c["sample"]
                k2, k2B = ps_alloc()
                S.add("pe", lambda e, k2=k2, P=P, off=off: e.matmul(PS[0:P, k2, 0:P], lhsT=kTb[:, off:off + P], rhs=qTb[:, off:off + P],
                                                                    start=True, stop=True), r=[kTbB, qTbB], w=[k2B])
                dtab = (CST[0:P, C_DT32 + h * 32:C_DT32 + h * 32 + 32] if smp else CST[0:P, C_DT + h * 128:C_DT + (h + 1) * 128])
                S.add("dve", lambda e, k2=k2, P=P, j=j, dtab=dtab: e.tensor_tensor(Pm[0:P, j, 0:P], PS[0:P, k2, 0:P], dtab, ALU.mult),
                      r=[k2B, cstB], w=[PmB[j]])
            for c in chs:
                P = c["P"]; off = c["off"]; j = c["j"]; smp = c["sample"]
                k3, k3B = ps_alloc()
                S.add("pe", lambda e, k3=k3, P=P, j=j: e.matmul(PS[:, k3, 0:256], lhsT=kdec[0:P, j, :], rhs=vb[0:P, j, :],
                                                                start=True, stop=True), r=[kdecB[j], vbB[j]], w=[k3B])
                if smp:
                    S.add("act", lambda e, j=j: e.copy(Sbv[:, j, :], Ss32), r=[Ss32B], w=[SbvB[j]])
                    S.add("dve", lambda e, k3=k3: e.scalar_tensor_tensor(Ss32, Ss32, cd32[h], PS[:, k3, 0:256], ALU.mult, ALU.add),
                          r=[Ss32B, k3B], w=[Ss32B])
                    S.add("sp", lambda e: e.dma_start(out=sts[l, h], in_=Ss32), r=[Ss32B], is_dma=True, store=True)
                else:
                    S.add("act", lambda e, j=j: e.copy(Sbv[:, j, :], S32[:, h, :]), r=[S32B[h]], w=[SbvB[j]])
                    S.add("dve", lambda e, k3=k3: e.scalar_tensor_tensor(S32[:, h, :], S32[:, h, :], cd128[h], PS[:, k3, 0:256], ALU.mult, ALU.add),
                          r=[S32B[h], k3B], w=[S32B[h]])
                    if c["g"] == 15:
                        S.add("sp", lambda e: e.dma_start(out=stp[l, h], in_=S32[:, h, :]), r=[S32B[h]], is_dma=True, store=True)

        def rec2(h):
            hbuf = hb_[h % 2]
            qdTb, qdTbB = hbuf["qdTb"]
            vb, vbB = hbuf["vb"], hbuf["vbB"]
            okb = []
            for c in chs:
                P = c["P"]; off = c["off"]; j = c["j"]
                k4 = 5 + j // 2; k4B = psB[k4]; c4 = (j % 2) * 256

                def fn(e, k4=k4, P=P, off=off, j=j, c4=c4):
                    e.matmul(PS[0:P, k4, c4:c4 + 256], lhsT=Pm[0:P, j, 0:P], rhs=vb[0:P, j, :], start=True, stop=False)
                    return e.matmul(PS[0:P, k4, c4:c4 + 256], lhsT=qdTb[:, off:off + P], rhs=Sbv[:, j, :], start=False, stop=True)
                S.add("pe", fn, r=[PmB[j], vbB[j], qdTbB, SbvB[j]], w=[k4B])
                okb.append((k4, k4B, c4))
            for ci_, c in enumerate(chs):
                P = c["P"]; j = c["j"]
                k4, k4B, c4 = okb[ci_]
                S.add("dve", lambda e, k4=k4, P=P, j=j, c4=c4: e.bn_stats(ost[0:P, j, :], PS[0:P, k4, c4:c4 + 256]), r=[k4B], w=[ostB])
                S.add("dve", lambda e, P=P, j=j: e.bn_aggr(omv[0:P, j, :], ost[0:P, j, :]), r=[ostB], w=[omvB])
            y, yB = rstd_chain(omv[:, 0:nch, 1], omvB, EPS, nch, 128, otmp)
            S.add("dve", lambda e: e.scalar_tensor_tensor(nmr[:, 0:nch], omv[:, 0:nch, 0], -1.0, y[:, 0:nch], ALU.mult, ALU.mult),
                  r=[omvB, yB], w=[nmrB])
            rec["okb"] = okb; rec["y"] = (y, yB)

        def rec3(h):
            hbuf = hb_[h % 2]
            sgr, sgrB = hbuf["sgr"], hbuf["sgrB"]
            okb = rec["okb"]; y, yB = rec["y"]
            for ci, c in enumerate(chs):
                P = c["P"]; off = c["off"]; j = c["j"]
                k4, k4B, c4 = okb[ci]
                oi = ci % 2
                S.add("act", lambda e, k4=k4, P=P, j=j, oi=oi, c4=c4: e.activation(onb[0:P, oi, :], PS[0:P, k4, c4:c4 + 256], AF.Identity,
                                                                            bias=nmr[0:P, j:j + 1], scale=y[0:P, j:j + 1]),
                      r=[k4B, nmrB, yB], w=[onB[oi]])
                k5, k5B = ps_alloc()

                def fn(e, k5=k5, P=P, oi=oi):
                    e.transpose(PSb[:, k5, 0:P], onb[0:P, oi, 0:128], IDB[0:P, 0:P])
                    return e.transpose(PSb[:, k5, 128:128 + P], onb[0:P, oi, 128:256], IDB[0:P, 0:P])
                S.add("pe", fn, r=[onB[oi], idbB], w=[k5B])
                for half in range(2):
                    S.add("dve", lambda e, k5=k5, P=P, off=off, half=half: e.tensor_tensor(
                        goT[:, 2 * h + half, off:off + P], PSb[:, k5, half * 128:half * 128 + P], sgr[:, half, off:off + P], ALU.mult),
                        r=[k5B, sgrB[half]], w=[goB[2 * h + half]])

        import os
        state["pslim"] = 5
        ohalf = {(k_, hf_): Buf(f"ps{k_}h{hf_}", prior=psB[k_].pending()) for k_ in (5, 6, 7) for hf_ in (0, 1)}
        if os.environ.get("DBG_NOPIPE"):
            for h in range(H):
                proj_qk(h); proj_vg(h); rec1(h); rec2(h); rec3(h)
        else:
            proj_qk(0); proj_vg(0)
            for h in range(H):
                rec1(h)
                if h + 1 < H:
                    proj_qk(h + 1)
                rec2(h)
                if h + 1 < H:
                    proj_vg(h + 1)
                rec3(h)
        state["pslim"] = 8
        for k_ in (5, 6, 7):
            psB[k_].writers = pending_of([ohalf[(k_, 0)], ohalf[(k_, 1)]])
            psB[k_].readers = []

        if stop == 'r':
            return True
        arena.begin()
        sa = [[arena.alloc(f"sa{i}{p}", [128, maxn]) for p in range(npc)] for i in range(2)]
        sm = [[arena.alloc(f"sm{i}{p}", [128, maxn]) for p in range(npc)] for i in range(2)]
        m1, m1B = arena.alloc("m1", [128, 512])
        m2_, m2B = arena.alloc("m2", [128, 512])
        mg = {}
        for b in range(16):
            if b % 2 == 0:
                mg["AR"] = fetch_u(w_in, 12288 + b * 128)
                mg["AM"] = fetch_u(w_in, 14336 + b * 128)
                mg["RO"] = fetch_u(w_ro, b * 128)
                mg["MO"] = fetch_u(w_mo, b * 128)
            sAR2, sAM2, sRO2, sMO2 = mg["AR"], mg["AM"], mg["RO"], mg["MO"]
            bo = (b % 2) * 128
            for pi, (a, n) in enumerate(pieces):
                pcs = chunks_in(chs, a, n)
                hr = [hTB[c["j"]] for c in pcs]
                kAR, kARB = ps_alloc(); kAM, kAMB = ps_alloc(); kR, kRB = ps_alloc(); kM, kMB = ps_alloc()
                mm_group(lambda pi_, a_, n_, k=kAR: PS[:, k, 0:n_], [(a, n)], lambda kc, s=sAR2, bo=bo: ring[:, s, kc, bo:bo + 128],
                         lambda kc, a_, n_: hT[:, kc, a_:a_ + n_], hr + [ringB[sAR2]], [kAR])
                mm_group(lambda pi_, a_, n_, k=kAM: PS[:, k, 0:n_], [(a, n)], lambda kc, s=sAM2, bo=bo: ring[:, s, kc, bo:bo + 128],
                         lambda kc, a_, n_: hT[:, kc, a_:a_ + n_], hr + [ringB[sAM2]], [kAM])
                mm_group(lambda pi_, a_, n_, k=kR: PS[:, k, 0:n_], [(a, n)], lambda kc, s=sRO2, bo=bo: ring[:, s, kc, bo:bo + 128],
                         lambda kc, a_, n_: goT[:, kc, a_:a_ + n_], goB + [ringB[sRO2]], [kR])
                mm_group(lambda pi_, a_, n_, k=kM: PS[:, k, 0:n_], [(a, n)], lambda kc, s=sMO2, bo=bo: ring[:, s, kc, bo:bo + 128],
                         lambda kc, a_, n_: gmT[:, kc, a_:a_ + n_], gmB + [ringB[sMO2]], [kM])
                sa_, saB = sa[b % 2][pi]; sm_, smB = sm[b % 2][pi]
                S.add("act", lambda e, sa_=sa_, k=kAR, n=n: e.activation(sa_[:, 0:n], PS[:, k, 0:n], AF.Sigmoid), r=[kARB], w=[saB])
                S.add("act", lambda e, sm_=sm_, k=kAM, n=n: e.activation(sm_[:, 0:n], PS[:, k, 0:n], AF.Sigmoid), r=[kAMB], w=[smB])
                S.add("dve", lambda e, sa_=sa_, k=kR, n=n: e.tensor_tensor(m1[:, 0:n], sa_[:, 0:n], PS[:, k, 0:n], ALU.mult), r=[saB, kRB], w=[m1B])
                S.add("dve", lambda e, sm_=sm_, k=kM, n=n: e.tensor_tensor(m2_[:, 0:n], sm_[:, 0:n], PS[:, k, 0:n], ALU.mult), r=[smB, kMB], w=[m2B])
                S.add("dve", lambda e, a=a, n=n, b=b: e.tensor_tensor(mgT[:, b, a:a + n], m1[:, 0:n], m2_[:, 0:n], ALU.add),
                      r=[m1B, m2B], w=[mgB[b]])

        if stop == 'mg':
            return True
        io_view()
        nsteps = phase0_steps(*nxt) if nxt is not None else []
        groups = [(0, 2), (2, 2), (4, 2), (6, 2)]
        if state["ring"] % 2 == 1:
            state["ring"] = (state["ring"] + 1) % NS
        items = [(g0, gn, c) for (g0, gn) in groups for c in chs]
        slots = {}
        ns_i = 0

        def o_load(it):
            g0, gn, c = items[it]
            xi, xiB = io["xi"][it % 2]
            P = c["P"]; ncol = gn * 256
            src = x_src(l, c, (g0 * 256, g0 * 256 + ncol))
            S.add("sp", lambda e: e.dma_start(out=xi[0:P, 0:ncol], in_=src), r=[xscrB[c["g"]]], w=[xiB], is_dma=True)
        o_load(0)
        sched_at = {}
        if nsteps:
            sched_at.setdefault(0, []).append(nsteps[0])
            for ci, (fL, fC, fT) in enumerate(nsteps[1:]):
                sched_at.setdefault(4 * ci, []).append(fL)
                sched_at.setdefault(4 * ci + 2, []).append(fC)
                sched_at.setdefault(4 * ci + 5, []).append(fT)
        for it, (g0, gn, c) in enumerate(items):
            if g0 not in slots:
                slots[g0] = [fetch_u(w_o, u * 256) for u in range(g0, g0 + gn)]
            if it + 1 < len(items):
                o_load(it + 1)
            P = c["P"]; off = c["off"]
            xi, xiB = io["xi"][it % 2]
            xo, xoB = io["xo"][it % 2]
            ncol = gn * 256
            s0, s1 = slots[g0]
            assert s1 == s0 + 1
            k, kB = ps_alloc()

            def fn(e, k=k, s0=s0, P=P, off=off):
                last = None
                for kc in range(16):
                    last = e.matmul(PS[0:P, k, :].rearrange("p (a b) -> p a b", a=2), lhsT=mgT[:, kc, off:off + P],
                                    rhs=ring[:, s0:s0 + 2, kc, :], start=(kc == 0), stop=(kc == 15))
                return last
            S.add("pe", fn, r=mgB + [ringB[s0], ringB[s1]], w=[kB])
            S.add("dve", lambda e, k=k, P=P, xi=xi, xo=xo: e.tensor_tensor(xo[0:P, 0:512], PS[0:P, k, :], xi[0:P, 0:512], ALU.add),
                  r=[kB, xiB], w=[xoB])
            dst = xscr[c["row0"]:c["row0"] + P, g0 * 256:g0 * 256 + ncol]
            S.add("sp", lambda e, dst=dst, xo=xo, P=P, ncol=ncol: e.dma_start(out=dst, in_=xo[0:P, 0:ncol]), r=[xoB], w=[xscrB[c["g"]]], is_dma=True)
            for f_ in sched_at.pop(it, []):
                f_()
        for k_ in sorted(sched_at):
            for f_ in sched_at[k_]:
                f_()

        if l == depth - 1:
            S.add("sp", lambda e: e.dma_start(out=rows[:, 0, :], in_=final_g.ap().partition_broadcast(128)), w=[rowB[0]], is_dma=True)
            fsrc = lambda c: xscr[c["row0"]:c["row0"] + c["P"], :]
            pend = [norm_load(chs[0], fsrc(chs[0]))]
            for i_, c in enumerate(chs):
                if i_ + 1 < len(chs):
                    pend.append(norm_load(chs[i_ + 1], fsrc(chs[i_ + 1])))
                norm_square(c, pend[i_])
                norm_compute(c, pend[i_], rowB[0], "y")
        flush_wb(0)
        return False

    io_view()
    for st_ in phase0_steps(0, 0):
        if isinstance(st_, tuple):
            for f_ in st_:
                f_()
        else:
            st_()
    seq = [(l, t) for l in range(depth) for t in range(ntile)]
    for i, (l, t) in enumerate(seq):
        nxt = seq[i + 1] if i + 1 < len(seq) else None
        if tile_layer(l, t, nxt) or stop == 'o':
            break

    last_ops = list(S.stores)
    fin = Op("sp", lambda e: None)
    fin.deps = last_ops
    S.ops["sp"].append(fin)

    S.finalize()
    esems = {}
    for en in ENGS:
        esems[en] = es.enter_context(nc.semaphore("e_" + en))
    dsems = [es.enter_context(nc.semaphore(f"d{i}")) for i in range(S.n_dma_sems)]
    with es:
        with nc.Block() as block:
            @block.tensor
            def _(e):
                S.emit("pe", e, esems, dsems)

            @block.scalar
            def _(e):
                S.emit("act", e, esems, dsems)

            @block.vector
            def _(e):
                S.emit("dve", e, esems, dsems)

            @block.gpsimd
            def _(e):
                S.emit("pool", e, esems, dsems)

            @block.sync
            def _(e):
                S.emit("sp", e, esems, dsems)
    return nc, consts_np, cs_np


_CACHE = {}


def kernel(x_prompt, x_sample, state_ret, norm_g, w_in, ws, ws_b, ln_g, ln_b,
           w_ret_out, w_mlp_out, w_o, final_g):
    f = lambda a: np.ascontiguousarray(np.asarray(a, dtype=np.float32))
    if "nc" not in _CACHE:
        _CACHE["nc"] = build()
    nc, consts_np, cs_np = _CACHE["nc"]
    x_prompt, x_sample, state_ret = f(x_prompt), f(x_sample), f(state_ret)
    shared = dict(norm_g=f(norm_g), w_in=f(w_in), ws=f(ws), ws_b=f(ws_b).reshape(DEPTH, 8 * 128), ln_g=f(ln_g), ln_b=f(ln_b),
                  w_ro=f(w_ret_out), w_mo=f(w_mlp_out), w_o=f(w_o), final_g=f(final_g), cst=consts_np, cs=cs_np)
    in_maps = []
    for c in range(8):
        m = dict(shared)
        m["xp"] = x_prompt[c]
        m["xs"] = x_sample[c]
        m["st0"] = np.ascontiguousarray(state_ret[:, c])
        in_maps.append(m)
    res = run_bass_kernel_spmd(nc, in_maps, core_ids=list(range(8)))
    r = res.results
    y_prompt = np.stack([r[c]["yp"] for c in range(8)]).astype(np.float32)
    y_sample = np.stack([r[c]["ys"] for c in range(8)]).astype(np.float32)
    st_p = np.stack([r[c]["stp"] for c in range(8)], axis=1).astype(np.float32)
    st_s = np.stack([r[c]["sts"] for c in range(8)], axis=1).astype(np.float32)
    v_s = np.stack([r[c]["vs"] for c in range(8)], axis=1).astype(np.float32)
    return (y_prompt, y_sample, st_p, st_s, v_s)
```

```python
import numpy as np
import ml_dtypes
from contextlib import ExitStack
import concourse.bass as bass
import concourse.mybir as mybir
from concourse.bass_utils import run_bass_kernel_spmd

F32, BF16 = mybir.dt.float32, mybir.dt.bfloat16
AF = mybir.ActivationFunctionType
ALU = mybir.AluOpType

D = 2048
SEQ = 2048
DEPTH = 4
SS = 32
PAST = 1024
H = 8
DK = 128
DV = 256
NIN = 16384
EPS = 1e-6
NS = 6
TMAX = 544
NTILE = 4
USE_WCACHE = True
WC_MOD = 1
ENGS = ("pe", "act", "dve", "pool", "sp")

C_ID, C_PERM, C_DT, C_QD, C_KD, C_MASK, C_ONES, C_DT32, C_KD32 = 0, 128, 256, 1280, 2304, 2312, 2440, 2568, 2824
NCONST = 2832


class Buf:
    __slots__ = ("name", "writers", "readers")

    def __init__(self, name, prior=()):
        self.name = name
        self.writers = list(prior)
        self.readers = []

    def pending(self):
        return list(self.writers) + list(self.readers)


class Op:
    __slots__ = ("eng", "fn", "deps", "sig", "sig_idx", "is_dma", "dsem", "dval", "ndma", "prev_on_sem")

    def __init__(self, eng, fn, is_dma=False, ndma=1):
        self.eng = eng
        self.fn = fn
        self.deps = []
        self.sig = False
        self.sig_idx = None
        self.is_dma = is_dma
        self.ndma = ndma
        self.dsem = None
        self.dval = None
        self.prev_on_sem = None


class Sched:
    def __init__(self, n_dma_sems=64):
        self.ops = {e: [] for e in ENGS}
        self.n_dma_sems = n_dma_sems
        self.dma_rr = 0
        self.dma_sem_val = [0] * n_dma_sems
        self.dma_sem_last = [None] * n_dma_sems
        self.stores = []

    def add(self, eng, fn, r=(), w=(), is_dma=False, ndma=1, store=False):
        op = Op(eng, fn, is_dma, ndma)
        deps = []
        seen = set()
        for b in r:
            for d in b.writers:
                if id(d) not in seen:
                    seen.add(id(d)); deps.append(d)
        for b in w:
            for d in b.writers:
                if id(d) not in seen:
                    seen.add(id(d)); deps.append(d)
            for d in b.readers:
                if id(d) not in seen:
                    seen.add(id(d)); deps.append(d)
        op.deps = [d for d in deps if d is not op]
        for b in r:
            b.readers.append(op)
        for b in w:
            b.writers = [op]
            b.readers = []
        if is_dma:
            k = self.dma_rr
            self.dma_rr = (self.dma_rr + 1) % self.n_dma_sems
            op.dsem = k
            op.prev_on_sem = self.dma_sem_last[k]
            self.dma_sem_val[k] += 16 * ndma
            op.dval = self.dma_sem_val[k]
            self.dma_sem_last[k] = op
        self.ops[eng].append(op)
        if store:
            self.stores.append(op)
        return op

    def finalize(self):
        for e in ENGS:
            for op in self.ops[e]:
                for d in op.deps:
                    if d.is_dma:
                        continue
                    if d.eng == "pe" and op.eng == "pe" and not op.is_dma:
                        continue
                    d.sig = True
        for e in ENGS:
            n = 0
            for op in self.ops[e]:
                if op.sig and not op.is_dma:
                    n += 1
                    op.sig_idx = n

    def emit(self, eng_name, e, esems, dsems):
        seen = {}
        for op in self.ops[eng_name]:
            waits = []
            for d in op.deps:
                if d.is_dma:
                    waits.append((("d", d.dsem), d.dval))
                else:
                    if d.eng == "pe" and eng_name == "pe" and not op.is_dma:
                        continue
                    waits.append((("e", d.eng), d.sig_idx))
            if op.is_dma and op.prev_on_sem is not None:
                p = op.prev_on_sem
                waits.append((("d", p.dsem), p.dval))
            need = {}
            for k, v in waits:
                if v > need.get(k, 0):
                    need[k] = v
            for k, v in need.items():
                if seen.get(k, 0) >= v:
                    continue
                seen[k] = v
                sem = dsems[k[1]] if k[0] == "d" else esems[k[1]]
                e.wait_ge(sem, v)
            ins = op.fn(e)
            if op.is_dma:
                if not isinstance(ins, (list, tuple)):
                    ins = [ins]
                assert len(ins) == op.ndma
                for i in ins:
                    i.then_inc(dsems[op.dsem], 16)
            elif op.sig:
                if isinstance(ins, (list, tuple)):
                    ins = ins[-1]
                ins.then_inc(esems[eng_name], 1)


class Arena:
    def __init__(self, t32, nwords):
        self.t32 = t32
        self.tb = t32.bitcast(BF16)
        self.nwords = nwords
        self.bufs = []
        self.off = 0
        self.prior = []

    def begin(self):
        prior = []
        seen = set()
        for b in self.bufs:
            for o in b.pending():
                if id(o) not in seen:
                    seen.add(id(o)); prior.append(o)
        self.prior = prior
        self.bufs = []
        self.off = 0

    def alloc(self, name, shape, dtype=F32):
        n = int(np.prod(shape[1:]))
        words = n if dtype == F32 else (n + 1) // 2
        words = (words + 7) // 8 * 8
        assert self.off + words <= self.nwords, (name, self.off, words, self.nwords)
        if dtype == F32:
            ap = self.t32[:, self.off:self.off + n]
        else:
            ap = self.tb[:, 2 * self.off:2 * self.off + n]
        if len(shape) == 3:
            ap = ap.rearrange("p (a b) -> p a b", a=shape[1])
        elif len(shape) == 4:
            ap = ap.rearrange("p (a b c) -> p a b c", a=shape[1], b=shape[2])
        self.off += words
        b = Buf(name, self.prior)
        self.bufs.append(b)
        return ap, b


def mkbufs(names, prior=()):
    return [Buf(n, prior) for n in names]


def pending_of(bufs):
    out, seen = [], set()
    for b in bufs:
        for o in b.pending():
            if id(o) not in seen:
                seen.add(id(o)); out.append(o)
    return out


def _tables():
    f = np.float32
    lg = np.log((1.0 - 2.0 ** (-5.0 - np.arange(H, dtype=f))).astype(f)).astype(f)
    c = np.zeros((128, NCONST), f)
    c[:, C_ID:C_ID + 128] = np.eye(128, dtype=f)
    for m in range(128):
        c[(m + 64) % 128, C_PERM + m] = 1.0
    idx = np.arange(128, dtype=f)
    dist = np.abs(idx[:, None] - idx[None, :]).astype(f)
    blk = (np.arange(128)[None, :] // 64) >= (np.arange(128)[:, None] // 64)
    for h in range(H):
        dt = np.exp((lg[h] * dist).astype(f)).astype(f) * blk.astype(f)
        c[:, C_DT + h * 128:C_DT + (h + 1) * 128] = dt
        c[:, C_QD + h * 128:C_QD + (h + 1) * 128] = np.exp((lg[h] * (idx + 1.0)).astype(f)).astype(f)[None, :]
        c[:, C_KD + h] = np.exp((lg[h] * (127.0 - idx)).astype(f)).astype(f)
        i32 = np.arange(32, dtype=f)
        d32 = np.abs(i32[:, None] - i32[None, :]).astype(f)
        c[0:32, C_DT32 + h * 32:C_DT32 + (h + 1) * 32] = np.exp((lg[h] * d32).astype(f)).astype(f)
        c[0:32, C_KD32 + h] = np.exp((lg[h] * (31.0 - i32)).astype(f)).astype(f)
    c[:, C_MASK:C_MASK + 128] = blk.astype(f)
    c[:, C_ONES:C_ONES + 128] = 1.0
    cd128 = [float(np.exp(np.float32(lg[h] * np.float32(128.0)))) for h in range(H)]
    cd32 = [float(np.exp(np.float32(lg[h] * np.float32(32.0)))) for h in range(H)]
    half = DK // 2
    inv_freq = (1.0 / (np.float32(10000.0) ** (np.arange(half, dtype=f) / np.float32(half)))).astype(f)
    pos = np.concatenate([np.arange(SEQ), PAST + np.arange(SS)]).astype(f)
    ang = (pos[:, None] * inv_freq[None, :]).astype(f)
    cos = np.cos(ang).astype(f).T
    sin = np.sin(ang).astype(f).T
    cs = np.zeros((2, 128, SEQ + SS), f)
    cs[0, 0:64] = cos
    cs[0, 64:128] = cos
    cs[1, 0:64] = -sin
    cs[1, 64:128] = sin
    return c, cs, cd128, cd32


def build(depth=DEPTH, ntile=NTILE, stop=None):
    consts_np, cs_np, cd128, cd32 = _tables()
    nc = bass.Bass("TRN2", target_bir_lowering=False)
    dt_ = nc.dram_tensor
    xp = dt_("xp", [SEQ, D], F32, kind="ExternalInput")
    xs = dt_("xs", [SS, D], F32, kind="ExternalInput")
    st0 = dt_("st0", [depth, H, DK, DV], F32, kind="ExternalInput")
    norm_g = dt_("norm_g", [depth, D], F32, kind="ExternalInput")
    w_in = dt_("w_in", [depth, D, NIN], F32, kind="ExternalInput")
    ws = dt_("ws", [depth, 8, 128, 128], F32, kind="ExternalInput")
    ws_b = dt_("ws_b", [depth, 8 * 128], F32, kind="ExternalInput")
    ln_g = dt_("ln_g", [depth, D], F32, kind="ExternalInput")
    ln_b = dt_("ln_b", [depth, D], F32, kind="ExternalInput")
    w_ro = dt_("w_ro", [depth, D, D], F32, kind="ExternalInput")
    w_mo = dt_("w_mo", [depth, D, D], F32, kind="ExternalInput")
    w_o = dt_("w_o", [depth, D, D], F32, kind="ExternalInput")
    final_g = dt_("final_g", [D], F32, kind="ExternalInput")
    cst = dt_("cst", [128, NCONST], F32, kind="ExternalInput")
    cs = dt_("cs", [2, 128, SEQ + SS], F32, kind="ExternalInput")
    yp = dt_("yp", [SEQ, D], F32, kind="ExternalOutput")
    ys = dt_("ys", [SS, D], F32, kind="ExternalOutput")
    stp = dt_("stp", [depth, H, DK, DV], F32, kind="ExternalOutput")
    sts = dt_("sts", [depth, H, DK, DV], F32, kind="ExternalOutput")
    vs = dt_("vs", [depth, SS, D], F32, kind="ExternalOutput")
    xscr = dt_("xscr", [SEQ + SS, D], F32, kind="Internal")
    wcs = [dt_(f"wcache{q}", [88, 128, 4096], BF16, kind="Internal") for q in range(2)]

    es = ExitStack()
    sb = lambda name, shape, dt: es.enter_context(nc.sbuf_tensor(name, shape, dt))
    hT = sb("hT", [128, 16, TMAX], BF16)
    gmT = sb("gmT", [128, 16, TMAX], BF16)
    Cr = sb("Cr", [128, 5 * 2048], F32)
    ring = sb("ring", [128, NS, 16, 256], BF16)
    CST = sb("CST", [128, NCONST], F32)
    IDB = sb("IDB", [128, 128], BF16)
    rows = sb("rows", [128, 2, 2048], F32)
    S32 = sb("S32", [128, H, DV], F32)
    WsT = sb("WsT", [128, 8, 128], BF16)
    AR = sb("AR", [128, 10240], F32)
    PS = es.enter_context(nc.psum_tensor("PS", [128, 8, 512], F32))
    PSb = PS.bitcast(BF16)
    Cb = Cr.bitcast(BF16)

    S = Sched()
    arena = Arena(AR, 10240)

    hTB = mkbufs([f"hT{j}" for j in range(5)])
    gmB = mkbufs([f"gm{b}" for b in range(16)])
    ringB = mkbufs([f"ring{s}" for s in range(NS)])
    psB = mkbufs([f"ps{k}" for k in range(8)])
    rowB = mkbufs(["row0", "row1"])
    S32B = mkbufs([f"S32_{h}" for h in range(H)])
    cstB = Buf("cst")
    idbB = Buf("idb")
    wstB = Buf("WsT")
    xscrB = mkbufs([f"xscr{j}" for j in range(17)])
    state = {"ring": 0, "ps": 0, "Cbufs": []}

    def ps_alloc():
        lim = state.get("pslim", 8)
        k = state["ps"] % lim
        state["ps"] = (k + 1) % lim
        return k, psB[k]

    wcBs = [mkbufs([f"wc{q}_{i}" for i in range(96)]) for q in range(2)]
    convB = mkbufs(["conv0", "conv1"])
    state["fidx"] = 0
    state["wb"] = []
    state["cur"] = (0, 0)
    state["units"] = []
    state["conv"] = None

    def flush_wb(keep):
        while len(state["wb"]) > keep:
            (idx, s) = state["wb"].pop(0)
            S.add("pool", lambda e, idx=idx, s=s: e.dma_start(out=wcs[0][idx], in_=ring[:, s].rearrange("p k c -> p (k c)")),
                  r=[ringB[s]], w=[wcBs[0][idx]], is_dma=True)

    def maybe_convert():
        cv = state["conv"]
        if cv is None:
            return
        lc, nxt_i, cnt, every = cv
        cnt += 1
        if cnt >= every and nxt_i < len(state["units"]):
            cnt = 0
            tens, c0 = state["units"][nxt_i]
            q = lc % 2
            i_ = nxt_i
            S.add("pool", lambda e, tens=tens, c0=c0, q=q, i_=i_, lc=lc: e.dma_start(
                out=wcs[q][i_].rearrange("p (k c) -> p k c", k=16),
                in_=tens[lc, :, c0:c0 + 256].rearrange("(kc p) c -> p kc c", p=128)),
                w=[wcBs[q][i_], convB[i_ % 2]], is_dma=True)
            nxt_i += 1
        state["conv"] = (lc, nxt_i, cnt, every)

    def fetch_u(tens, c0):
        s = state["ring"]
        state["ring"] = (s + 1) % NS
        l_, t_ = state["cur"]
        idx = state["fidx"]; state["fidx"] += 1
        if l_ == 0 and t_ == 0:
            state["units"].append((tens, c0))
            flush_wb(1)
            S.add("pool", lambda e, s=s, tens=tens, c0=c0: e.dma_start(
                out=ring[:, s, :, :], in_=tens[0, :, c0:c0 + 256].rearrange("(kc p) c -> p kc c", p=128)),
                w=[ringB[s]], is_dma=True)
            if ntile > 1:
                state["wb"].append((idx, s))
        else:
            q = l_ % 2
            S.add("pool", lambda e, s=s, idx=idx, q=q: e.dma_start(out=ring[:, s].rearrange("p k c -> p (k c)"), in_=wcs[q][idx]),
                  r=[wcBs[q][idx]], w=[ringB[s]], is_dma=True)
            maybe_convert()
        return s

    S.add("sp", lambda e: e.dma_start(out=CST[:], in_=cst.ap()), w=[cstB], is_dma=True)
    S.add("dve", lambda e: e.tensor_copy(IDB[:], CST[:, C_ID:C_ID + 128]), r=[cstB], w=[idbB])

    def tile_chunks(t):
        ch = []
        for j in range(4):
            g = t * 4 + j
            ch.append(dict(j=j, off=j * 128, P=128, g=g, row0=g * 128, sample=False))
        if t == ntile - 1:
            ch.append(dict(j=4, off=512, P=32, g=16, row0=SEQ, sample=True))
        return ch

    def tile_pieces(t):
        if t == ntile - 1:
            return [(0, 256), (256, 288)]
        return [(0, 512)]

    def chunks_in(chs, a, n):
        return [c for c in chs if c["off"] >= a and c["off"] < a + n]

    def x_src(l, c, cols=None):
        if l == 0:
            base = xs.ap() if c["sample"] else xp[c["row0"]:c["row0"] + c["P"], :]
        else:
            base = xscr[c["row0"]:c["row0"] + c["P"], :]
        return base if cols is None else base[:, cols[0]:cols[1]]

    def rstd_chain(x_ap, xB, eps, k, P, tmp):
        xe, xeB = tmp["xe"]; s_, sB = tmp["s"]; y, yB = tmp["y"]; t_, tB = tmp["t"]
        sl = lambda a: a[0:P, 0:k]
        S.add("dve", lambda e: e.tensor_scalar_add(sl(xe), x_ap, eps), r=[xB], w=[xeB])
        S.add("act", lambda e: e.activation(sl(s_), sl(xe), AF.Sqrt), r=[xeB], w=[sB])
        S.add("dve", lambda e: e.reciprocal(sl(y), sl(s_)), r=[sB], w=[yB])
        for _ in range(1):
            S.add("dve", lambda e: e.tensor_tensor(sl(t_), sl(y), sl(y), ALU.mult), r=[yB], w=[tB])
            S.add("dve", lambda e: e.tensor_tensor(sl(t_), sl(t_), sl(xe), ALU.mult), r=[tB, xeB], w=[tB])
            S.add("dve", lambda e: e.tensor_scalar(sl(t_), sl(t_), -0.5, 1.5, ALU.mult, ALU.add), r=[tB], w=[tB])
            S.add("dve", lambda e: e.tensor_tensor(sl(y), sl(y), sl(t_), ALU.mult), r=[yB, tB], w=[yB])
        return y, yB

    def mm_group(out_fn, pieces, lhs_fn, rhs_fn, reads, pbanks):
        def fn(e):
            last = None
            for kc in range(16):
                for pi, (a, n) in enumerate(pieces):
                    last = e.matmul(out_fn(pi, a, n), lhsT=lhs_fn(kc), rhs=rhs_fn(kc, a, n),
                                    start=(kc == 0), stop=(kc == 15))
            return last
        S.add("pe", fn, r=reads, w=[psB[k] for k in pbanks])

    io = {}

    def io_view():
        arena.begin()
        io["xin"] = [arena.alloc(f"xin{i}", [128, 2048]) for i in range(2)]
        io["hb"] = arena.alloc("hb", [128, 2048], BF16)
        io["junk"] = arena.alloc("junk", [128, 2048], BF16)
        io["xi"] = [arena.alloc(f"xi{i}", [128, 768]) for i in range(2)]
        io["xo"] = [arena.alloc(f"xo{i}", [128, 768]) for i in range(2)]
        io["st"] = arena.alloc("st", [128, 4, 6])
        io["mv"] = arena.alloc("mv", [128, 2])
        io["msq"] = arena.alloc("msq", [128, 1])
        io["tmp"] = {n: arena.alloc("r" + n, [128, 1]) for n in ("xe", "s", "y", "t")}
        io["xin_i"] = 0

    def norm_load(c, src_ap):
        P = c["P"]
        i = io["xin_i"]; io["xin_i"] ^= 1
        xin, xinB = io["xin"][i]
        S.add("sp", lambda e: e.dma_start(out=xin[0:P, :], in_=src_ap), r=[xscrB[c["g"]]], w=[xinB], is_dma=True)
        return xin, xinB

    def norm_dve(c, src_ap, rowbuf, out_kind):
        xp_ = norm_load(c, src_ap)
        norm_square(c, xp_)
        norm_compute(c, xp_, rowbuf, out_kind)

    def norm_square(c, xpair):
        P = c["P"]
        xin, xinB = xpair
        mv, mvB = io["mv"]
        hbj, hbjB = io["junk"]
        S.add("act", lambda e: e.activation(hbj[0:P, :], xin[0:P, :], AF.Square, accum_out=mv[0:P, 0:1]), r=[xinB], w=[hbjB, mvB])

    def norm_compute(c, xpair, rowbuf, out_kind):
        P = c["P"]
        xin, xinB = xpair
        mv, mvB = io["mv"]; msq, msqB = io["msq"]
        S.add("dve", lambda e: e.tensor_scalar_mul(msq[0:P, :], mv[0:P, 0:1], 1.0 / D), r=[mvB], w=[msqB])
        y, yB = rstd_chain(msq[0:P, 0:1], msqB, EPS, 1, P, io["tmp"])
        if out_kind == "y":
            S.add("dve", lambda e: e.scalar_tensor_tensor(xin[0:P, :], xin[0:P, :], y[0:P, 0:1], rows[0:P, 0, :], ALU.mult, ALU.mult),
                  r=[xinB, yB, rowbuf], w=[xinB])
            dst = ys.ap() if c["sample"] else yp[c["row0"]:c["row0"] + P, :]
            S.add("sp", lambda e: e.dma_start(out=dst, in_=xin[0:P, :]), r=[xinB], is_dma=True, store=True)
            return
        hb, hbB = io["hb"]
        S.add("dve", lambda e: e.scalar_tensor_tensor(hb[0:P, :], xin[0:P, :], y[0:P, 0:1], rows[0:P, 0, :], ALU.mult, ALU.mult),
              r=[xinB, yB, rowbuf], w=[hbB])

    def norm_pe(c):
        P = c["P"]
        hb, hbB = io["hb"]
        for half in range(2):
            k, kB = ps_alloc()

            def fn(e, k=k, half=half):
                last = None
                for q in range(8):
                    kc = half * 8 + q
                    last = e.transpose(PSb[:, k, q * 128:q * 128 + P], hb[0:P, kc * 128:(kc + 1) * 128], IDB[0:P, 0:P])
                return last
            S.add("pe", fn, r=[hbB, idbB], w=[kB])
            src = PSb[:, k, :].rearrange("p (q c) -> p q c", q=8)[:, :, 0:P]
            dstap = hT[:, half * 8:(half + 1) * 8, c["off"]:c["off"] + P]
            S.add("act", lambda e, src=src, dstap=dstap: e.copy(dstap, src), r=[kB], w=[hTB[c["j"]]])

    pre0 = {}

    def phase0_steps(l, t):
        chs = tile_chunks(t)
        steps = []
        steps.append(lambda: S.add("sp", lambda e: e.dma_start(out=rows[:, 0, :], in_=norm_g[l].partition_broadcast(128)),
                                   w=[rowB[0]], is_dma=True))
        hold = {}

        def Ld(c):
            if c["j"] == 0 and (l, t) in pre0:
                hold[0] = pre0.pop((l, t))
            else:
                hold[c["j"]] = norm_load(c, x_src(l, c))
        for c in chs:
            steps.append((lambda c=c: Ld(c), lambda c=c: norm_square(c, hold[c["j"]]),
                          lambda c=c: norm_compute(c, hold[c["j"]], rowB[0], "hT"), lambda c=c: norm_pe(c)))
        return steps

    def tile_layer(l, t, nxt):
        state["cur"] = (l, t)
        state["fidx"] = 0
        if t == 0 and state["conv"] is not None:
            lc, nxt_i, cnt, every = state["conv"]
            while nxt_i < len(state["units"]):
                state["conv"] = (lc, nxt_i, every, every)
                maybe_convert()
                lc, nxt_i, cnt, every = state["conv"]
            state["conv"] = None
        if l + 1 < depth:
            if l == 0 and t == 1:
                state["conv"] = (1, 0, 0, 3)
            elif l >= 1 and t == 0:
                state["conv"] = (l + 1, 0, 0, 4)
        chs = tile_chunks(t)
        pieces = tile_pieces(t)
        sch = [c for c in chs if c["sample"]]
        nch = len(chs)
        npc = len(pieces)
        maxn = max(n for _, n in pieces)

        arena.begin()
        etmp0 = arena.alloc("etmp0", [128, 512])
        t1, t1B = arena.alloc("t1", [128, 2048])
        mst, mstB = arena.alloc("mst", [128, 4, 6])
        s12, s12B = arena.alloc("s12", [128, 2, 5, 4])
        S12, S12B = arena.alloc("S12", [128, 2, 5])
        lmean, lmeanB = arena.alloc("lmean", [128, 5])
        lvar, lvarB = arena.alloc("lvar", [128, 5])
        ltmp = {n: arena.alloc("l" + n, [128, 5]) for n in ("xe", "s", "y", "t")}
        mmv, mmvB = arena.alloc("mmv", [128, 2])
        mtmp = {n: arena.alloc("m" + n, [128, 1]) for n in ("xe", "s", "y", "t")}
        eU_all, _ = arena.alloc("eUall", [128, 2 * npc * maxn])
        eUBs = mkbufs([f"eU{i}" for i in range(2 * npc)], prior=arena.prior); arena.bufs += eUBs
        eU = [[(eU_all[:, (i * npc + p) * maxn:(i * npc + p + 1) * maxn], eUBs[i * npc + p]) for p in range(npc)] for i in range(2)]
        junk = eU_all.bitcast(BF16)[:, 0:2048]
        sG = [[arena.alloc(f"sG{i}{p}", [128, maxn]) for p in range(npc)] for i in range(2)]
        aT = [[arena.alloc(f"aT{i}{p}", [128, maxn]) for p in range(npc)] for i in range(2)]
        bT = [[arena.alloc(f"bT{i}{p}", [128, maxn]) for p in range(npc)] for i in range(2)]
        cT, cB = arena.alloc("cT", [128, max(maxn, 512)])
        etmp = [etmp0, (cT, cB)]
        wsb, wsbB = arena.alloc("wsb", [128, 1024])
        if t == 0:
            wsraw, wsrawB = arena.alloc("wsraw", [128, 8, 128])
        gvB = mkbufs([f"gv{j}" for j in range(5)], prior=pending_of(state["Cbufs"]))
        state["Cbufs"] = gvB
        gv = lambda j: Cr[:, j * 2048:(j + 1) * 2048]
        vnb = lambda j: Cb[:, j * 4096:j * 4096 + 2048]

        S.add("sp", lambda e: e.dma_start(out=rows[:, 1, :], in_=ln_b[l].partition_broadcast(128)), w=[rowB[1]], is_dma=True)
        S.add("sp", lambda e: e.dma_start(out=wsb[:, :], in_=ws_b[l].partition_broadcast(128)), w=[wsbB], is_dma=True)
        if t == 0:
            S.add("sp", lambda e: e.dma_start(out=wsraw, in_=ws[l].rearrange("g i j -> i g j")), w=[wsrawB], is_dma=True)
            for hf in range(2):
                k, kB = ps_alloc()

                def fn(e, k=k, hf=hf):
                    last = None
                    for q in range(4):
                        g = hf * 4 + q
                        last = e.transpose(PS[:, k, q * 128:(q + 1) * 128], wsraw[:, g, :], CST[:, C_ID:C_ID + 128])
                    return last
                S.add("pe", fn, r=[wsrawB, cstB], w=[kB])
                src = PS[:, k, :].rearrange("p (q c) -> p q c", q=4)
                msk = CST[:, C_MASK:C_MASK + 128].unsqueeze(1).to_broadcast([128, 4, 128])
                S.add("dve", lambda e, src=src, hf=hf, msk=msk: e.tensor_tensor(WsT[:, hf * 4:(hf + 1) * 4, :], src, msk, ALU.mult),
                      r=[kB, cstB], w=[wstB])

        ei = 0
        if state["ring"] % 2 == 1:
            state["ring"] = (state["ring"] + 1) % NS
        for up in range(4):
            s0 = fetch_u(w_in, 8192 + up * 512)
            s1 = fetch_u(w_in, 8192 + up * 512 + 256)
            assert s1 == s0 + 1
            for c in chs:
                P = c["P"]; off = c["off"]
                k, kB = ps_alloc()

                def fn(e, k=k, s0=s0, P=P, off=off):
                    last = None
                    for kc in range(16):
                        last = e.matmul(PS[0:P, k, :].rearrange("p (a b) -> p a b", a=2), lhsT=hT[:, kc, off:off + P],
                                        rhs=ring[:, s0:s0 + 2, kc, :], start=(kc == 0), stop=(kc == 15))
                    return last
                S.add("pe", fn, r=[hTB[c["j"]], ringB[s0], ringB[s1]], w=[kB])
                et, etB = etmp[ei]; ei ^= 1
                S.add("act", lambda e, et=et, k=k, P=P: e.activation(et[0:P, 0:512], PS[0:P, k, :], AF.Erf, scale=0.7071067811865476),
                      r=[kB], w=[etB])
                S.add("dve", lambda e, et=et, k=k, P=P, j=c["j"], up=up: e.scalar_tensor_tensor(
                    gv(j)[0:P, up * 512:(up + 1) * 512], et[0:P, 0:512], 1.0, PS[0:P, k, :], ALU.add, ALU.mult,
                    accum_out=s12[0:P, 0, j, up:up + 1]),
                    r=[etB, kB], w=[gvB[c["j"]], s12B])
                S.add("act", lambda e, P=P, j=c["j"], up=up: e.activation(junk[0:P, 0:512], gv(j)[0:P, up * 512:(up + 1) * 512], AF.Square,
                                                                         accum_out=s12[0:P, 1, j, up:up + 1]),
                      r=[gvB[c["j"]]], w=eUBs + [s12B])
        S.add("sp", lambda e: e.dma_start(out=rows[:, 0, :], in_=ln_g[l].partition_broadcast(128)), w=[rowB[0]], is_dma=True)

        s12v = s12.rearrange("p a c k -> p (a c) k")
        S12v = S12.rearrange("p a c -> p (a c)")
        S.add("dve", lambda e: e.tensor_tensor(S12v, s12v[:, :, 0], s12v[:, :, 1], ALU.add), r=[s12B], w=[S12B])
        S.add("dve", lambda e: e.tensor_tensor(S12v, S12v, s12v[:, :, 2], ALU.add), r=[s12B, S12B], w=[S12B])
        S.add("dve", lambda e: e.tensor_tensor(S12v, S12v, s12v[:, :, 3], ALU.add), r=[s12B, S12B], w=[S12B])
        S.add("dve", lambda e: e.tensor_scalar_mul(lmean[:, :], S12[:, 0, :], 1.0 / D), r=[S12B], w=[lmeanB])
        S.add("dve", lambda e: e.tensor_tensor(lvar[:, :], lmean[:, :], lmean[:, :], ALU.mult), r=[lmeanB], w=[lvarB])
        S.add("dve", lambda e: e.scalar_tensor_tensor(lvar[:, :], S12[:, 1, :], 1.0 / D, lvar[:, :], ALU.mult, ALU.subtract),
              r=[S12B, lvarB], w=[lvarB])
        lrstd, lrstdB = rstd_chain(lvar[:, 0:5], lvarB, 4.0 * EPS, 5, 128, ltmp)

        vnhB = [[Buf(f"vn{j}_{hf}") for hf in range(2)] for j in range(5)]
        state["Cbufs"] = gvB + [b_ for row_ in vnhB for b_ in row_]

        def ln_half(c, hf):
            P = c["P"]; j = c["j"]
            c0, c1 = hf * 1024, (hf + 1) * 1024
            y, yB = lrstd[:, j:j + 1], lrstdB
            S.add("dve", lambda e: e.scalar_tensor_tensor(t1[0:P, c0:c1], gv(j)[0:P, c0:c1], lmean[0:P, j:j + 1], rows[0:P, 0, c0:c1],
                                                          ALU.subtract, ALU.mult),
                  r=[gvB[j], lmeanB, rowB[0]], w=[t1B])
            if not c["sample"]:
                S.add("dve", lambda e: e.scalar_tensor_tensor(vnb(j)[0:P, c0:c1], t1[0:P, c0:c1], y[0:P, 0:1], rows[0:P, 1, c0:c1],
                                                              ALU.mult, ALU.add),
                      r=[t1B, yB, rowB[1]], w=[gvB[j], vnhB[j][hf]])
            else:
                S.add("dve", lambda e: e.scalar_tensor_tensor(gv(j)[0:P, c0:c1], t1[0:P, c0:c1], y[0:P, 0:1], rows[0:P, 1, c0:c1],
                                                              ALU.mult, ALU.add),
                      r=[t1B, yB, rowB[1]], w=[gvB[j]])
        for hf in range(2):
            for c in chs:
                ln_half(c, hf)
        for c in chs:
            if c["sample"]:
                P = c["P"]; j = c["j"]
                S.add("sp", lambda e, P=P, j=j: e.dma_start(out=vs[l], in_=gv(j)[0:P, :]), r=[gvB[j]], is_dma=True, store=True)
                t1b = t1.bitcast(BF16)
                S.add("act", lambda e, P=P, j=j, t1b=t1b: e.copy(t1b[0:P, 0:2048], gv(j)[0:P, :]), r=[gvB[j]], w=[t1B])

        if nxt is not None:
            c0n = tile_chunks(nxt[1])[0]
            srcn = x_src(nxt[0], c0n)
            S.add("sp", lambda e: e.dma_start(out=rows[:, 1, :], in_=srcn), r=[xscrB[c0n["g"]]], w=[rowB[1]], is_dma=True)
            pre0[nxt] = (rows[:, 1, :], rowB[1])

        def vn_lhs(c, b):
            if c["sample"]:
                return t1.bitcast(BF16)[0:c["P"], b * 128:(b + 1) * 128], t1B
            return vnb(c["j"])[0:c["P"], b * 128:(b + 1) * 128], vnhB[c["j"]][b // 8]

        m2 = {}

        def m2_A(b):
            if b % 2 == 0:
                m2["sU"] = fetch_u(w_in, 6144 + b * 128)
                m2["sG"] = fetch_u(w_in, 10240 + b * 128)
            sU2, sG2 = m2["sU"], m2["sG"]
            bo = (b % 2) * 128
            for pi, (a, n) in enumerate(pieces):
                pcs = chunks_in(chs, a, n)
                kU, kUB = ps_alloc(); kG, kGB = ps_alloc()
                hr = [hTB[c["j"]] for c in pcs]
                mm_group(lambda pi_, a_, n_, kU=kU: PS[:, kU, 0:n_], [(a, n)],
                         lambda kc, s=sU2, bo=bo: ring[:, s, kc, bo:bo + 128], lambda kc, a_, n_: hT[:, kc, a_:a_ + n_],
                         hr + [ringB[sU2]], [kU])
                mm_group(lambda pi_, a_, n_, kG=kG: PS[:, kG, 0:n_], [(a, n)],
                         lambda kc, s=sG2, bo=bo: ring[:, s, kc, bo:bo + 128], lambda kc, a_, n_: hT[:, kc, a_:a_ + n_],
                         hr + [ringB[sG2]], [kG])
                eu, euB = eU[b % 2][pi]; sg, sgB = sG[b % 2][pi]
                a_t, a_B = aT[b % 2][pi]; b_t, b_B = bT[b % 2][pi]
                S.add("act", lambda e, eu=eu, kU=kU, n=n: e.activation(eu[:, 0:n], PS[:, kU, 0:n], AF.Erf, scale=0.7071067811865476),
                      r=[kUB], w=[euB])
                S.add("act", lambda e, sg=sg, kG=kG, n=n: e.activation(sg[:, 0:n], PS[:, kG, 0:n], AF.Sigmoid), r=[kGB], w=[sgB])
                S.add("dve", lambda e, eu=eu, kU=kU, n=n, a_t=a_t: e.scalar_tensor_tensor(a_t[:, 0:n], eu[:, 0:n], 1.0, PS[:, kU, 0:n], ALU.add, ALU.mult),
                      r=[euB, kUB], w=[a_B])
                S.add("dve", lambda e, sg=sg, kG=kG, n=n, b_t=b_t: e.tensor_tensor(b_t[:, 0:n], sg[:, 0:n], PS[:, kG, 0:n], ALU.mult),
                      r=[sgB, kGB], w=[b_B])

        def m2_B(b):
            g = b // 2
            for pi, (a, n) in enumerate(pieces):
                pcs = chunks_in(chs, a, n)
                kS, kSB = ps_alloc()

                def fnS(e, kS=kS, pcs=pcs, a=a):
                    last = None
                    for c in pcs:
                        P = c["P"]; o = c["off"] - a
                        lhs, _ = vn_lhs(c, b)
                        last = e.matmul(PS[:, kS, o:o + P], lhsT=lhs, rhs=WsT[0:P, g, 0:P], start=True, stop=True)
                    return last
                S.add("pe", fnS, r=[vn_lhs(c, b)[1] for c in pcs] + [wstB], w=[kSB])
                a_t, a_B = aT[b % 2][pi]; b_t, b_B = bT[b % 2][pi]
                npc_ = [c for c in pcs if not c["sample"]]
                if npc_:
                    o0 = npc_[0]["off"] - a; m_ = len(npc_)
                    btab = wsb[:, g * 128:(g + 1) * 128].unsqueeze(1).to_broadcast([128, m_, 128])
                    S.add("dve", lambda e, kS=kS, o0=o0, m_=m_, btab=btab: e.tensor_tensor(
                        cT[:, o0:o0 + m_ * 128].rearrange("p (c i) -> p c i", c=m_),
                        PS[:, kS, o0:o0 + m_ * 128].rearrange("p (c i) -> p c i", c=m_), btab, ALU.add),
                        r=[kSB, wsbB], w=[cB])
                for c in pcs:
                    if c["sample"]:
                        o = c["off"] - a
                        S.add("dve", lambda e, kS=kS, o=o: e.tensor_tensor(cT[:, o:o + 32], PS[:, kS, o:o + 32], wsb[:, g * 128:g * 128 + 32], ALU.add),
                              r=[kSB, wsbB], w=[cB])
                S.add("dve", lambda e, n=n, a_t=a_t: e.tensor_tensor(cT[:, 0:n], cT[:, 0:n], a_t[:, 0:n], ALU.mult),
                      r=[a_B, cB], w=[cB])
                S.add("dve", lambda e, a=a, n=n, b_t=b_t: e.scalar_tensor_tensor(gmT[:, b, a:a + n], cT[:, 0:n], 0.5, b_t[:, 0:n], ALU.mult, ALU.mult),
                      r=[cB, b_B], w=[gmB[b]])
        m2_A(0)
        for b in range(16):
            if b + 1 < 16:
                m2_A(b + 1)
            m2_B(b)

        if stop == 'm':
            return True
        arena.begin()
        cosT, cosB = arena.alloc("cosT", [128, TMAX])
        sinT, sinB = arena.alloc("sinT", [128, TMAX])
        xsb, xsbB = arena.alloc("xsb", [128, TMAX])
        r1, r1B = arena.alloc("r1", [128, TMAX])
        r2, r2B = arena.alloc("r2", [128, TMAX])
        hb_ = []
        for i in range(2):
            d_ = {}
            d_["qTb"] = arena.alloc(f"qTb{i}", [128, TMAX], BF16)
            d_["kTb"] = arena.alloc(f"kTb{i}", [128, TMAX], BF16)
            d_["qdTb"] = arena.alloc(f"qdTb{i}", [128, TMAX], BF16)
            d_["vb"], _ = arena.alloc(f"vb{i}", [128, 5, 256], BF16)
            d_["vbB"] = mkbufs([f"vb{i}_{j}" for j in range(5)], prior=arena.prior); arena.bufs += d_["vbB"]
            d_["sgr"], _ = arena.alloc(f"sgr{i}", [128, 2, TMAX])
            d_["sgrB"] = mkbufs([f"sgr{i}_0", f"sgr{i}_1"], prior=arena.prior); arena.bufs += d_["sgrB"]
            d_["Ss32"] = arena.alloc(f"Ss32_{i}", [128, 256])
            hb_.append(d_)
        kdec, _ = arena.alloc("kdec", [128, 5, 128], BF16)
        kdecB = mkbufs([f"kdec{j}" for j in range(5)], prior=arena.prior); arena.bufs += kdecB
        Pm, _ = arena.alloc("Pm", [128, 5, 128], BF16)
        PmB = mkbufs([f"Pm{j}" for j in range(5)], prior=arena.prior); arena.bufs += PmB
        onb, _ = arena.alloc("onb", [128, 2, 256], BF16)
        onB = mkbufs(["on0", "on1"], prior=arena.prior); arena.bufs += onB
        Sbv, _ = arena.alloc("Sbv", [128, 5, 256], BF16)
        SbvB = mkbufs([f"Sbv{j}" for j in range(5)], prior=arena.prior); arena.bufs += SbvB
        ost, ostB = arena.alloc("ost", [128, 5, 6])
        omv, omvB = arena.alloc("omv", [128, 5, 2])
        nmr, nmrB = arena.alloc("nmr", [128, 5])
        otmp = {n: arena.alloc("o" + n, [128, 5]) for n in ("xe", "s", "y", "t")}
        goB = mkbufs([f"go{b}" for b in range(16)], prior=pending_of(state["Cbufs"]))
        mgB = mkbufs([f"mg{b}" for b in range(16)], prior=pending_of(state["Cbufs"]))
        state["Cbufs"] = goB + mgB
        goT = Cb[:, 0:16 * TMAX].rearrange("p (k t) -> p k t", k=16)
        mgT = Cb[:, 16 * TMAX:32 * TMAX].rearrange("p (k t) -> p k t", k=16)

        def fn_cs(e, which, dst):
            out = [e.dma_start(out=dst[:, 0:512], in_=cs[which, :, t * 512:(t + 1) * 512])]
            if sch:
                out.append(e.dma_start(out=dst[:, 512:544], in_=cs[which, :, SEQ:SEQ + SS]))
            return out
        S.add("sp", lambda e: fn_cs(e, 0, cosT), w=[cosB], is_dma=True, ndma=1 + len(sch))
        S.add("sp", lambda e: fn_cs(e, 1, sinT), w=[sinB], is_dma=True, ndma=1 + len(sch))
        if t == 0:
            for h in range(H):
                S.add("dve", lambda e, h=h: e.memset(S32[:, h, :], 0.0), w=[S32B[h]])
        rs = {}

        def proj_qk(h):
            hbuf = hb_[h % 2]
            if h % 2 == 0:
                rs["sQ"] = fetch_u(w_in, h * 128)
                rs["sK"] = fetch_u(w_in, 1024 + h * 128)
            ho = (h % 2) * 128
            if sch:
                Ss32, Ss32B = hbuf["Ss32"]
                S.add("sp", lambda e: e.dma_start(out=Ss32, in_=st0[l, h]), w=[Ss32B], is_dma=True)
            qTb, qTbB = hbuf["qTb"]; kTb, kTbB = hbuf["kTb"]; qdTb, qdTbB = hbuf["qdTb"]
            for which in range(2):
                for (a, n) in pieces:
                    pcs = chunks_in(chs, a, n)
                    hr = [hTB[c["j"]] for c in pcs]
                    k1, k1B = ps_alloc()
                    sqk = rs["sQ"] if which == 0 else rs["sK"]
                    mm_group(lambda pi, a_, n_, k1=k1: PS[:, k1, 0:n_], [(a, n)],
                             lambda kc, s=sqk, ho=ho: ring[:, s, kc, ho:ho + 128],
                             lambda kc, a_, n_: hT[:, kc, a_:a_ + n_], hr + [ringB[sqk]], [k1])
                    if which == 0:
                        S.add("act", lambda e, k1=k1, a=a, n=n: e.copy(xsb[:, a:a + n], PS[:, k1, 0:n]), r=[k1B], w=[xsbB])
                    else:
                        S.add("act", lambda e, k1=k1, a=a, n=n: e.mul(xsb[:, a:a + n], PS[:, k1, 0:n], float(DK) ** -0.5), r=[k1B], w=[xsbB])
                    k2, k2B = ps_alloc()
                    S.add("pe", lambda e, k2=k2, a=a, n=n: e.matmul(PS[:, k2, 0:n], lhsT=CST[:, C_PERM:C_PERM + 128], rhs=xsb[:, a:a + n],
                                                                    start=True, stop=True), r=[xsbB, cstB], w=[k2B])
                    S.add("dve", lambda e, a=a, n=n: e.tensor_tensor(r1[:, a:a + n], xsb[:, a:a + n], cosT[:, a:a + n], ALU.mult),
                          r=[xsbB, cosB], w=[r1B])
                    S.add("dve", lambda e, k2=k2, a=a, n=n: e.tensor_tensor(r2[:, a:a + n], PS[:, k2, 0:n], sinT[:, a:a + n], ALU.mult),
                          r=[k2B, sinB], w=[r2B])
                    if which == 0:
                        S.add("dve", lambda e, a=a, n=n: e.tensor_tensor(r1[:, a:a + n], r1[:, a:a + n], r2[:, a:a + n], ALU.add),
                              r=[r1B, r2B], w=[r1B])
                        S.add("act", lambda e, a=a, n=n: e.copy(qTb[:, a:a + n], r1[:, a:a + n]), r=[r1B], w=[qTbB])
                        npc_ = [c for c in pcs if not c["sample"]]
                        if npc_:
                            a0 = npc_[0]["off"]; m = len(npc_)
                            qdt = CST[:, C_QD + h * 128:C_QD + (h + 1) * 128].unsqueeze(1).to_broadcast([128, m, 128])
                            S.add("dve", lambda e, a0=a0, m=m, qdt=qdt: e.tensor_tensor(
                                qdTb[:, a0:a0 + m * 128].rearrange("p (c i) -> p c i", c=m),
                                r1[:, a0:a0 + m * 128].rearrange("p (c i) -> p c i", c=m), qdt, ALU.mult),
                                r=[r1B, cstB], w=[qdTbB])
                        for c in pcs:
                            if c["sample"]:
                                o = c["off"]
                                S.add("dve", lambda e, o=o: e.tensor_tensor(qdTb[:, o:o + 32], r1[:, o:o + 32],
                                                                           CST[:, C_QD + h * 128:C_QD + h * 128 + 32], ALU.mult),
                                      r=[r1B, cstB], w=[qdTbB])
                    else:
                        S.add("dve", lambda e, a=a, n=n: e.tensor_tensor(kTb[:, a:a + n], r1[:, a:a + n], r2[:, a:a + n], ALU.add),
                              r=[r1B, r2B], w=[kTbB])

        def proj_vg(h):
            hbuf = hb_[h % 2]
            vb, vbB = hbuf["vb"], hbuf["vbB"]; sgr, sgrB = hbuf["sgr"], hbuf["sgrB"]
            sV = fetch_u(w_in, 2048 + h * 256)
            sGR = fetch_u(w_in, 4096 + h * 256)
            for c in chs:
                P = c["P"]; off = c["off"]; j = c["j"]
                k, kB = ps_alloc()

                def fn(e, k=k, s=sV, P=P, off=off):
                    last = None
                    for kc in range(16):
                        last = e.matmul(PS[0:P, k, 0:256], lhsT=hT[:, kc, off:off + P], rhs=ring[:, s, kc, :],
                                        start=(kc == 0), stop=(kc == 15))
                    return last
                S.add("pe", fn, r=[hTB[j], ringB[sV]], w=[kB])
                S.add("act", lambda e, k=k, P=P, j=j: e.copy(vb[0:P, j, :], PS[0:P, k, 0:256]), r=[kB], w=[vbB[j]])
            for half in range(2):
                for (a, n) in pieces:
                    pcs = chunks_in(chs, a, n)
                    hr = [hTB[c["j"]] for c in pcs]
                    k1, k1B = ps_alloc()
                    mm_group(lambda pi, a_, n_, k1=k1: PS[:, k1, 0:n_], [(a, n)],
                             lambda kc, s=sGR, half=half: ring[:, s, kc, half * 128:(half + 1) * 128],
                             lambda kc, a_, n_: hT[:, kc, a_:a_ + n_], hr + [ringB[sGR]], [k1])
                    S.add("act", lambda e, k1=k1, a=a, n=n, half=half: e.activation(sgr[:, half, a:a + n], PS[:, k1, 0:n], AF.Sigmoid),
                          r=[k1B], w=[sgrB[half]])
                    S.add("dve", lambda e, k1=k1, a=a, n=n, half=half: e.tensor_tensor(sgr[:, half, a:a + n], sgr[:, half, a:a + n], PS[:, k1, 0:n], ALU.mult),
                          r=[sgrB[half], k1B], w=[sgrB[half]])

        rec = {}

        def rec1(h):
            hbuf = hb_[h % 2]
            qTb, qTbB = hbuf["qTb"]; kTb, kTbB = hbuf["kTb"]
            vb, vbB = hbuf["vb"], hbuf["vbB"]
            Ss32, Ss32B = hbuf["Ss32"]
            for c in chs:
                P = c["P"]; off = c["off"]; j = c["j"]; smp = c["sample"]
                k, kB = ps_alloc()
                S.add("pe", lambda e, k=k, P=P, off=off: e.transpose(PSb[0:P, k, 0:128], kTb[:, off:off + P], IDB[:, :]),
                      r=[kTbB, idbB], w=[kB])
                kdcol = (CST[0:P, C_KD32 + h:C_KD32 + h + 1] if smp else CST[0:P, C_KD + h:C_KD + h + 1])
                S.add("dve", lambda e, k=k, P=P, j=j, kdcol=kdcol: e.tensor_scalar_mul(kdec[0:P, j, :], PSb[0:P, k, 0:128], kdcol),
                      r=[kB, cstB], w=[kdecB[j]])
            for c in chs:
                P = c["P"]; off = c["off"]; j = c["j"]; smp = c["sample"]
                k2, k2B = ps_alloc()
                S.add("pe", lambda e, k2=k2, P=P, off=off: e.matmul(PS[0:P, k2, 0:P], lhsT=kTb[:, off:off + P], rhs=qTb[:, off:off + P],
                                                                    start=True, stop=True), r=[kTbB, qTbB], w=[k2B])
                dtab = (CST[0:P, C_DT32 + h * 32:C_DT32 + h * 32 + 32] if smp else CST[0:P, C_DT + h * 128:C_DT + (h + 1) * 128])
                S.add("dve", lambda e, k2=k2, P=P, j=j, dtab=dtab: e.tensor_tensor(Pm[0:P, j, 0:P], PS[0:P, k2, 0:P], dtab, ALU.mult),
                      r=[k2B, cstB], w=[PmB[j]])
            for c in chs:
                P = c["P"]; off = c["off"]; j = c["j"]; smp = c["sample"]
                k3, k3B = ps_alloc()
                S.add("pe", lambda e, k3=k3, P=P, j=j: e.matmul(PS[:, k3, 0:256], lhsT=kdec[0:P, j, :], rhs=vb[0:P, j, :],
                                                                start=True, stop=True), r=[kdecB[j], vbB[j]], w=[k3B])
                if smp:
                    S.add("act", lambda e, j=j: e.copy(Sbv[:, j, :], Ss32), r=[Ss32B], w=[SbvB[j]])
                    S.add("dve", lambda e, k3=k3: e.scalar_tensor_tensor(Ss32, Ss32, cd32[h], PS[:, k3, 0:256], ALU.mult, ALU.add),
                          r=[Ss32B, k3B], w=[Ss32B])
                    S.add("sp", lambda e: e.dma_start(out=sts[l, h], in_=Ss32), r=[Ss32B], is_dma=True, store=True)
                else:
                    S.add("act", lambda e, j=j: e.copy(Sbv[:, j, :], S32[:, h, :]), r=[S32B[h]], w=[SbvB[j]])
                    S.add("dve", lambda e, k3=k3: e.scalar_tensor_tensor(S32[:, h, :], S32[:, h, :], cd128[h], PS[:, k3, 0:256], ALU.mult, ALU.add),
                          r=[S32B[h], k3B], w=[S32B[h]])
                    if c["g"] == 15:
                        S.add("sp", lambda e: e.dma_start(out=stp[l, h], in_=S32[:, h, :]), r=[S32B[h]], is_dma=True, store=True)

        def rec2(h):
            hbuf = hb_[h % 2]
            qdTb, qdTbB = hbuf["qdTb"]
            vb, vbB = hbuf["vb"], hbuf["vbB"]
            okb = []
            for c in chs:
                P = c["P"]; off = c["off"]; j = c["j"]
                k4 = 5 + j // 2; k4B = psB[k4]; c4 = (j % 2) * 256

                def fn(e, k4=k4, P=P, off=off, j=j, c4=c4):
                    e.matmul(PS[0:P, k4, c4:c4 + 256], lhsT=Pm[0:P, j, 0:P], rhs=vb[0:P, j, :], start=True, stop=False)
                    return e.matmul(PS[0:P, k4, c4:c4 + 256], lhsT=qdTb[:, off:off + P], rhs=Sbv[:, j, :], start=False, stop=True)
                S.add("pe", fn, r=[PmB[j], vbB[j], qdTbB, SbvB[j]], w=[k4B])
                okb.append((k4, k4B, c4))
            for ci_, c in enumerate(chs):
                P = c["P"]; j = c["j"]
                k4, k4B, c4 = okb[ci_]
                S.add("dve", lambda e, k4=k4, P=P, j=j, c4=c4: e.bn_stats(ost[0:P, j, :], PS[0:P, k4, c4:c4 + 256]), r=[k4B], w=[ostB])
                S.add("dve", lambda e, P=P, j=j: e.bn_aggr(omv[0:P, j, :], ost[0:P, j, :]), r=[ostB], w=[omvB])
            y, yB = rstd_chain(omv[:, 0:nch, 1], omvB, EPS, nch, 128, otmp)
            S.add("dve", lambda e: e.scalar_tensor_tensor(nmr[:, 0:nch], omv[:, 0:nch, 0], -1.0, y[:, 0:nch], ALU.mult, ALU.mult),
                  r=[omvB, yB], w=[nmrB])
            rec["okb"] = okb; rec["y"] = (y, yB)

        def rec3(h):
            hbuf = hb_[h % 2]
            sgr, sgrB = hbuf["sgr"], hbuf["sgrB"]
            okb = rec["okb"]; y, yB = rec["y"]
            for ci, c in enumerate(chs):
                P = c["P"]; off = c["off"]; j = c["j"]
                k4, k4B, c4 = okb[ci]
                oi = ci % 2
                S.add("act", lambda e, k4=k4, P=P, j=j, oi=oi, c4=c4: e.activation(onb[0:P, oi, :], PS[0:P, k4, c4:c4 + 256], AF.Identity,
                                                                            bias=nmr[0:P, j:j + 1], scale=y[0:P, j:j + 1]),
                      r=[k4B, nmrB, yB], w=[onB[oi]])
                k5, k5B = ps_alloc()

                def fn(e, k5=k5, P=P, oi=oi):
                    e.transpose(PSb[:, k5, 0:P], onb[0:P, oi, 0:128], IDB[0:P, 0:P])
                    return e.transpose(PSb[:, k5, 128:128 + P], onb[0:P, oi, 128:256], IDB[0:P, 0:P])
                S.add("pe", fn, r=[onB[oi], idbB], w=[k5B])
                for half in range(2):
                    S.add("dve", lambda e, k5=k5, P=P, off=off, half=half: e.tensor_tensor(
                        goT[:, 2 * h + half, off:off + P], PSb[:, k5, half * 128:half * 128 + P], sgr[:, half, off:off + P], ALU.mult),
                        r=[k5B, sgrB[half]], w=[goB[2 * h + half]])

        import os
        state["pslim"] = 5
        ohalf = {(k_, hf_): Buf(f"ps{k_}h{hf_}", prior=psB[k_].pending()) for k_ in (5, 6, 7) for hf_ in (0, 1)}
        if os.environ.get("DBG_NOPIPE"):
            for h in range(H):
                proj_qk(h); proj_vg(h); rec1(h); rec2(h); rec3(h)
        else:
            proj_qk(0); proj_vg(0)
            for h in range(H):
                rec1(h)
                if h + 1 < H:
                    proj_qk(h + 1)
                rec2(h)
                if h + 1 < H:
                    proj_vg(h + 1)
                rec3(h)
        state["pslim"] = 8
        for k_ in (5, 6, 7):
            psB[k_].writers = pending_of([ohalf[(k_, 0)], ohalf[(k_, 1)]])
            psB[k_].readers = []

        if stop == 'r':
            return True
        arena.begin()
        sa = [[arena.alloc(f"sa{i}{p}", [128, maxn]) for p in range(npc)] for i in range(2)]
        sm = [[arena.alloc(f"sm{i}{p}", [128, maxn]) for p in range(npc)] for i in range(2)]
        m1, m1B = arena.alloc("m1", [128, 512])
        m2_, m2B = arena.alloc("m2", [128, 512])
        mg = {}
        for b in range(16):
            if b % 2 == 0:
                mg["AR"] = fetch_u(w_in, 12288 + b * 128)
                mg["AM"] = fetch_u(w_in, 14336 + b * 128)
                mg["RO"] = fetch_u(w_ro, b * 128)
                mg["MO"] = fetch_u(w_mo, b * 128)
            sAR2, sAM2, sRO2, sMO2 = mg["AR"], mg["AM"], mg["RO"], mg["MO"]
            bo = (b % 2) * 128
            for pi, (a, n) in enumerate(pieces):
                pcs = chunks_in(chs, a, n)
                hr = [hTB[c["j"]] for c in pcs]
                kAR, kARB = ps_alloc(); kAM, kAMB = ps_alloc(); kR, kRB = ps_alloc(); kM, kMB = ps_alloc()
                mm_group(lambda pi_, a_, n_, k=kAR: PS[:, k, 0:n_], [(a, n)], lambda kc, s=sAR2, bo=bo: ring[:, s, kc, bo:bo + 128],
                         lambda kc, a_, n_: hT[:, kc, a_:a_ + n_], hr + [ringB[sAR2]], [kAR])
                mm_group(lambda pi_, a_, n_, k=kAM: PS[:, k, 0:n_], [(a, n)], lambda kc, s=sAM2, bo=bo: ring[:, s, kc, bo:bo + 128],
                         lambda kc, a_, n_: hT[:, kc, a_:a_ + n_], hr + [ringB[sAM2]], [kAM])
                mm_group(lambda pi_, a_, n_, k=kR: PS[:, k, 0:n_], [(a, n)], lambda kc, s=sRO2, bo=bo: ring[:, s, kc, bo:bo + 128],
                         lambda kc, a_, n_: goT[:, kc, a_:a_ + n_], goB + [ringB[sRO2]], [kR])
                mm_group(lambda pi_, a_, n_, k=kM: PS[:, k, 0:n_], [(a, n)], lambda kc, s=sMO2, bo=bo: ring[:, s, kc, bo:bo + 128],
                         lambda kc, a_, n_: gmT[:, kc, a_:a_ + n_], gmB + [ringB[sMO2]], [kM])
                sa_, saB = sa[b % 2][pi]; sm_, smB = sm[b % 2][pi]
                S.add("act", lambda e, sa_=sa_, k=kAR, n=n: e.activation(sa_[:, 0:n], PS[:, k, 0:n], AF.Sigmoid), r=[kARB], w=[saB])
                S.add("act", lambda e, sm_=sm_, k=kAM, n=n: e.activation(sm_[:, 0:n], PS[:, k, 0:n], AF.Sigmoid), r=[kAMB], w=[smB])
                S.add("dve", lambda e, sa_=sa_, k=kR, n=n: e.tensor_tensor(m1[:, 0:n], sa_[:, 0:n], PS[:, k, 0:n], ALU.mult), r=[saB, kRB], w=[m1B])
                S.add("dve", lambda e, sm_=sm_, k=kM, n=n: e.tensor_tensor(m2_[:, 0:n], sm_[:, 0:n], PS[:, k, 0:n], ALU.mult), r=[smB, kMB], w=[m2B])
                S.add("dve", lambda e, a=a, n=n, b=b: e.tensor_tensor(mgT[:, b, a:a + n], m1[:, 0:n], m2_[:, 0:n], ALU.add),
                      r=[m1B, m2B], w=[mgB[b]])

        if stop == 'mg':
            return True
        io_view()
        nsteps = phase0_steps(*nxt) if nxt is not None else []
        groups = [(0, 2), (2, 2), (4, 2), (6, 2)]
        if state["ring"] % 2 == 1:
            state["ring"] = (state["ring"] + 1) % NS
        items = [(g0, gn, c) for (g0, gn) in groups for c in chs]
        slots = {}
        ns_i = 0

        def o_load(it):
            g0, gn, c = items[it]
            xi, xiB = io["xi"][it % 2]
            P = c["P"]; ncol = gn * 256
            src = x_src(l, c, (g0 * 256, g0 * 256 + ncol))
            S.add("sp", lambda e: e.dma_start(out=xi[0:P, 0:ncol], in_=src), r=[xscrB[c["g"]]], w=[xiB], is_dma=True)
        o_load(0)
        sched_at = {}
        sched_pre = {}
        if nsteps:
            nsteps[0]()
            for ci, (fLd, fSq, fC, fT) in enumerate(nsteps[1:]):
                if ci == 0:
                    fLd()
                else:
                    sched_pre.setdefault(4 * ci - 3, []).append(fLd)
                sched_at.setdefault(4 * ci, []).append(fSq)
                sched_at.setdefault(4 * ci + 2, []).append(fC)
                sched_at.setdefault(4 * ci + 5, []).append(fT)
        for it, (g0, gn, c) in enumerate(items):
            for f_ in sched_pre.pop(it, []):
                f_()
            if g0 not in slots:
                slots[g0] = [fetch_u(w_o, u * 256) for u in range(g0, g0 + gn)]
            if it + 1 < len(items):
                o_load(it + 1)
            P = c["P"]; off = c["off"]
            xi, xiB = io["xi"][it % 2]
            xo, xoB = io["xo"][it % 2]
            ncol = gn * 256
            s0, s1 = slots[g0]
            assert s1 == s0 + 1
            k, kB = ps_alloc()

            def fn(e, k=k, s0=s0, P=P, off=off):
                last = None
                for kc in range(16):
                    last = e.matmul(PS[0:P, k, :].rearrange("p (a b) -> p a b", a=2), lhsT=mgT[:, kc, off:off + P],
                                    rhs=ring[:, s0:s0 + 2, kc, :], start=(kc == 0), stop=(kc == 15))
                return last
            S.add("pe", fn, r=mgB + [ringB[s0], ringB[s1]], w=[kB])
            S.add("dve", lambda e, k=k, P=P, xi=xi, xo=xo: e.tensor_tensor(xo[0:P, 0:512], PS[0:P, k, :], xi[0:P, 0:512], ALU.add),
                  r=[kB, xiB], w=[xoB])
            dst = xscr[c["row0"]:c["row0"] + P, g0 * 256:g0 * 256 + ncol]
            S.add("sp", lambda e, dst=dst, xo=xo, P=P, ncol=ncol: e.dma_start(out=dst, in_=xo[0:P, 0:ncol]), r=[xoB], w=[xscrB[c["g"]]], is_dma=True)
            for f_ in sched_at.pop(it, []):
                f_()
        for k_ in sorted(set(sched_at) | set(sched_pre)):
            for f_ in sched_pre.get(k_, []) + sched_at.get(k_, []):
                f_()

        if l == depth - 1:
            S.add("sp", lambda e: e.dma_start(out=rows[:, 0, :], in_=final_g.ap().partition_broadcast(128)), w=[rowB[0]], is_dma=True)
            fsrc = lambda c: xscr[c["row0"]:c["row0"] + c["P"], :]
            pend = [norm_load(chs[0], fsrc(chs[0]))]
            for i_, c in enumerate(chs):
                if i_ + 1 < len(chs):
                    pend.append(norm_load(chs[i_ + 1], fsrc(chs[i_ + 1])))
                norm_square(c, pend[i_])
                norm_compute(c, pend[i_], rowB[0], "y")
        flush_wb(0)
        return False

    io_view()
    for st_ in phase0_steps(0, 0):
        if isinstance(st_, tuple):
            for f_ in st_:
                f_()
        else:
            st_()
    seq = [(l, t) for l in range(depth) for t in range(ntile)]
    for i, (l, t) in enumerate(seq):
        nxt = seq[i + 1] if i + 1 < len(seq) else None
        if tile_layer(l, t, nxt) or stop == 'o':
            break

    last_ops = list(S.stores)
    fin = Op("sp", lambda e: None)
    fin.deps = last_ops
    S.ops["sp"].append(fin)

    S.finalize()
    esems = {}
    for en in ENGS:
        esems[en] = es.enter_context(nc.semaphore("e_" + en))
    dsems = [es.enter_context(nc.semaphore(f"d{i}")) for i in range(S.n_dma_sems)]
    with es:
        with nc.Block() as block:
            @block.tensor
            def _(e):
                S.emit("pe", e, esems, dsems)

            @block.scalar
            def _(e):
                S.emit("act", e, esems, dsems)

            @block.vector
            def _(e):
                S.emit("dve", e, esems, dsems)

            @block.gpsimd
            def _(e):
                S.emit("pool", e, esems, dsems)

            @block.sync
            def _(e):
                S.emit("sp", e, esems, dsems)
    return nc, consts_np, cs_np


_CACHE = {}


def kernel(x_prompt, x_sample, state_ret, norm_g, w_in, ws, ws_b, ln_g, ln_b,
           w_ret_out, w_mlp_out, w_o, final_g):
    f = lambda a: np.ascontiguousarray(np.asarray(a, dtype=np.float32))
    if "nc" not in _CACHE:
        _CACHE["nc"] = build()
    nc, consts_np, cs_np = _CACHE["nc"]
    x_prompt, x_sample, state_ret = f(x_prompt), f(x_sample), f(state_ret)
    shared = dict(norm_g=f(norm_g), w_in=f(w_in), ws=f(ws), ws_b=f(ws_b).reshape(DEPTH, 8 * 128), ln_g=f(ln_g), ln_b=f(ln_b),
                  w_ro=f(w_ret_out), w_mo=f(w_mlp_out), w_o=f(w_o), final_g=f(final_g), cst=consts_np, cs=cs_np)
    in_maps = []
    for c in range(8):
        m = dict(shared)
        m["xp"] = x_prompt[c]
        m["xs"] = x_sample[c]
        m["st0"] = np.ascontiguousarray(state_ret[:, c])
        in_maps.append(m)
    res = run_bass_kernel_spmd(nc, in_maps, core_ids=list(range(8)))
    r = res.results
    y_prompt = np.stack([r[c]["yp"] for c in range(8)]).astype(np.float32)
    y_sample = np.stack([r[c]["ys"] for c in range(8)]).astype(np.float32)
    st_p = np.stack([r[c]["stp"] for c in range(8)], axis=1).astype(np.float32)
    st_s = np.stack([r[c]["sts"] for c in range(8)], axis=1).astype(np.float32)
    v_s = np.stack([r[c]["vs"] for c in range(8)], axis=1).astype(np.float32)
    return (y_prompt, y_sample, st_p, st_s, v_s)
```

```python
import numpy as np
import ml_dtypes
from contextlib import ExitStack
import concourse.bass as bass
import concourse.mybir as mybir
from concourse.bass_utils import run_bass_kernel_spmd

F32, BF16 = mybir.dt.float32, mybir.dt.bfloat16
AF = mybir.ActivationFunctionType
ALU = mybir.AluOpType

D = 2048
SEQ = 2048
DEPTH = 4
SS = 32
PAST = 1024
H = 8
DK = 128
DV = 256
NIN = 16384
EPS = 1e-6
NS = 6
TMAX = 544
NTILE = 4
USE_WCACHE = True
WC_MOD = 1
ENGS = ("pe", "act", "dve", "pool", "sp")

C_ID, C_PERM, C_DT, C_QD, C_KD, C_MASK, C_ONES, C_DT32, C_KD32 = 0, 128, 256, 1280, 2304, 2312, 2440, 2568, 2824
NCONST = 2832


class Buf:
    __slots__ = ("name", "writers", "readers")

    def __init__(self, name, prior=()):
        self.name = name
        self.writers = list(prior)
        self.readers = []

    def pending(self):
        return list(self.writers) + list(self.readers)


class Op:
    __slots__ = ("eng", "fn", "deps", "sig", "sig_idx", "is_dma", "dsem", "dval", "ndma", "prev_on_sem")

    def __init__(self, eng, fn, is_dma=False, ndma=1):
        self.eng = eng
        self.fn = fn
        self.deps = []
        self.sig = False
        self.sig_idx = None
        self.is_dma = is_dma
        self.ndma = ndma
        self.dsem = None
        self.dval = None
        self.prev_on_sem = None


class Sched:
    def __init__(self, n_dma_sems=64):
        self.ops = {e: [] for e in ENGS}
        self.n_dma_sems = n_dma_sems
        self.dma_rr = 0
        self.dma_sem_val = [0] * n_dma_sems
        self.dma_sem_last = [None] * n_dma_sems
        self.stores = []

    def add(self, eng, fn, r=(), w=(), is_dma=False, ndma=1, store=False):
        op = Op(eng, fn, is_dma, ndma)
        deps = []
        seen = set()
        for b in r:
            for d in b.writers:
                if id(d) not in seen:
                    seen.add(id(d)); deps.append(d)
        for b in w:
            for d in b.writers:
                if id(d) not in seen:
                    seen.add(id(d)); deps.append(d)
            for d in b.readers:
                if id(d) not in seen:
                    seen.add(id(d)); deps.append(d)
        op.deps = [d for d in deps if d is not op]
        for b in r:
            b.readers.append(op)
        for b in w:
            b.writers = [op]
            b.readers = []
        if is_dma:
            k = self.dma_rr
            self.dma_rr = (self.dma_rr + 1) % self.n_dma_sems
            op.dsem = k
            op.prev_on_sem = self.dma_sem_last[k]
            self.dma_sem_val[k] += 16 * ndma
            op.dval = self.dma_sem_val[k]
            self.dma_sem_last[k] = op
        self.ops[eng].append(op)
        if store:
            self.stores.append(op)
        return op

    def finalize(self):
        for e in ENGS:
            for op in self.ops[e]:
                for d in op.deps:
                    if d.is_dma:
                        continue
                    if d.eng == "pe" and op.eng == "pe" and not op.is_dma:
                        continue
                    d.sig = True
        for e in ENGS:
            n = 0
            for op in self.ops[e]:
                if op.sig and not op.is_dma:
                    n += 1
                    op.sig_idx = n

    def emit(self, eng_name, e, esems, dsems):
        seen = {}
        for op in self.ops[eng_name]:
            waits = []
            for d in op.deps:
                if d.is_dma:
                    waits.append((("d", d.dsem), d.dval))
                else:
                    if d.eng == "pe" and eng_name == "pe" and not op.is_dma:
                        continue
                    waits.append((("e", d.eng), d.sig_idx))
            if op.is_dma and op.prev_on_sem is not None:
                p = op.prev_on_sem
                waits.append((("d", p.dsem), p.dval))
            need = {}
            for k, v in waits:
                if v > need.get(k, 0):
                    need[k] = v
            for k, v in need.items():
                if seen.get(k, 0) >= v:
                    continue
                seen[k] = v
                sem = dsems[k[1]] if k[0] == "d" else esems[k[1]]
                e.wait_ge(sem, v)
            ins = op.fn(e)
            if op.is_dma:
                if not isinstance(ins, (list, tuple)):
                    ins = [ins]
                assert len(ins) == op.ndma
                for i in ins:
                    i.then_inc(dsems[op.dsem], 16)
            elif op.sig:
                if isinstance(ins, (list, tuple)):
                    ins = ins[-1]
                ins.then_inc(esems[eng_name], 1)


class Arena:
    def __init__(self, t32, nwords):
        self.t32 = t32
        self.tb = t32.bitcast(BF16)
        self.nwords = nwords
        self.bufs = []
        self.off = 0
        self.prior = []

    def begin(self):
        prior = []
        seen = set()
        for b in self.bufs:
            for o in b.pending():
                if id(o) not in seen:
                    seen.add(id(o)); prior.append(o)
        self.prior = prior
        self.bufs = []
        self.off = 0

    def alloc(self, name, shape, dtype=F32):
        n = int(np.prod(shape[1:]))
        words = n if dtype == F32 else (n + 1) // 2
        words = (words + 7) // 8 * 8
        assert self.off + words <= self.nwords, (name, self.off, words, self.nwords)
        if dtype == F32:
            ap = self.t32[:, self.off:self.off + n]
        else:
            ap = self.tb[:, 2 * self.off:2 * self.off + n]
        if len(shape) == 3:
            ap = ap.rearrange("p (a b) -> p a b", a=shape[1])
        elif len(shape) == 4:
            ap = ap.rearrange("p (a b c) -> p a b c", a=shape[1], b=shape[2])
        self.off += words
        b = Buf(name, self.prior)
        self.bufs.append(b)
        return ap, b


def mkbufs(names, prior=()):
    return [Buf(n, prior) for n in names]


def pending_of(bufs):
    out, seen = [], set()
    for b in bufs:
        for o in b.pending():
            if id(o) not in seen:
                seen.add(id(o)); out.append(o)
    return out


def _tables():
    f = np.float32
    lg = np.log((1.0 - 2.0 ** (-5.0 - np.arange(H, dtype=f))).astype(f)).astype(f)
    c = np.zeros((128, NCONST), f)
    c[:, C_ID:C_ID + 128] = np.eye(128, dtype=f)
    for m in range(128):
        c[(m + 64) % 128, C_PERM + m] = 1.0
    idx = np.arange(128, dtype=f)
    dist = np.abs(idx[:, None] - idx[None, :]).astype(f)
    blk = (np.arange(128)[None, :] // 64) >= (np.arange(128)[:, None] // 64)
    for h in range(H):
        dt = np.exp((lg[h] * dist).astype(f)).astype(f) * blk.astype(f)
        c[:, C_DT + h * 128:C_DT + (h + 1) * 128] = dt
        c[:, C_QD + h * 128:C_QD + (h + 1) * 128] = np.exp((lg[h] * (idx + 1.0)).astype(f)).astype(f)[None, :]
        c[:, C_KD + h] = np.exp((lg[h] * (127.0 - idx)).astype(f)).astype(f)
        i32 = np.arange(32, dtype=f)
        d32 = np.abs(i32[:, None] - i32[None, :]).astype(f)
        c[0:32, C_DT32 + h * 32:C_DT32 + (h + 1) * 32] = np.exp((lg[h] * d32).astype(f)).astype(f)
        c[0:32, C_KD32 + h] = np.exp((lg[h] * (31.0 - i32)).astype(f)).astype(f)
    c[:, C_MASK:C_MASK + 128] = blk.astype(f)
    c[:, C_ONES:C_ONES + 128] = 1.0
    cd128 = [float(np.exp(np.float32(lg[h] * np.float32(128.0)))) for h in range(H)]
    cd32 = [float(np.exp(np.float32(lg[h] * np.float32(32.0)))) for h in range(H)]
    half = DK // 2
    inv_freq = (1.0 / (np.float32(10000.0) ** (np.arange(half, dtype=f) / np.float32(half)))).astype(f)
    pos = np.concatenate([np.arange(SEQ), PAST + np.arange(SS)]).astype(f)
    ang = (pos[:, None] * inv_freq[None, :]).astype(f)
    cos = np.cos(ang).astype(f).T
    sin = np.sin(ang).astype(f).T
    cs = np.zeros((2, 128, SEQ + SS), f)
    cs[0, 0:64] = cos
    cs[0, 64:128] = cos
    cs[1, 0:64] = -sin
    cs[1, 64:128] = sin
    return c, cs, cd128, cd32


def build(depth=DEPTH, ntile=NTILE, stop=None):
    consts_np, cs_np, cd128, cd32 = _tables()
    nc = bass.Bass("TRN2", target_bir_lowering=False)
    dt_ = nc.dram_tensor
    xp = dt_("xp", [SEQ, D], F32, kind="ExternalInput")
    xs = dt_("xs", [SS, D], F32, kind="ExternalInput")
    st0 = dt_("st0", [depth, H, DK, DV], F32, kind="ExternalInput")
    norm_g = dt_("norm_g", [depth, D], F32, kind="ExternalInput")
    w_in = dt_("w_in", [depth, D, NIN], F32, kind="ExternalInput")
    ws = dt_("ws", [depth, 8, 128, 128], F32, kind="ExternalInput")
    ws_b = dt_("ws_b", [depth, 8 * 128], F32, kind="ExternalInput")
    ln_g = dt_("ln_g", [depth, D], F32, kind="ExternalInput")
    ln_b = dt_("ln_b", [depth, D], F32, kind="ExternalInput")
    w_ro = dt_("w_ro", [depth, D, D], F32, kind="ExternalInput")
    w_mo = dt_("w_mo", [depth, D, D], F32, kind="ExternalInput")
    w_o = dt_("w_o", [depth, D, D], F32, kind="ExternalInput")
    final_g = dt_("final_g", [D], F32, kind="ExternalInput")
    cst = dt_("cst", [128, NCONST], F32, kind="ExternalInput")
    cs = dt_("cs", [2, 128, SEQ + SS], F32, kind="ExternalInput")
    yp = dt_("yp", [SEQ, D], F32, kind="ExternalOutput")
    ys = dt_("ys", [SS, D], F32, kind="ExternalOutput")
    stp = dt_("stp", [depth, H, DK, DV], F32, kind="ExternalOutput")
    sts = dt_("sts", [depth, H, DK, DV], F32, kind="ExternalOutput")
    vs = dt_("vs", [depth, SS, D], F32, kind="ExternalOutput")
    xscr = dt_("xscr", [SEQ + SS, D], F32, kind="Internal")
    wcs = [dt_(f"wcache{q}", [88, 128, 4096], BF16, kind="Internal") for q in range(2)]

    es = ExitStack()
    sb = lambda name, shape, dt: es.enter_context(nc.sbuf_tensor(name, shape, dt))
    hT = sb("hT", [128, 16, TMAX], BF16)
    gmT = sb("gmT", [128, 16, TMAX], BF16)
    Cr = sb("Cr", [128, 5 * 2048], F32)
    ring = sb("ring", [128, NS, 16, 256], BF16)
    CST = sb("CST", [128, NCONST], F32)
    IDB = sb("IDB", [128, 128], BF16)
    rows = sb("rows", [128, 2, 2048], F32)
    S32 = sb("S32", [128, H, DV], F32)
    WsT = sb("WsT", [128, 8, 128], BF16)
    AR = sb("AR", [128, 10240], F32)
    PS = es.enter_context(nc.psum_tensor("PS", [128, 8, 512], F32))
    PSb = PS.bitcast(BF16)
    Cb = Cr.bitcast(BF16)

    S = Sched()
    arena = Arena(AR, 10240)

    hTB = mkbufs([f"hT{j}" for j in range(5)])
    gmB = mkbufs([f"gm{b}" for b in range(16)])
    ringB = mkbufs([f"ring{s}" for s in range(NS)])
    psB = mkbufs([f"ps{k}" for k in range(8)])
    rowB = mkbufs(["row0", "row1"])
    S32B = mkbufs([f"S32_{h}" for h in range(H)])
    cstB = Buf("cst")
    idbB = Buf("idb")
    wstB = Buf("WsT")
    xscrB = mkbufs([f"xscr{j}" for j in range(17)])
    state = {"ring": 0, "ps": 0, "Cbufs": []}

    def ps_alloc():
        lim = state.get("pslim", 8)
        k = state["ps"] % lim
        state["ps"] = (k + 1) % lim
        return k, psB[k]

    wcBs = [mkbufs([f"wc{q}_{i}" for i in range(96)]) for q in range(2)]
    convB = mkbufs(["conv0", "conv1"])
    state["fidx"] = 0
    state["wb"] = []
    state["cur"] = (0, 0)
    state["units"] = []
    state["conv"] = None

    def flush_wb(keep):
        while len(state["wb"]) > keep:
            (idx, s) = state["wb"].pop(0)
            S.add("pool", lambda e, idx=idx, s=s: e.dma_start(out=wcs[0][idx], in_=ring[:, s].rearrange("p k c -> p (k c)")),
                  r=[ringB[s]], w=[wcBs[0][idx]], is_dma=True)

    def maybe_convert():
        cv = state["conv"]
        if cv is None:
            return
        lc, nxt_i, cnt, every = cv
        cnt += 1
        if cnt >= every and nxt_i < len(state["units"]):
            cnt = 0
            tens, c0 = state["units"][nxt_i]
            q = lc % 2
            i_ = nxt_i
            S.add("pool", lambda e, tens=tens, c0=c0, q=q, i_=i_, lc=lc: e.dma_start(
                out=wcs[q][i_].rearrange("p (k c) -> p k c", k=16),
                in_=tens[lc, :, c0:c0 + 256].rearrange("(kc p) c -> p kc c", p=128)),
                w=[wcBs[q][i_], convB[i_ % 2]], is_dma=True)
            nxt_i += 1
        state["conv"] = (lc, nxt_i, cnt, every)

    def fetch_u(tens, c0):
        s = state["ring"]
        state["ring"] = (s + 1) % NS
        l_, t_ = state["cur"]
        idx = state["fidx"]; state["fidx"] += 1
        if l_ == 0 and t_ == 0:
            state["units"].append((tens, c0))
            flush_wb(1)
            S.add("pool", lambda e, s=s, tens=tens, c0=c0: e.dma_start(
                out=ring[:, s, :, :], in_=tens[0, :, c0:c0 + 256].rearrange("(kc p) c -> p kc c", p=128)),
                w=[ringB[s]], is_dma=True)
            if ntile > 1:
                state["wb"].append((idx, s))
        else:
            q = l_ % 2
            S.add("pool", lambda e, s=s, idx=idx, q=q: e.dma_start(out=ring[:, s].rearrange("p k c -> p (k c)"), in_=wcs[q][idx]),
                  r=[wcBs[q][idx]], w=[ringB[s]], is_dma=True)
            maybe_convert()
        return s

    S.add("sp", lambda e: e.dma_start(out=CST[:], in_=cst.ap()), w=[cstB], is_dma=True)
    S.add("dve", lambda e: e.tensor_copy(IDB[:], CST[:, C_ID:C_ID + 128]), r=[cstB], w=[idbB])

    def tile_chunks(t):
        ch = []
        for j in range(4):
            g = t * 4 + j
            ch.append(dict(j=j, off=j * 128, P=128, g=g, row0=g * 128, sample=False))
        if t == ntile - 1:
            ch.append(dict(j=4, off=512, P=32, g=16, row0=SEQ, sample=True))
        return ch

    def tile_pieces(t):
        if t == ntile - 1:
            return [(0, 256), (256, 288)]
        return [(0, 512)]

    def chunks_in(chs, a, n):
        return [c for c in chs if c["off"] >= a and c["off"] < a + n]

    def x_src(l, c, cols=None):
        if l == 0:
            base = xs.ap() if c["sample"] else xp[c["row0"]:c["row0"] + c["P"], :]
        else:
            base = xscr[c["row0"]:c["row0"] + c["P"], :]
        return base if cols is None else base[:, cols[0]:cols[1]]

    def rstd_chain(x_ap, xB, eps, k, P, tmp):
        xe, xeB = tmp["xe"]; s_, sB = tmp["s"]; y, yB = tmp["y"]; t_, tB = tmp["t"]
        sl = lambda a: a[0:P, 0:k]
        S.add("dve", lambda e: e.tensor_scalar_add(sl(xe), x_ap, eps), r=[xB], w=[xeB])
        S.add("act", lambda e: e.activation(sl(s_), sl(xe), AF.Sqrt), r=[xeB], w=[sB])
        S.add("dve", lambda e: e.reciprocal(sl(y), sl(s_)), r=[sB], w=[yB])
        for _ in range(1):
            S.add("dve", lambda e: e.tensor_tensor(sl(t_), sl(y), sl(y), ALU.mult), r=[yB], w=[tB])
            S.add("dve", lambda e: e.tensor_tensor(sl(t_), sl(t_), sl(xe), ALU.mult), r=[tB, xeB], w=[tB])
            S.add("dve", lambda e: e.tensor_scalar(sl(t_), sl(t_), -0.5, 1.5, ALU.mult, ALU.add), r=[tB], w=[tB])
            S.add("dve", lambda e: e.tensor_tensor(sl(y), sl(y), sl(t_), ALU.mult), r=[yB, tB], w=[yB])
        return y, yB

    def mm_group(out_fn, pieces, lhs_fn, rhs_fn, reads, pbanks):
        def fn(e):
            last = None
            for kc in range(16):
                for pi, (a, n) in enumerate(pieces):
                    last = e.matmul(out_fn(pi, a, n), lhsT=lhs_fn(kc), rhs=rhs_fn(kc, a, n),
                                    start=(kc == 0), stop=(kc == 15))
            return last
        S.add("pe", fn, r=reads, w=[psB[k] for k in pbanks])

    io = {}

    def io_view():
        arena.begin()
        io["xin"] = [arena.alloc(f"xin{i}", [128, 2048]) for i in range(2)]
        io["hb"] = arena.alloc("hb", [128, 2048], BF16)
        io["junk"] = arena.alloc("junk", [128, 2048], BF16)
        io["xi"] = [arena.alloc(f"xi{i}", [128, 768]) for i in range(2)]
        io["xo"] = [arena.alloc(f"xo{i}", [128, 768]) for i in range(2)]
        io["st"] = arena.alloc("st", [128, 4, 6])
        io["mv"] = arena.alloc("mv", [128, 2])
        io["msq"] = arena.alloc("msq", [128, 1])
        io["tmp"] = {n: arena.alloc("r" + n, [128, 1]) for n in ("xe", "s", "y", "t")}
        io["xin_i"] = 0

    def norm_load(c, src_ap):
        P = c["P"]
        i = io["xin_i"]; io["xin_i"] ^= 1
        xin, xinB = io["xin"][i]
        S.add("sp", lambda e: e.dma_start(out=xin[0:P, :], in_=src_ap), r=[xscrB[c["g"]]], w=[xinB], is_dma=True)
        return xin, xinB

    def norm_dve(c, src_ap, rowbuf, out_kind):
        xp_ = norm_load(c, src_ap)
        norm_square(c, xp_)
        norm_compute(c, xp_, rowbuf, out_kind)

    def norm_square(c, xpair):
        P = c["P"]
        xin, xinB = xpair
        mv, mvB = io["mv"]
        hbj, hbjB = io["junk"]
        S.add("act", lambda e: e.activation(hbj[0:P, :], xin[0:P, :], AF.Square, accum_out=mv[0:P, 0:1]), r=[xinB], w=[hbjB, mvB])

    def norm_compute(c, xpair, rowbuf, out_kind):
        P = c["P"]
        xin, xinB = xpair
        mv, mvB = io["mv"]; msq, msqB = io["msq"]
        S.add("dve", lambda e: e.tensor_scalar_mul(msq[0:P, :], mv[0:P, 0:1], 1.0 / D), r=[mvB], w=[msqB])
        y, yB = rstd_chain(msq[0:P, 0:1], msqB, EPS, 1, P, io["tmp"])
        if out_kind == "y":
            S.add("dve", lambda e: e.scalar_tensor_tensor(xin[0:P, :], xin[0:P, :], y[0:P, 0:1], rows[0:P, 0, :], ALU.mult, ALU.mult),
                  r=[xinB, yB, rowbuf], w=[xinB])
            dst = ys.ap() if c["sample"] else yp[c["row0"]:c["row0"] + P, :]
            S.add("sp", lambda e: e.dma_start(out=dst, in_=xin[0:P, :]), r=[xinB], is_dma=True, store=True)
            return
        hb, hbB = io["hb"]
        S.add("dve", lambda e: e.scalar_tensor_tensor(hb[0:P, :], xin[0:P, :], y[0:P, 0:1], rows[0:P, 0, :], ALU.mult, ALU.mult),
              r=[xinB, yB, rowbuf], w=[hbB])

    def norm_pe(c):
        P = c["P"]
        hb, hbB = io["hb"]
        for half in range(2):
            k, kB = ps_alloc()

            def fn(e, k=k, half=half):
                last = None
                for q in range(8):
                    kc = half * 8 + q
                    last = e.transpose(PSb[:, k, q * 128:q * 128 + P], hb[0:P, kc * 128:(kc + 1) * 128], IDB[0:P, 0:P])
                return last
            S.add("pe", fn, r=[hbB, idbB], w=[kB])
            src = PSb[:, k, :].rearrange("p (q c) -> p q c", q=8)[:, :, 0:P]
            dstap = hT[:, half * 8:(half + 1) * 8, c["off"]:c["off"] + P]
            S.add("act", lambda e, src=src, dstap=dstap: e.copy(dstap, src), r=[kB], w=[hTB[c["j"]]])

    pre0 = {}

    def phase0_steps(l, t):
        chs = tile_chunks(t)
        steps = []
        steps.append(lambda: S.add("sp", lambda e: e.dma_start(out=rows[:, 0, :], in_=norm_g[l].partition_broadcast(128)),
                                   w=[rowB[0]], is_dma=True))
        hold = {}

        def Ld(c):
            if c["j"] == 0 and (l, t) in pre0:
                hold[0] = pre0.pop((l, t))
            else:
                hold[c["j"]] = norm_load(c, x_src(l, c))
        for c in chs:
            steps.append((lambda c=c: Ld(c), lambda c=c: norm_square(c, hold[c["j"]]),
                          lambda c=c: norm_compute(c, hold[c["j"]], rowB[0], "hT"), lambda c=c: norm_pe(c)))
        return steps

    def tile_layer(l, t, nxt):
        state["cur"] = (l, t)
        state["fidx"] = 0
        if t == 0 and state["conv"] is not None:
            lc, nxt_i, cnt, every = state["conv"]
            while nxt_i < len(state["units"]):
                state["conv"] = (lc, nxt_i, every, every)
                maybe_convert()
                lc, nxt_i, cnt, every = state["conv"]
            state["conv"] = None
        if l + 1 < depth:
            if l == 0 and t == 1:
                state["conv"] = (1, 0, 0, 3)
            elif l >= 1 and t == 0:
                state["conv"] = (l + 1, 0, 0, 4)
        chs = tile_chunks(t)
        pieces = tile_pieces(t)
        sch = [c for c in chs if c["sample"]]
        nch = len(chs)
        npc = len(pieces)
        maxn = max(n for _, n in pieces)

        arena.begin()
        etmp0 = arena.alloc("etmp0", [128, 512])
        t1, t1B = arena.alloc("t1", [128, 2048])
        mst, mstB = arena.alloc("mst", [128, 4, 6])
        s12, s12B = arena.alloc("s12", [128, 2, 5, 4])
        S12, S12B = arena.alloc("S12", [128, 2, 5])
        lmean, lmeanB = arena.alloc("lmean", [128, 5])
        lvar, lvarB = arena.alloc("lvar", [128, 5])
        ltmp = {n: arena.alloc("l" + n, [128, 5]) for n in ("xe", "s", "y", "t")}
        mmv, mmvB = arena.alloc("mmv", [128, 2])
        mtmp = {n: arena.alloc("m" + n, [128, 1]) for n in ("xe", "s", "y", "t")}
        eU_all, _ = arena.alloc("eUall", [128, 2 * npc * maxn])
        eUBs = mkbufs([f"eU{i}" for i in range(2 * npc)], prior=arena.prior); arena.bufs += eUBs
        eU = [[(eU_all[:, (i * npc + p) * maxn:(i * npc + p + 1) * maxn], eUBs[i * npc + p]) for p in range(npc)] for i in range(2)]
        junk = eU_all.bitcast(BF16)[:, 0:2048]
        sG = [[arena.alloc(f"sG{i}{p}", [128, maxn]) for p in range(npc)] for i in range(2)]
        aT = [[arena.alloc(f"aT{i}{p}", [128, maxn]) for p in range(npc)] for i in range(2)]
        bT = [[arena.alloc(f"bT{i}{p}", [128, maxn]) for p in range(npc)] for i in range(2)]
        cT, cB = arena.alloc("cT", [128, max(maxn, 512)])
        etmp = [etmp0, (cT, cB)]
        wsb, wsbB = arena.alloc("wsb", [128, 1024])
        if t == 0:
            wsraw, wsrawB = arena.alloc("wsraw", [128, 8, 128])
        gvB = mkbufs([f"gv{j}" for j in range(5)], prior=pending_of(state["Cbufs"]))
        state["Cbufs"] = gvB
        gv = lambda j: Cr[:, j * 2048:(j + 1) * 2048]
        vnb = lambda j: Cb[:, j * 4096:j * 4096 + 2048]

        S.add("sp", lambda e: e.dma_start(out=rows[:, 1, :], in_=ln_b[l].partition_broadcast(128)), w=[rowB[1]], is_dma=True)
        S.add("sp", lambda e: e.dma_start(out=wsb[:, :], in_=ws_b[l].partition_broadcast(128)), w=[wsbB], is_dma=True)
        if t == 0:
            S.add("sp", lambda e: e.dma_start(out=wsraw, in_=ws[l].rearrange("g i j -> i g j")), w=[wsrawB], is_dma=True)
            for hf in range(2):
                k, kB = ps_alloc()

                def fn(e, k=k, hf=hf):
                    last = None
                    for q in range(4):
                        g = hf * 4 + q
                        last = e.transpose(PS[:, k, q * 128:(q + 1) * 128], wsraw[:, g, :], CST[:, C_ID:C_ID + 128])
                    return last
                S.add("pe", fn, r=[wsrawB, cstB], w=[kB])
                src = PS[:, k, :].rearrange("p (q c) -> p q c", q=4)
                msk = CST[:, C_MASK:C_MASK + 128].unsqueeze(1).to_broadcast([128, 4, 128])
                S.add("dve", lambda e, src=src, hf=hf, msk=msk: e.tensor_tensor(WsT[:, hf * 4:(hf + 1) * 4, :], src, msk, ALU.mult),
                      r=[kB, cstB], w=[wstB])

        ei = 0
        if state["ring"] % 2 == 1:
            state["ring"] = (state["ring"] + 1) % NS
        for up in range(4):
            s0 = fetch_u(w_in, 8192 + up * 512)
            s1 = fetch_u(w_in, 8192 + up * 512 + 256)
            assert s1 == s0 + 1
            for c in chs:
                P = c["P"]; off = c["off"]
                k, kB = ps_alloc()

                def fn(e, k=k, s0=s0, P=P, off=off):
                    last = None
                    for kc in range(16):
                        last = e.matmul(PS[0:P, k, :].rearrange("p (a b) -> p a b", a=2), lhsT=hT[:, kc, off:off + P],
                                        rhs=ring[:, s0:s0 + 2, kc, :], start=(kc == 0), stop=(kc == 15))
                    return last
                S.add("pe", fn, r=[hTB[c["j"]], ringB[s0], ringB[s1]], w=[kB])
                et, etB = etmp[ei]; ei ^= 1
                S.add("act", lambda e, et=et, k=k, P=P: e.activation(et[0:P, 0:512], PS[0:P, k, :], AF.Erf, scale=0.7071067811865476),
                      r=[kB], w=[etB])
                S.add("dve", lambda e, et=et, k=k, P=P, j=c["j"], up=up: e.scalar_tensor_tensor(
                    gv(j)[0:P, up * 512:(up + 1) * 512], et[0:P, 0:512], 1.0, PS[0:P, k, :], ALU.add, ALU.mult,
                    accum_out=s12[0:P, 0, j, up:up + 1]),
                    r=[etB, kB], w=[gvB[c["j"]], s12B])
                S.add("act", lambda e, P=P, j=c["j"], up=up: e.activation(junk[0:P, 0:512], gv(j)[0:P, up * 512:(up + 1) * 512], AF.Square,
                                                                         accum_out=s12[0:P, 1, j, up:up + 1]),
                      r=[gvB[c["j"]]], w=eUBs + [s12B])
        S.add("sp", lambda e: e.dma_start(out=rows[:, 0, :], in_=ln_g[l].partition_broadcast(128)), w=[rowB[0]], is_dma=True)

        s12v = s12.rearrange("p a c k -> p (a c) k")
        S12v = S12.rearrange("p a c -> p (a c)")
        S.add("dve", lambda e: e.tensor_tensor(S12v, s12v[:, :, 0], s12v[:, :, 1], ALU.add), r=[s12B], w=[S12B])
        S.add("dve", lambda e: e.tensor_tensor(S12v, S12v, s12v[:, :, 2], ALU.add), r=[s12B, S12B], w=[S12B])
        S.add("dve", lambda e: e.tensor_tensor(S12v, S12v, s12v[:, :, 3], ALU.add), r=[s12B, S12B], w=[S12B])
        S.add("dve", lambda e: e.tensor_scalar_mul(lmean[:, :], S12[:, 0, :], 1.0 / D), r=[S12B], w=[lmeanB])
        S.add("dve", lambda e: e.tensor_tensor(lvar[:, :], lmean[:, :], lmean[:, :], ALU.mult), r=[lmeanB], w=[lvarB])
        S.add("dve", lambda e: e.scalar_tensor_tensor(lvar[:, :], S12[:, 1, :], 1.0 / D, lvar[:, :], ALU.mult, ALU.subtract),
              r=[S12B, lvarB], w=[lvarB])
        lrstd, lrstdB = rstd_chain(lvar[:, 0:5], lvarB, 4.0 * EPS, 5, 128, ltmp)

        vnhB = [[Buf(f"vn{j}_{hf}") for hf in range(2)] for j in range(5)]
        state["Cbufs"] = gvB + [b_ for row_ in vnhB for b_ in row_]

        def ln_half(c, hf):
            P = c["P"]; j = c["j"]
            c0, c1 = hf * 1024, (hf + 1) * 1024
            y, yB = lrstd[:, j:j + 1], lrstdB
            S.add("dve", lambda e: e.scalar_tensor_tensor(t1[0:P, c0:c1], gv(j)[0:P, c0:c1], lmean[0:P, j:j + 1], rows[0:P, 0, c0:c1],
                                                          ALU.subtract, ALU.mult),
                  r=[gvB[j], lmeanB, rowB[0]], w=[t1B])
            if not c["sample"]:
                S.add("dve", lambda e: e.scalar_tensor_tensor(vnb(j)[0:P, c0:c1], t1[0:P, c0:c1], y[0:P, 0:1], rows[0:P, 1, c0:c1],
                                                              ALU.mult, ALU.add),
                      r=[t1B, yB, rowB[1]], w=[gvB[j], vnhB[j][hf]])
            else:
                S.add("dve", lambda e: e.scalar_tensor_tensor(gv(j)[0:P, c0:c1], t1[0:P, c0:c1], y[0:P, 0:1], rows[0:P, 1, c0:c1],
                                                              ALU.mult, ALU.add),
                      r=[t1B, yB, rowB[1]], w=[gvB[j]])
        for hf in range(2):
            for c in chs:
                ln_half(c, hf)
        for c in chs:
            if c["sample"]:
                P = c["P"]; j = c["j"]
                S.add("sp", lambda e, P=P, j=j: e.dma_start(out=vs[l], in_=gv(j)[0:P, :]), r=[gvB[j]], is_dma=True, store=True)
                t1b = t1.bitcast(BF16)
                S.add("act", lambda e, P=P, j=j, t1b=t1b: e.copy(t1b[0:P, 0:2048], gv(j)[0:P, :]), r=[gvB[j]], w=[t1B])

        if nxt is not None:
            c0n = tile_chunks(nxt[1])[0]
            srcn = x_src(nxt[0], c0n)
            S.add("sp", lambda e: e.dma_start(out=rows[:, 1, :], in_=srcn), r=[xscrB[c0n["g"]]], w=[rowB[1]], is_dma=True)
            pre0[nxt] = (rows[:, 1, :], rowB[1])

        def vn_lhs(c, b):
            if c["sample"]:
                return t1.bitcast(BF16)[0:c["P"], b * 128:(b + 1) * 128], t1B
            return vnb(c["j"])[0:c["P"], b * 128:(b + 1) * 128], vnhB[c["j"]][b // 8]

        m2 = {}

        def m2_A(b):
            if b % 2 == 0:
                m2["sU"] = fetch_u(w_in, 6144 + b * 128)
                m2["sG"] = fetch_u(w_in, 10240 + b * 128)
            sU2, sG2 = m2["sU"], m2["sG"]
            bo = (b % 2) * 128
            for pi, (a, n) in enumerate(pieces):
                pcs = chunks_in(chs, a, n)
                kU, kUB = ps_alloc(); kG, kGB = ps_alloc()
                hr = [hTB[c["j"]] for c in pcs]
                mm_group(lambda pi_, a_, n_, kU=kU: PS[:, kU, 0:n_], [(a, n)],
                         lambda kc, s=sU2, bo=bo: ring[:, s, kc, bo:bo + 128], lambda kc, a_, n_: hT[:, kc, a_:a_ + n_],
                         hr + [ringB[sU2]], [kU])
                mm_group(lambda pi_, a_, n_, kG=kG: PS[:, kG, 0:n_], [(a, n)],
                         lambda kc, s=sG2, bo=bo: ring[:, s, kc, bo:bo + 128], lambda kc, a_, n_: hT[:, kc, a_:a_ + n_],
                         hr + [ringB[sG2]], [kG])
                eu, euB = eU[b % 2][pi]; sg, sgB = sG[b % 2][pi]
                a_t, a_B = aT[b % 2][pi]; b_t, b_B = bT[b % 2][pi]
                S.add("act", lambda e, eu=eu, kU=kU, n=n: e.activation(eu[:, 0:n], PS[:, kU, 0:n], AF.Erf, scale=0.7071067811865476),
                      r=[kUB], w=[euB])
                S.add("act", lambda e, sg=sg, kG=kG, n=n: e.activation(sg[:, 0:n], PS[:, kG, 0:n], AF.Sigmoid), r=[kGB], w=[sgB])
                S.add("dve", lambda e, eu=eu, kU=kU, n=n, a_t=a_t: e.scalar_tensor_tensor(a_t[:, 0:n], eu[:, 0:n], 1.0, PS[:, kU, 0:n], ALU.add, ALU.mult),
                      r=[euB, kUB], w=[a_B])
                S.add("dve", lambda e, sg=sg, kG=kG, n=n, b_t=b_t: e.tensor_tensor(b_t[:, 0:n], sg[:, 0:n], PS[:, kG, 0:n], ALU.mult),
                      r=[sgB, kGB], w=[b_B])

        def m2_B(b):
            g = b // 2
            for pi, (a, n) in enumerate(pieces):
                pcs = chunks_in(chs, a, n)
                kS, kSB = ps_alloc()

                def fnS(e, kS=kS, pcs=pcs, a=a):
                    last = None
                    for c in pcs:
                        P = c["P"]; o = c["off"] - a
                        lhs, _ = vn_lhs(c, b)
                        last = e.matmul(PS[:, kS, o:o + P], lhsT=lhs, rhs=WsT[0:P, g, 0:P], start=True, stop=True)
                    return last
                S.add("pe", fnS, r=[vn_lhs(c, b)[1] for c in pcs] + [wstB], w=[kSB])
                a_t, a_B = aT[b % 2][pi]; b_t, b_B = bT[b % 2][pi]
                npc_ = [c for c in pcs if not c["sample"]]
                if npc_:
                    o0 = npc_[0]["off"] - a; m_ = len(npc_)
                    btab = wsb[:, g * 128:(g + 1) * 128].unsqueeze(1).to_broadcast([128, m_, 128])
                    S.add("dve", lambda e, kS=kS, o0=o0, m_=m_, btab=btab: e.tensor_tensor(
                        cT[:, o0:o0 + m_ * 128].rearrange("p (c i) -> p c i", c=m_),
                        PS[:, kS, o0:o0 + m_ * 128].rearrange("p (c i) -> p c i", c=m_), btab, ALU.add),
                        r=[kSB, wsbB], w=[cB])
                for c in pcs:
                    if c["sample"]:
                        o = c["off"] - a
                        S.add("dve", lambda e, kS=kS, o=o: e.tensor_tensor(cT[:, o:o + 32], PS[:, kS, o:o + 32], wsb[:, g * 128:g * 128 + 32], ALU.add),
                              r=[kSB, wsbB], w=[cB])
                S.add("dve", lambda e, n=n, a_t=a_t: e.tensor_tensor(cT[:, 0:n], cT[:, 0:n], a_t[:, 0:n], ALU.mult),
                      r=[a_B, cB], w=[cB])
                S.add("dve", lambda e, a=a, n=n, b_t=b_t: e.scalar_tensor_tensor(gmT[:, b, a:a + n], cT[:, 0:n], 0.5, b_t[:, 0:n], ALU.mult, ALU.mult),
                      r=[cB, b_B], w=[gmB[b]])
        m2_A(0)
        for b in range(16):
            if b + 1 < 16:
                m2_A(b + 1)
            m2_B(b)

        if stop == 'm':
            return True
        arena.begin()
        cosT, cosB = arena.alloc("cosT", [128, TMAX])
        sinT, sinB = arena.alloc("sinT", [128, TMAX])
        xsb, xsbB = arena.alloc("xsb", [128, TMAX])
        r1, r1B = arena.alloc("r1", [128, TMAX])
        r2, r2B = arena.alloc("r2", [128, TMAX])
        hb_ = []
        for i in range(2):
            d_ = {}
            d_["qTb"] = arena.alloc(f"qTb{i}", [128, TMAX], BF16)
            d_["kTb"] = arena.alloc(f"kTb{i}", [128, TMAX], BF16)
            d_["qdTb"] = arena.alloc(f"qdTb{i}", [128, TMAX], BF16)
            d_["vb"], _ = arena.alloc(f"vb{i}", [128, 5, 256], BF16)
            d_["vbB"] = mkbufs([f"vb{i}_{j}" for j in range(5)], prior=arena.prior); arena.bufs += d_["vbB"]
            d_["sgr"], _ = arena.alloc(f"sgr{i}", [128, 2, TMAX])
            d_["sgrB"] = mkbufs([f"sgr{i}_0", f"sgr{i}_1"], prior=arena.prior); arena.bufs += d_["sgrB"]
            d_["Ss32"] = arena.alloc(f"Ss32_{i}", [128, 256])
            hb_.append(d_)
        kdec, _ = arena.alloc("kdec", [128, 5, 128], BF16)
        kdecB = mkbufs([f"kdec{j}" for j in range(5)], prior=arena.prior); arena.bufs += kdecB
        Pm, _ = arena.alloc("Pm", [128, 5, 128], BF16)
        PmB = mkbufs([f"Pm{j}" for j in range(5)], prior=arena.prior); arena.bufs += PmB
        onb, _ = arena.alloc("onb", [128, 2, 256], BF16)
        onB = mkbufs(["on0", "on1"], prior=arena.prior); arena.bufs += onB
        Sbv, _ = arena.alloc("Sbv", [128, 5, 256], BF16)
        SbvB = mkbufs([f"Sbv{j}" for j in range(5)], prior=arena.prior); arena.bufs += SbvB
        ost, ostB = arena.alloc("ost", [128, 5, 6])
        omv, omvB = arena.alloc("omv", [128, 5, 2])
        nmr, nmrB = arena.alloc("nmr", [128, 5])
        otmp = {n: arena.alloc("o" + n, [128, 5]) for n in ("xe", "s", "y", "t")}
        goB = mkbufs([f"go{b}" for b in range(16)], prior=pending_of(state["Cbufs"]))
        mgB = mkbufs([f"mg{b}" for b in range(16)], prior=pending_of(state["Cbufs"]))
        state["Cbufs"] = goB + mgB
        goT = Cb[:, 0:16 * TMAX].rearrange("p (k t) -> p k t", k=16)
        mgT = Cb[:, 16 * TMAX:32 * TMAX].rearrange("p (k t) -> p k t", k=16)

        def fn_cs(e, which, dst):
            out = [e.dma_start(out=dst[:, 0:512], in_=cs[which, :, t * 512:(t + 1) * 512])]
            if sch:
                out.append(e.dma_start(out=dst[:, 512:544], in_=cs[which, :, SEQ:SEQ + SS]))
            return out
        S.add("sp", lambda e: fn_cs(e, 0, cosT), w=[cosB], is_dma=True, ndma=1 + len(sch))
        S.add("sp", lambda e: fn_cs(e, 1, sinT), w=[sinB], is_dma=True, ndma=1 + len(sch))
        if t == 0:
            for h in range(H):
                S.add("dve", lambda e, h=h: e.memset(S32[:, h, :], 0.0), w=[S32B[h]])
        rs = {}

        def proj_qk(h):
            hbuf = hb_[h % 2]
            if h % 2 == 0:
                rs["sQ"] = fetch_u(w_in, h * 128)
                rs["sK"] = fetch_u(w_in, 1024 + h * 128)
            ho = (h % 2) * 128
            if sch:
                Ss32, Ss32B = hbuf["Ss32"]
                S.add("sp", lambda e: e.dma_start(out=Ss32, in_=st0[l, h]), w=[Ss32B], is_dma=True)
            qTb, qTbB = hbuf["qTb"]; kTb, kTbB = hbuf["kTb"]; qdTb, qdTbB = hbuf["qdTb"]
            for which in range(2):
                for (a, n) in pieces:
                    pcs = chunks_in(chs, a, n)
                    hr = [hTB[c["j"]] for c in pcs]
                    k1, k1B = ps_alloc()
                    sqk = rs["sQ"] if which == 0 else rs["sK"]
                    mm_group(lambda pi, a_, n_, k1=k1: PS[:, k1, 0:n_], [(a, n)],
                             lambda kc, s=sqk, ho=ho: ring[:, s, kc, ho:ho + 128],
                             lambda kc, a_, n_: hT[:, kc, a_:a_ + n_], hr + [ringB[sqk]], [k1])
                    if which == 0:
                        S.add("act", lambda e, k1=k1, a=a, n=n: e.copy(xsb[:, a:a + n], PS[:, k1, 0:n]), r=[k1B], w=[xsbB])
                    else:
                        S.add("act", lambda e, k1=k1, a=a, n=n: e.mul(xsb[:, a:a + n], PS[:, k1, 0:n], float(DK) ** -0.5), r=[k1B], w=[xsbB])
                    k2, k2B = ps_alloc()
                    S.add("pe", lambda e, k2=k2, a=a, n=n: e.matmul(PS[:, k2, 0:n], lhsT=CST[:, C_PERM:C_PERM + 128], rhs=xsb[:, a:a + n],
                                                                    start=True, stop=True), r=[xsbB, cstB], w=[k2B])
                    S.add("dve", lambda e, a=a, n=n: e.tensor_tensor(r1[:, a:a + n], xsb[:, a:a + n], cosT[:, a:a + n], ALU.mult),
                          r=[xsbB, cosB], w=[r1B])
                    S.add("dve", lambda e, k2=k2, a=a, n=n: e.tensor_tensor(r2[:, a:a + n], PS[:, k2, 0:n], sinT[:, a:a + n], ALU.mult),
                          r=[k2B, sinB], w=[r2B])
                    if which == 0:
                        S.add("dve", lambda e, a=a, n=n: e.tensor_tensor(r1[:, a:a + n], r1[:, a:a + n], r2[:, a:a + n], ALU.add),
                              r=[r1B, r2B], w=[r1B])
                        S.add("act", lambda e, a=a, n=n: e.copy(qTb[:, a:a + n], r1[:, a:a + n]), r=[r1B], w=[qTbB])
                        npc_ = [c for c in pcs if not c["sample"]]
                        if npc_:
                            a0 = npc_[0]["off"]; m = len(npc_)
                            qdt = CST[:, C_QD + h * 128:C_QD + (h + 1) * 128].unsqueeze(1).to_broadcast([128, m, 128])
                            S.add("dve", lambda e, a0=a0, m=m, qdt=qdt: e.tensor_tensor(
                                qdTb[:, a0:a0 + m * 128].rearrange("p (c i) -> p c i", c=m),
                                r1[:, a0:a0 + m * 128].rearrange("p (c i) -> p c i", c=m), qdt, ALU.mult),
                                r=[r1B, cstB], w=[qdTbB])
                        for c in pcs:
                            if c["sample"]:
                                o = c["off"]
                                S.add("dve", lambda e, o=o: e.tensor_tensor(qdTb[:, o:o + 32], r1[:, o:o + 32],
                                                                           CST[:, C_QD + h * 128:C_QD + h * 128 + 32], ALU.mult),
                                      r=[r1B, cstB], w=[qdTbB])
                    else:
                        S.add("dve", lambda e, a=a, n=n: e.tensor_tensor(kTb[:, a:a + n], r1[:, a:a + n], r2[:, a:a + n], ALU.add),
                              r=[r1B, r2B], w=[kTbB])

        def proj_vg(h):
            hbuf = hb_[h % 2]
            vb, vbB = hbuf["vb"], hbuf["vbB"]; sgr, sgrB = hbuf["sgr"], hbuf["sgrB"]
            sV = fetch_u(w_in, 2048 + h * 256)
            sGR = fetch_u(w_in, 4096 + h * 256)
            for c in chs:
                P = c["P"]; off = c["off"]; j = c["j"]
                k, kB = ps_alloc()

                def fn(e, k=k, s=sV, P=P, off=off):
                    last = None
                    for kc in range(16):
                        last = e.matmul(PS[0:P, k, 0:256], lhsT=hT[:, kc, off:off + P], rhs=ring[:, s, kc, :],
                                        start=(kc == 0), stop=(kc == 15))
                    return last
                S.add("pe", fn, r=[hTB[j], ringB[sV]], w=[kB])
                S.add("act", lambda e, k=k, P=P, j=j: e.copy(vb[0:P, j, :], PS[0:P, k, 0:256]), r=[kB], w=[vbB[j]])
            for half in range(2):
                for (a, n) in pieces:
                    pcs = chunks_in(chs, a, n)
                    hr = [hTB[c["j"]] for c in pcs]
                    k1, k1B = ps_alloc()
                    mm_group(lambda pi, a_, n_, k1=k1: PS[:, k1, 0:n_], [(a, n)],
                             lambda kc, s=sGR, half=half: ring[:, s, kc, half * 128:(half + 1) * 128],
                             lambda kc, a_, n_: hT[:, kc, a_:a_ + n_], hr + [ringB[sGR]], [k1])
                    S.add("act", lambda e, k1=k1, a=a, n=n, half=half: e.activation(sgr[:, half, a:a + n], PS[:, k1, 0:n], AF.Sigmoid),
                          r=[k1B], w=[sgrB[half]])
                    S.add("dve", lambda e, k1=k1, a=a, n=n, half=half: e.tensor_tensor(sgr[:, half, a:a + n], sgr[:, half, a:a + n], PS[:, k1, 0:n], ALU.mult),
                          r=[sgrB[half], k1B], w=[sgrB[half]])

        rec = {}

        def rec1(h):
            hbuf = hb_[h % 2]
            qTb, qTbB = hbuf["qTb"]; kTb, kTbB = hbuf["kTb"]
            vb, vbB = hbuf["vb"], hbuf["vbB"]
            Ss32, Ss32B = hbuf["Ss32"]
            for c in chs:
                P = c["P"]; off = c["off"]; j = c["j"]; smp = c["sample"]
                k, kB = ps_alloc()
                S.add("pe", lambda e, k=k, P=P, off=off: e.transpose(PSb[0:P, k, 0:128], kTb[:, off:off + P], IDB[:, :]),
                      r=[kTbB, idbB], w=[kB])
                kdcol = (CST[0:P, C_KD32 + h:C_KD32 + h + 1] if smp else CST[0:P, C_KD + h:C_KD + h + 1])
                S.add("dve", lambda e, k=k, P=P, j=j, kdcol=kdcol: e.tensor_scalar_mul(kdec[0:P, j, :], PSb[0:P, k, 0:128], kdcol),
                      r=[kB, cstB], w=[kdecB[j]])
            for c in chs:
                P = c["P"]; off = c["off"]; j = c["j"]; smp = c["sample"]
                k2, k2B = ps_alloc()
                S.add("pe", lambda e, k2=k2, P=P, off=off: e.matmul(PS[0:P, k2, 0:P], lhsT=kTb[:, off:off + P], rhs=qTb[:, off:off + P],
                                                                    start=True, stop=True), r=[kTbB, qTbB], w=[k2B])
                dtab = (CST[0:P, C_DT32 + h * 32:C_DT32 + h * 32 + 32] if smp else CST[0:P, C_DT + h * 128:C_DT + (h + 1) * 128])
                S.add("dve", lambda e, k2=k2, P=P, j=j, dtab=dtab: e.tensor_tensor(Pm[0:P, j, 0:P], PS[0:P, k2, 0:P], dtab, ALU.mult),
                      r=[k2B, cstB], w=[PmB[j]])
            for c in chs:
                P = c["P"]; off = c["off"]; j = c["j"]; smp = c["sample"]
                k3, k3B = ps_alloc()
                S.add("pe", lambda e, k3=k3, P=P, j=j: e.matmul(PS[:, k3, 0:256], lhsT=kdec[0:P, j, :], rhs=vb[0:P, j, :],
                                                                start=True, stop=True), r=[kdecB[j], vbB[j]], w=[k3B])
                if smp:
                    S.add("act", lambda e, j=j: e.copy(Sbv[:, j, :], Ss32), r=[Ss32B], w=[SbvB[j]])
                    S.add("dve", lambda e, k3=k3: e.scalar_tensor_tensor(Ss32, Ss32, cd32[h], PS[:, k3, 0:256], ALU.mult, ALU.add),
                          r=[Ss32B, k3B], w=[Ss32B])
                    S.add("sp", lambda e: e.dma_start(out=sts[l, h], in_=Ss32), r=[Ss32B], is_dma=True, store=True)
                else:
                    S.add("act", lambda e, j=j: e.copy(Sbv[:, j, :], S32[:, h, :]), r=[S32B[h]], w=[SbvB[j]])
                    S.add("dve", lambda e, k3=k3: e.scalar_tensor_tensor(S32[:, h, :], S32[:, h, :], cd128[h], PS[:, k3, 0:256], ALU.mult, ALU.add),
                          r=[S32B[h], k3B], w=[S32B[h]])
                    if c["g"] == 15:
                        S.add("sp", lambda e: e.dma_start(out=stp[l, h], in_=S32[:, h, :]), r=[S32B[h]], is_dma=True, store=True)

        def rec2(h):
            hbuf = hb_[h % 2]
            qdTb, qdTbB = hbuf["qdTb"]
            vb, vbB = hbuf["vb"], hbuf["vbB"]
            okb = []
            for c in chs:
                P = c["P"]; off = c["off"]; j = c["j"]
                k4 = 5 + j // 2; k4B = psB[k4]; c4 = (j % 2) * 256

                def fn(e, k4=k4, P=P, off=off, j=j, c4=c4):
                    e.matmul(PS[0:P, k4, c4:c4 + 256], lhsT=Pm[0:P, j, 0:P], rhs=vb[0:P, j, :], start=True, stop=False)
                    return e.matmul(PS[0:P, k4, c4:c4 + 256], lhsT=qdTb[:, off:off + P], rhs=Sbv[:, j, :], start=False, stop=True)
                S.add("pe", fn, r=[PmB[j], vbB[j], qdTbB, SbvB[j]], w=[k4B])
                okb.append((k4, k4B, c4))
            for ci_, c in enumerate(chs):
                P = c["P"]; j = c["j"]
                k4, k4B, c4 = okb[ci_]
                S.add("dve", lambda e, k4=k4, P=P, j=j, c4=c4: e.bn_stats(ost[0:P, j, :], PS[0:P, k4, c4:c4 + 256]), r=[k4B], w=[ostB])
                S.add("dve", lambda e, P=P, j=j: e.bn_aggr(omv[0:P, j, :], ost[0:P, j, :]), r=[ostB], w=[omvB])
            y, yB = rstd_chain(omv[:, 0:nch, 1], omvB, EPS, nch, 128, otmp)
            S.add("dve", lambda e: e.scalar_tensor_tensor(nmr[:, 0:nch], omv[:, 0:nch, 0], -1.0, y[:, 0:nch], ALU.mult, ALU.mult),
                  r=[omvB, yB], w=[nmrB])
            rec["okb"] = okb; rec["y"] = (y, yB)

        def rec3(h):
            hbuf = hb_[h % 2]
            sgr, sgrB = hbuf["sgr"], hbuf["sgrB"]
            okb = rec["okb"]; y, yB = rec["y"]
            for ci, c in enumerate(chs):
                P = c["P"]; off = c["off"]; j = c["j"]
                k4, k4B, c4 = okb[ci]
                oi = ci % 2
                S.add("act", lambda e, k4=k4, P=P, j=j, oi=oi, c4=c4: e.activation(onb[0:P, oi, :], PS[0:P, k4, c4:c4 + 256], AF.Identity,
                                                                            bias=nmr[0:P, j:j + 1], scale=y[0:P, j:j + 1]),
                      r=[k4B, nmrB, yB], w=[onB[oi]])
                k5, k5B = ps_alloc()

                def fn(e, k5=k5, P=P, oi=oi):
                    e.transpose(PSb[:, k5, 0:P], onb[0:P, oi, 0:128], IDB[0:P, 0:P])
                    return e.transpose(PSb[:, k5, 128:128 + P], onb[0:P, oi, 128:256], IDB[0:P, 0:P])
                S.add("pe", fn, r=[onB[oi], idbB], w=[k5B])
                for half in range(2):
                    S.add("dve", lambda e, k5=k5, P=P, off=off, half=half: e.tensor_tensor(
                        goT[:, 2 * h + half, off:off + P], PSb[:, k5, half * 128:half * 128 + P], sgr[:, half, off:off + P], ALU.mult),
                        r=[k5B, sgrB[half]], w=[goB[2 * h + half]])

        import os
        state["pslim"] = 5
        ohalf = {(k_, hf_): Buf(f"ps{k_}h{hf_}", prior=psB[k_].pending()) for k_ in (5, 6, 7) for hf_ in (0, 1)}
        if os.environ.get("DBG_NOPIPE"):
            for h in range(H):
                proj_qk(h); proj_vg(h); rec1(h); rec2(h); rec3(h)
        else:
            proj_qk(0); proj_vg(0)
            for h in range(H):
                rec1(h)
                if h + 1 < H:
                    proj_qk(h + 1)
                rec2(h)
                if h + 1 < H:
                    proj_vg(h + 1)
                rec3(h)
        state["pslim"] = 8
        for k_ in (5, 6, 7):
            psB[k_].writers = pending_of([ohalf[(k_, 0)], ohalf[(k_, 1)]])
            psB[k_].readers = []

        if stop == 'r':
            return True
        arena.begin()
        sa = [[arena.alloc(f"sa{i}{p}", [128, maxn]) for p in range(npc)] for i in range(2)]
        sm = [[arena.alloc(f"sm{i}{p}", [128, maxn]) for p in range(npc)] for i in range(2)]
        m1, m1B = arena.alloc("m1", [128, 512])
        m2_, m2B = arena.alloc("m2", [128, 512])
        mgs = {}

        def mg_A(b):
            p = b // 2
            if b % 2 == 0:
                mgs[p] = {"AR": fetch_u(w_in, 12288 + b * 128), "AM": fetch_u(w_in, 14336 + b * 128)}
            sAR2, sAM2 = mgs[p]["AR"], mgs[p]["AM"]
            bo = (b % 2) * 128
            for pi, (a, n) in enumerate(pieces):
                pcs = chunks_in(chs, a, n)
                hr = [hTB[c["j"]] for c in pcs]
                kAR, kARB = ps_alloc(); kAM, kAMB = ps_alloc()
                mm_group(lambda pi_, a_, n_, k=kAR: PS[:, k, 0:n_], [(a, n)], lambda kc, s=sAR2, bo=bo: ring[:, s, kc, bo:bo + 128],
                         lambda kc, a_, n_: hT[:, kc, a_:a_ + n_], hr + [ringB[sAR2]], [kAR])
                mm_group(lambda pi_, a_, n_, k=kAM: PS[:, k, 0:n_], [(a, n)], lambda kc, s=sAM2, bo=bo: ring[:, s, kc, bo:bo + 128],
                         lambda kc, a_, n_: hT[:, kc, a_:a_ + n_], hr + [ringB[sAM2]], [kAM])
                sa_, saB = sa[b % 2][pi]; sm_, smB = sm[b % 2][pi]
                S.add("act", lambda e, sa_=sa_, k=kAR, n=n: e.activation(sa_[:, 0:n], PS[:, k, 0:n], AF.Sigmoid), r=[kARB], w=[saB])
                S.add("act", lambda e, sm_=sm_, k=kAM, n=n: e.activation(sm_[:, 0:n], PS[:, k, 0:n], AF.Sigmoid), r=[kAMB], w=[smB])

        def mg_B(b):
            p = b // 2
            if b % 2 == 0:
                mgs[p]["RO"] = fetch_u(w_ro, b * 128)
                mgs[p]["MO"] = fetch_u(w_mo, b * 128)
            sRO2, sMO2 = mgs[p]["RO"], mgs[p]["MO"]
            bo = (b % 2) * 128
            for pi, (a, n) in enumerate(pieces):
                kR, kRB = ps_alloc(); kM, kMB = ps_alloc()
                mm_group(lambda pi_, a_, n_, k=kR: PS[:, k, 0:n_], [(a, n)], lambda kc, s=sRO2, bo=bo: ring[:, s, kc, bo:bo + 128],
                         lambda kc, a_, n_: goT[:, kc, a_:a_ + n_], goB + [ringB[sRO2]], [kR])
                mm_group(lambda pi_, a_, n_, k=kM: PS[:, k, 0:n_], [(a, n)], lambda kc, s=sMO2, bo=bo: ring[:, s, kc, bo:bo + 128],
                         lambda kc, a_, n_: gmT[:, kc, a_:a_ + n_], gmB + [ringB[sMO2]], [kM])
                sa_, saB = sa[b % 2][pi]; sm_, smB = sm[b % 2][pi]
                S.add("dve", lambda e, sa_=sa_, k=kR, n=n: e.tensor_tensor(m1[:, 0:n], sa_[:, 0:n], PS[:, k, 0:n], ALU.mult), r=[saB, kRB], w=[m1B])
                S.add("dve", lambda e, sm_=sm_, k=kM, n=n: e.tensor_tensor(m2_[:, 0:n], sm_[:, 0:n], PS[:, k, 0:n], ALU.mult), r=[smB, kMB], w=[m2B])
                S.add("dve", lambda e, a=a, n=n, b=b: e.tensor_tensor(mgT[:, b, a:a + n], m1[:, 0:n], m2_[:, 0:n], ALU.add),
                      r=[m1B, m2B], w=[mgB[b]])
        mg_A(0)
        for b in range(16):
            if b + 1 < 16:
                mg_A(b + 1)
            mg_B(b)

        if stop == 'mg':
            return True
        io_view()
        nsteps = phase0_steps(*nxt) if nxt is not None else []
        groups = [(0, 2), (2, 2), (4, 2), (6, 2)]
        if state["ring"] % 2 == 1:
            state["ring"] = (state["ring"] + 1) % NS
        items = [(g0, gn, c) for (g0, gn) in groups for c in chs]
        slots = {}
        ns_i = 0

        def o_load(it):
            g0, gn, c = items[it]
            xi, xiB = io["xi"][it % 2]
            P = c["P"]; ncol = gn * 256
            src = x_src(l, c, (g0 * 256, g0 * 256 + ncol))
            S.add("sp", lambda e: e.dma_start(out=xi[0:P, 0:ncol], in_=src), r=[xscrB[c["g"]]], w=[xiB], is_dma=True)
        o_load(0)
        sched_at = {}
        sched_pre = {}
        if nsteps:
            nsteps[0]()
            for ci, (fLd, fSq, fC, fT) in enumerate(nsteps[1:]):
                if ci == 0:
                    fLd()
                else:
                    sched_pre.setdefault(4 * ci - 3, []).append(fLd)
                sched_at.setdefault(4 * ci, []).append(fSq)
                sched_at.setdefault(4 * ci + 2, []).append(fC)
                sched_at.setdefault(4 * ci + 5, []).append(fT)
        for it, (g0, gn, c) in enumerate(items):
            for f_ in sched_pre.pop(it, []):
                f_()
            if g0 not in slots:
                slots[g0] = [fetch_u(w_o, u * 256) for u in range(g0, g0 + gn)]
            if it + 1 < len(items):
                o_load(it + 1)
            P = c["P"]; off = c["off"]
            xi, xiB = io["xi"][it % 2]
            xo, xoB = io["xo"][it % 2]
            ncol = gn * 256
            s0, s1 = slots[g0]
            assert s1 == s0 + 1
            k, kB = ps_alloc()

            def fn(e, k=k, s0=s0, P=P, off=off):
                last = None
                for kc in range(16):
                    last = e.matmul(PS[0:P, k, :].rearrange("p (a b) -> p a b", a=2), lhsT=mgT[:, kc, off:off + P],
                                    rhs=ring[:, s0:s0 + 2, kc, :], start=(kc == 0), stop=(kc == 15))
                return last
            S.add("pe", fn, r=mgB + [ringB[s0], ringB[s1]], w=[kB])
            S.add("dve", lambda e, k=k, P=P, xi=xi, xo=xo: e.tensor_tensor(xo[0:P, 0:512], PS[0:P, k, :], xi[0:P, 0:512], ALU.add),
                  r=[kB, xiB], w=[xoB])
            dst = xscr[c["row0"]:c["row0"] + P, g0 * 256:g0 * 256 + ncol]
            S.add("sp", lambda e, dst=dst, xo=xo, P=P, ncol=ncol: e.dma_start(out=dst, in_=xo[0:P, 0:ncol]), r=[xoB], w=[xscrB[c["g"]]], is_dma=True)
            for f_ in sched_at.pop(it, []):
                f_()
        for k_ in sorted(set(sched_at) | set(sched_pre)):
            for f_ in sched_pre.get(k_, []) + sched_at.get(k_, []):
                f_()

        if l == depth - 1:
            S.add("sp", lambda e: e.dma_start(out=rows[:, 0, :], in_=final_g.ap().partition_broadcast(128)), w=[rowB[0]], is_dma=True)
            fsrc = lambda c: xscr[c["row0"]:c["row0"] + c["P"], :]
            pend = [norm_load(chs[0], fsrc(chs[0]))]
            for i_, c in enumerate(chs):
                if i_ + 1 < len(chs):
                    pend.append(norm_load(chs[i_ + 1], fsrc(chs[i_ + 1])))
                norm_square(c, pend[i_])
                norm_compute(c, pend[i_], rowB[0], "y")
        flush_wb(0)
        return False

    io_view()
    for st_ in phase0_steps(0, 0):
        if isinstance(st_, tuple):
            for f_ in st_:
                f_()
        else:
            st_()
    seq = [(l, t) for l in range(depth) for t in range(ntile)]
    for i, (l, t) in enumerate(seq):
        nxt = seq[i + 1] if i + 1 < len(seq) else None
        if tile_layer(l, t, nxt) or stop == 'o':
            break

    last_ops = list(S.stores)
    fin = Op("sp", lambda e: None)
    fin.deps = last_ops
    S.ops["sp"].append(fin)

    S.finalize()
    esems = {}
    for en in ENGS:
        esems[en] = es.enter_context(nc.semaphore("e_" + en))
    dsems = [es.enter_context(nc.semaphore(f"d{i}")) for i in range(S.n_dma_sems)]
    with es:
        with nc.Block() as block:
            @block.tensor
            def _(e):
                S.emit("pe", e, esems, dsems)

            @block.scalar
            def _(e):
                S.emit("act", e, esems, dsems)

            @block.vector
            def _(e):
                S.emit("dve", e, esems, dsems)

            @block.gpsimd
            def _(e):
                S.emit("pool", e, esems, dsems)

            @block.sync
            def _(e):
                S.emit("sp", e, esems, dsems)
    return nc, consts_np, cs_np


_CACHE = {}


def kernel(x_prompt, x_sample, state_ret, norm_g, w_in, ws, ws_b, ln_g, ln_b,
           w_ret_out, w_mlp_out, w_o, final_g):
    f = lambda a: np.ascontiguousarray(np.asarray(a, dtype=np.float32))
    if "nc" not in _CACHE:
        _CACHE["nc"] = build()
    nc, consts_np, cs_np = _CACHE["nc"]
    x_prompt, x_sample, state_ret = f(x_prompt), f(x_sample), f(state_ret)
    shared = dict(norm_g=f(norm_g), w_in=f(w_in), ws=f(ws), ws_b=f(ws_b).reshape(DEPTH, 8 * 128), ln_g=f(ln_g), ln_b=f(ln_b),
                  w_ro=f(w_ret_out), w_mo=f(w_mlp_out), w_o=f(w_o), final_g=f(final_g), cst=consts_np, cs=cs_np)
    in_maps = []
    for c in range(8):
        m = dict(shared)
        m["xp"] = x_prompt[c]
        m["xs"] = x_sample[c]
        m["st0"] = np.ascontiguousarray(state_ret[:, c])
        in_maps.append(m)
    res = run_bass_kernel_spmd(nc, in_maps, core_ids=list(range(8)))
    r = res.results
    y_prompt = np.stack([r[c]["yp"] for c in range(8)]).astype(np.float32)
    y_sample = np.stack([r[c]["ys"] for c in range(8)]).astype(np.float32)
    st_p = np.stack([r[c]["stp"] for c in range(8)], axis=1).astype(np.float32)
    st_s = np.stack([r[c]["sts"] for c in range(8)], axis=1).astype(np.float32)
    v_s = np.stack([r[c]["vs"] for c in range(8)], axis=1).astype(np.float32)
    return (y_prompt, y_sample, st_p, st_s, v_s)
```
